# Optimizing a Trainium2 kernel written in Bass

```python
import math
import jax, jax.numpy as jnp
from jax import lax
import numpy as np

D_MODEL = 2048
BATCH = 16
SEQ = 256
DEPTH = 1
DEC_BATCH = 8
DEC_SEQ = 2048
PAST_LEN = 256

GRID_W = 64
N_RET_HEADS = 8
RET_DK = D_MODEL // N_RET_HEADS
RET_DV = 2 * RET_DK
RET_CHUNK = 128
N_DIFF_HEADS = 8
DIFF_DH = D_MODEL // N_DIFF_HEADS // 2
ROPE_HALF = DIFF_DH // 2
ROPE_BASE = 10000.0
Q_BLOCK = 128
D_FF = 4 * D_MODEL
N_MOD = 6
RET_QK_W = N_RET_HEADS * RET_DK
RET_V_W = N_RET_HEADS * RET_DV
DIFF_QK_W = N_DIFF_HEADS * 2 * DIFF_DH
DIFF_V_W = N_DIFF_HEADS * 2 * DIFF_DH
GATE_W = 2 * D_MODEL
IN_SPLITS = (RET_QK_W, RET_QK_W, RET_V_W, RET_V_W, DIFF_QK_W, DIFF_QK_W, DIFF_V_W, GATE_W)
IN_W = sum(IN_SPLITS)
IN_OFFSETS = [int(o) for o in np.cumsum(IN_SPLITS)[:-1]]
EPS = 1e-6
SUBLN_EPS = 1e-5

kernel_name = 'hybrid_retention_diffattn_prefix_dit_step'

F32 = jnp.float32


def rmsnorm(x, g=None, eps=EPS):
    xf = x.astype(F32)
    y = xf * lax.rsqrt(jnp.mean(xf * xf, axis=-1, keepdims=True) + eps)
    if g is not None:
        y = y * g.astype(F32)
    return y.astype(x.dtype)


def adaln(cond, w_ada, b_ada):
    m = jax.nn.silu(cond) @ w_ada + b_ada
    return jnp.split(m, N_MOD, axis=-1)


def modulate(x, g, shift, scale):
    return rmsnorm(x, g) * (1.0 + scale) + shift


def project_in(h, w_in):
    B, T, _ = h.shape
    p = h @ w_in
    rq, rk, rv, rg, dq, dk, dv, gates = jnp.split(p, IN_OFFSETS, axis=-1)
    rq = rq.reshape(B, T, N_RET_HEADS, RET_DK).transpose(0, 2, 1, 3)
    rk = rk.reshape(B, T, N_RET_HEADS, RET_DK).transpose(0, 2, 1, 3) * (RET_DK ** -0.5)
    rv = rv.reshape(B, T, N_RET_HEADS, RET_DV).transpose(0, 2, 1, 3)
    dq = dq.reshape(B, T, N_DIFF_HEADS, 2, DIFF_DH)
    dk = dk.reshape(B, T, N_DIFF_HEADS, 2, DIFF_DH)
    dv = dv.reshape(B, T, N_DIFF_HEADS, 2 * DIFF_DH)
    gate_ret, gate_diff = jnp.split(gates, 2, axis=-1)
    return rq, rk, rv, rg, dq, dk, dv, gate_ret, gate_diff


def retention_chunked(q, k, v, log_gamma, s0):
    B, H, T, _ = q.shape
    n = T // RET_CHUNK
    idx = jnp.arange(RET_CHUNK, dtype=F32)
    lg = log_gamma[:, None]
    rel = idx[:, None] - idx[None, :]
    decay = jnp.where(rel >= 0, jnp.exp(jnp.maximum(rel, 0.0)[None] * lg[:, :, None]), 0.0)
    xi = jnp.exp((idx + 1.0)[None] * lg)
    zeta = jnp.exp((RET_CHUNK - 1.0 - idx)[None] * lg)
    g_chunk = jnp.exp(RET_CHUNK * log_gamma)

    def to_chunks(a):
        return jnp.moveaxis(a.astype(F32).reshape(B, H, n, RET_CHUNK, a.shape[-1]), 2, 0)

    def step(S, blk):
        qc, kc, vc = blk
        scores = jnp.einsum('bhid,bhjd->bhij', qc, kc) * decay[None]
        o = (jnp.einsum('bhij,bhjv->bhiv', scores, vc)
             + jnp.einsum('bhid,bhdv->bhiv', qc * xi[None, :, :, None], S))
        S = (g_chunk[None, :, None, None] * S
             + jnp.einsum('bhjd,bhjv->bhdv', kc * zeta[None, :, :, None], vc))
        return S, o

    s_fin, o = lax.scan(step, s0.astype(F32), (to_chunks(q), to_chunks(k), to_chunks(v)))
    o = jnp.moveaxis(o, 0, 2).reshape(B, H, T, v.shape[-1])
    return o, s_fin


def retention_bidir(q, k, v, lg_f, lg_b, s0_f, s0_b):
    o_f, s_f = retention_chunked(q, k, v, lg_f, s0_f)
    o_b, s_b = retention_chunked(jnp.flip(q, 2), jnp.flip(k, 2), jnp.flip(v, 2), lg_b, s0_b)
    return o_f + jnp.flip(o_b, 2), s_f, s_b


def retention_out(o, rg, w_ret_o):
    B, H, T, _ = o.shape
    o = rmsnorm(o)
    o = o.transpose(0, 2, 1, 3).reshape(B, T, RET_V_W).astype(rg.dtype)
    return (jax.nn.silu(rg) * o) @ w_ret_o


def diff_lambda(lq1, lk1, lq2, lk2, lam_init):
    return (jnp.exp(jnp.sum(lq1.astype(F32) * lk1.astype(F32)))
            - jnp.exp(jnp.sum(lq2.astype(F32) * lk2.astype(F32))) + lam_init)


def diff_attend(q, k, v, lam):
    s = jnp.einsum('bqhmd,bkhmd->bhmqk', q.astype(F32), k.astype(F32)) * (DIFF_DH ** -0.5)
    a = jax.nn.softmax(s, axis=-1)
    a = a[:, :, 0] - lam * a[:, :, 1]
    return jnp.einsum('bhqk,bkhe->bqhe', a, v.astype(F32))


def diff_attend_blocked(q, k, v, lam):
    B, T = q.shape[0], q.shape[1]
    nb = T // Q_BLOCK
    qb = q.reshape(B, nb, Q_BLOCK, N_DIFF_HEADS, 2, DIFF_DH).transpose(1, 0, 2, 3, 4, 5)
    ob = lax.map(lambda qq: diff_attend(qq, k, v, lam), qb)
    return ob.transpose(1, 0, 2, 3, 4).reshape(B, T, N_DIFF_HEADS, 2 * DIFF_DH)


def diff_out(o, g_subln, lam_init, w_diff_o, dtype):
    B, T = o.shape[0], o.shape[1]
    o = rmsnorm(o, g_subln, SUBLN_EPS) * (1.0 - lam_init)
    return o.reshape(B, T, DIFF_V_W).astype(dtype) @ w_diff_o


def axial_rope_angles(T):
    ROWS = T // GRID_W
    row = jnp.repeat(jnp.arange(ROWS, dtype=F32), GRID_W)
    col = jnp.tile(jnp.arange(GRID_W, dtype=F32), ROWS)
    inv = ROPE_BASE ** (-jnp.arange(0, ROPE_HALF, 2, dtype=F32) / ROPE_HALF)
    return row[:, None] * inv, col[:, None] * inv


def rotate_half_block(x, ang):
    cos = jnp.cos(ang)[:, None, None, :]
    sin = jnp.sin(ang)[:, None, None, :]
    x1, x2 = x[..., :ROPE_HALF // 2], x[..., ROPE_HALF // 2:]
    return jnp.concatenate([x1 * cos - x2 * sin, x2 * cos + x1 * sin], axis=-1)


def apply_axial_rope(x, ang_r, ang_c):
    xf = x.astype(F32)
    out = jnp.concatenate([rotate_half_block(xf[..., :ROPE_HALF], ang_r),
                           rotate_half_block(xf[..., ROPE_HALF:], ang_c)], axis=-1)
    return out.astype(x.dtype)


def merge_branches(ret_b, diff_b, gate_ret, gate_diff, w_out):
    return (jax.nn.sigmoid(gate_ret) * ret_b + jax.nn.sigmoid(gate_diff) * diff_b) @ w_out


def channel_mlp(h, w_up, w_down):
    return jnp.square(jax.nn.relu(h @ w_up)) @ w_down


def setup_inputs(seed: int = 0) -> dict:
    key = jax.random.key(seed)
    ks = jax.random.split(key, 32)
    nrm = jax.random.normal
    D = D_MODEL
    gamma_logit0 = jnp.log(2.0 ** (5.0 + jnp.arange(N_RET_HEADS, dtype=F32)) - 1.0)
    return {
        'x_prompt': nrm(ks[0], (BATCH, SEQ, D), F32),
        'x_sample': nrm(ks[1], (DEC_BATCH, DEC_SEQ, D), F32),
        'cache_k': nrm(ks[2], (DEC_BATCH, DEPTH, PAST_LEN, N_DIFF_HEADS, 2, DIFF_DH), F32),
        'cache_v': nrm(ks[3], (DEC_BATCH, DEPTH, PAST_LEN, N_DIFF_HEADS, 2 * DIFF_DH), F32),
        'state_ret_fwd': 0.5 * nrm(ks[4], (DEC_BATCH, DEPTH, N_RET_HEADS, RET_DK, RET_DV), F32),
        'state_ret_bwd': 0.5 * nrm(ks[5], (DEC_BATCH, DEPTH, N_RET_HEADS, RET_DK, RET_DV), F32),
        'c': nrm(ks[6], (DEC_BATCH, D), F32),
        'c_ctx': nrm(ks[7], (D,), F32),
        'w_ada': 0.2 * D ** -0.5 * nrm(ks[8], (DEPTH, D, N_MOD * D), F32),
        'b_ada': 0.01 * nrm(ks[9], (DEPTH, N_MOD * D), F32),
        'g_mix_pre': 1.0 + 0.01 * nrm(ks[10], (DEPTH, D), F32),
        'g_mix_post': 1.0 + 0.01 * nrm(ks[11], (DEPTH, D), F32),
        'g_mlp_pre': 1.0 + 0.01 * nrm(ks[12], (DEPTH, D), F32),
        'g_mlp_post': 1.0 + 0.01 * nrm(ks[13], (DEPTH, D), F32),
        'w_in': D ** -0.5 * nrm(ks[14], (DEPTH, D, IN_W), F32),
        'ret_gamma_logit_fwd': gamma_logit0 + 0.1 * nrm(ks[15], (DEPTH, N_RET_HEADS), F32),
        'ret_gamma_logit_bwd': gamma_logit0 + 0.1 * nrm(ks[16], (DEPTH, N_RET_HEADS), F32),
        'w_ret_o': RET_V_W ** -0.5 * nrm(ks[17], (DEPTH, RET_V_W, D), F32),
        'lambda_q1': 0.1 * nrm(ks[18], (DEPTH, DIFF_DH), F32),
        'lambda_k1': 0.1 * nrm(ks[19], (DEPTH, DIFF_DH), F32),
        'lambda_q2': 0.1 * nrm(ks[20], (DEPTH, DIFF_DH), F32),
        'lambda_k2': 0.1 * nrm(ks[21], (DEPTH, DIFF_DH), F32),
        'g_diff_subln': 1.0 + 0.01 * nrm(ks[22], (DEPTH, 2 * DIFF_DH), F32),
        'w_diff_o': DIFF_V_W ** -0.5 * nrm(ks[23], (DEPTH, DIFF_V_W, D), F32),
        'w_out': D ** -0.5 * nrm(ks[24], (DEPTH, D, D), F32),
        'w_mlp_up': D ** -0.5 * nrm(ks[25], (DEPTH, D, D_FF), F32),
        'w_mlp_down': D_FF ** -0.5 * nrm(ks[26], (DEPTH, D_FF, D), F32),
    }


def reference(x_prompt, x_sample, cache_k, cache_v, state_ret_fwd, state_ret_bwd, c, c_ctx,
              w_ada, b_ada, g_mix_pre, g_mix_post, g_mlp_pre, g_mlp_post, w_in,
              ret_gamma_logit_fwd, ret_gamma_logit_bwd, w_ret_o,
              lambda_q1, lambda_k1, lambda_q2, lambda_k2, g_diff_subln, w_diff_o, w_out,
              w_mlp_up, w_mlp_down):
    xp = x_prompt
    Bp = xp.shape[0]
    new_k, new_v, new_sf, new_sb = [], [], [], []
    for l in range(DEPTH):
        lam_init = 0.8 - 0.6 * math.exp(-0.3 * l)
        lam = diff_lambda(lambda_q1[l], lambda_k1[l], lambda_q2[l], lambda_k2[l], lam_init)
        lg_f = jax.nn.log_sigmoid(ret_gamma_logit_fwd[l].astype(F32))
        lg_b = jax.nn.log_sigmoid(ret_gamma_logit_bwd[l].astype(F32))
        sh1, sc1, gt1, sh2, sc2, gt2 = adaln(c_ctx, w_ada[l], b_ada[l])
        h = modulate(xp, g_mix_pre[l], sh1, sc1)
        rq, rk, rv, rg, dq, dk, dv, gr, gd = project_in(h, w_in[l])
        zero_state = jnp.zeros((Bp, N_RET_HEADS, RET_DK, RET_DV), F32)
        ro, s_f, s_b = retention_bidir(rq, rk, rv, lg_f, lg_b, zero_state, zero_state)
        ret_b = retention_out(ro, rg, w_ret_o[l])
        do = diff_attend(dq, dk, dv, lam)
        diff_b = diff_out(do, g_diff_subln[l], lam_init, w_diff_o[l], xp.dtype)
        y = merge_branches(ret_b, diff_b, gr, gd, w_out[l])
        xp = xp + gt1 * rmsnorm(y, g_mix_post[l])
        h = modulate(xp, g_mlp_pre[l], sh2, sc2)
        xp = xp + gt2 * rmsnorm(channel_mlp(h, w_mlp_up[l], w_mlp_down[l]), g_mlp_post[l])
        new_k.append(dk)
        new_v.append(dv)
        new_sf.append(s_f.astype(xp.dtype))
        new_sb.append(s_b.astype(xp.dtype))
    y_prompt = xp
    new_cache_k = jnp.stack(new_k, axis=1)
    new_cache_v = jnp.stack(new_v, axis=1)
    new_state_fwd = jnp.stack(new_sf, axis=1)
    new_state_bwd = jnp.stack(new_sb, axis=1)

    xs = x_sample
    T = xs.shape[1]
    ang_r, ang_c = axial_rope_angles(T)
    for l in range(DEPTH):
        lam_init = 0.8 - 0.6 * math.exp(-0.3 * l)
        lam = diff_lambda(lambda_q1[l], lambda_k1[l], lambda_q2[l], lambda_k2[l], lam_init)
        lg_f = jax.nn.log_sigmoid(ret_gamma_logit_fwd[l].astype(F32))
        lg_b = jax.nn.log_sigmoid(ret_gamma_logit_bwd[l].astype(F32))
        sh1, sc1, gt1, sh2, sc2, gt2 = [m[:, None, :] for m in adaln(c, w_ada[l], b_ada[l])]
        h = modulate(xs, g_mix_pre[l], sh1, sc1)
        rq, rk, rv, rg, dq, dk, dv, gr, gd = project_in(h, w_in[l])
        ro, _, _ = retention_bidir(rq, rk, rv, lg_f, lg_b, state_ret_fwd[:, l], state_ret_bwd[:, l])
        ret_b = retention_out(ro, rg, w_ret_o[l])
        dq = apply_axial_rope(dq, ang_r, ang_c)
        dk = apply_axial_rope(dk, ang_r, ang_c)
        k_all = jnp.concatenate([dk, cache_k[:, l].astype(dk.dtype)], axis=1)
        v_all = jnp.concatenate([dv, cache_v[:, l].astype(dv.dtype)], axis=1)
        do = diff_attend_blocked(dq, k_all, v_all, lam)
        diff_b = diff_out(do, g_diff_subln[l], lam_init, w_diff_o[l], xs.dtype)
        y = merge_branches(ret_b, diff_b, gr, gd, w_out[l])
        xs = xs + gt1 * rmsnorm(y, g_mix_post[l])
        h = modulate(xs, g_mlp_pre[l], sh2, sc2)
        xs = xs + gt2 * rmsnorm(channel_mlp(h, w_mlp_up[l], w_mlp_down[l]), g_mlp_post[l])
    y_sample = xs
    return (y_prompt, y_sample, new_cache_k, new_cache_v, new_state_fwd, new_state_bwd)
```

```python
import math
import os
from contextlib import ExitStack

import numpy as np
import concourse.bass as bass
import concourse.mybir as mybir
from concourse.bass_utils import run_bass_kernel_spmd

F32 = mybir.dt.float32
BF16 = mybir.dt.bfloat16
AF = mybir.ActivationFunctionType
ALU = mybir.AluOpType
AX = mybir.AxisListType

D = 2048
KC = 16
H = 8
DFF = 8192
INW = 22528
EPS = 1e-6
SUBLN_EPS = 1e-5
LAM_INIT = 0.8 - 0.6 * math.exp(-0.3 * 0)
NCONST = 1154


class Res:
    __slots__ = ("w", "r", "x")

    def __init__(self, x=False):
        self.w = None
        self.r = []
        self.x = x


class FW:
    def __init__(self, nc, stack):
        self.nc = nc
        self.names = ["tensor", "vector", "scalar", "gpsimd", "sync"]
        self.prog = {n: [] for n in self.names}
        self.cnt = {n: 0 for n in self.names}
        self.waited = {n: {} for n in self.names}
        self.sem = {n: stack.enter_context(nc.semaphore("s_" + n)) for n in self.names}
        self.NDS = 8
        self.dma_sems = {}
        self.dma_cnt = {}
        self.dma_i = {n: 0 for n in self.names}
        for q in ["sync", "gpsimd"]:
            for i in range(self.NDS):
                self.dma_sems[(q, i)] = stack.enter_context(nc.semaphore("d_%s%d" % (q, i)))
                self.dma_cnt[(q, i)] = 0

    def _wait(self, eng, tok):
        if tok is None:
            return
        sem, val = tok
        if sem is self.sem.get(eng):
            if eng == "tensor" or val > self.cnt[eng]:
                return
        key = id(sem)
        if self.waited[eng].get(key, 0) >= val:
            return
        self.waited[eng][key] = val
        self.prog[eng].append(("wait", sem, val))

    def _deps(self, eng, reads, writes):
        for r in reads:
            self._wait(eng, r.w)
        for w in writes:
            self._wait(eng, w.w)
            for t in w.r:
                self._wait(eng, t)

    def _upd(self, tok, reads, writes):
        for r in reads:
            r.r.append(tok)
        for w in writes:
            w.w = tok
            w.r = []

    def op(self, eng, fn, reads=(), writes=(), inc=True):
        xr = [r for r in reads if r.x]
        if xr:
            reads = [r for r in reads if not r.x]
            writes = list(writes) + xr
        self._deps(eng, reads, writes)
        if inc:
            self.cnt[eng] += 1
            tok = (self.sem[eng], self.cnt[eng])
            self.prog[eng].append(("inst", fn, self.sem[eng], 1))
        else:
            tok = (self.sem[eng], self.cnt[eng] + 1)
            self.prog[eng].append(("inst", fn, None, 0))
        self._upd(tok, reads, writes)
        return tok

    def dma(self, q, out, in_, reads=(), writes=()):
        i = self.dma_i[q] % self.NDS
        self.dma_i[q] += 1
        k = (q, i)
        sem = self.dma_sems[k]
        if self.dma_cnt[k] > 0:
            self._wait(q, (sem, self.dma_cnt[k]))
        self._deps(q, reads, writes)
        self.dma_cnt[k] += 16
        tok = (sem, self.dma_cnt[k])
        self.prog[q].append(("inst", lambda e, o=out, i_=in_: e.dma_start(out=o, in_=i_), sem, 16))
        self._upd(tok, reads, writes)
        return tok

    def barrier(self):
        toks = [(self.sem[n], self.cnt[n]) for n in self.names if self.cnt[n] > 0]
        toks += [(self.dma_sems[k], v) for k, v in self.dma_cnt.items() if v > 0]
        for e in self.names:
            for t in toks:
                self._wait(e, t)

    def replay(self):
        self.barrier()
        with self.nc.Block() as block:
            def mk(name):
                items = self.prog[name]

                def body(e):
                    for it in items:
                        if it[0] == "wait":
                            e.wait_ge(it[1], it[2])
                        else:
                            ins = it[1](e)
                            if it[2] is not None:
                                ins.then_inc(it[2], it[3])
                return body
            block.tensor(mk("tensor"))
            block.vector(mk("vector"))
            block.scalar(mk("scalar"))
            block.gpsimd(mk("gpsimd"))
            block.sync(mk("sync"))
        self.prog = {n: [] for n in self.names}


class Rot:
    def __init__(self, tiles, x=False):
        self.tiles = [(t, Res(x)) for t in tiles]
        self.i = 0

    def next(self):
        t = self.tiles[self.i % len(self.tiles)]
        self.i += 1
        return t


def build(TS, TP, PAST):
    NTOK = TS + 2 * TP
    NT = NTOK // 128
    NTS = TS // 128
    PG = 2 * TP
    assert TS % 512 == 0 and PG <= 512 and PG % 128 == 0 and PAST % 128 == 0
    groups = [(g * 512, 512) for g in range(TS // 512)] + [(TS, PG)]
    seqs = [(0, TS, True, 0), (TS, TP, False, 0), (TS + TP, TP, False, 1)]

    nc = bass.Bass("TRN2", target_bir_lowering=False)

    def din(name, shape, dt=F32):
        return nc.dram_tensor(name, shape, dt, kind="ExternalInput").ap()

    def dout(name, shape):
        return nc.dram_tensor(name, shape, F32, kind="ExternalOutput").ap()

    def dscr(name, shape, dt=BF16):
        return nc.dram_tensor(name, shape, dt).ap()

    x_s = din("x_s", [TS, D])
    x_p = din("x_p", [PG, D])
    cache_k = din("cache_k", [PAST, H, 2, 128])
    cache_v = din("cache_v", [PAST, H, 256])
    st_f = din("st_f", [H, 256, 512])
    st_b = din("st_b", [H, 256, 512])
    cvec = din("cvec", [4, D])
    gpost = din("gpost", [2, D])
    w_ada = din("w_ada", [D, 6 * D])
    b_ada = din("b_ada", [1, 6 * D])
    w_in = din("w_in", [D, INW])
    glog = din("glog", [1, 16])
    w_ret_o = din("w_ret_o", [4096, D])
    lam4 = din("lam4", [1, 512])
    g_sub = din("g_sub", [1, 256])
    w_diff_o = din("w_diff_o", [D, D])
    w_out = din("w_out", [D, D])
    w_up = din("w_up", [D, DFF])
    w_down = din("w_down", [DFF, D])
    consts = din("consts", [128, NCONST])
    rope_cos = din("rope_cos", [128, TS])
    rope_sin = din("rope_sin", [128, TS])

    y_s = dout("y_s", [TS, D])
    y_p = dout("y_p", [PG, D])
    nk = dout("nk", [PG, D])
    nv = dout("nv", [PG, D])
    nsf = dout("nsf", [2, H, 256, 512])
    nsb = dout("nsb", [2, H, 256, 512])

    mod_scr = dscr("mod_scr", [2, 6 * D], F32)
    gt_scr = dscr("gt_scr", [2, 2, D], F32)
    rqT = dscr("rqT", [16, 128, NTOK])
    rkT = dscr("rkT", [16, 128, NTOK])
    rk_tok = dscr("rk_tok", [NTOK, 2048])
    rv_tok = dscr("rv_tok", [NTOK, 4096])
    rg_tok = dscr("rg_tok", [NTOK, 4096])
    dqT = dscr("dqT", [16, 128, NTOK])
    dkT = dscr("dkT", [16, 128, NTOK])
    dv_tok = dscr("dv_tok", [NTOK, 2048])
    gatesT = dscr("gatesT", [32, 128, NTOK])
    goT = dscr("goT", [32, 128, NTOK])
    doT = dscr("doT", [16, 128, NTOK])
    x1_scr = dscr("x1_scr", [NTOK, D], F32)
    wro_b = dscr("wro_b", [4, 128, 32, 512])
    wdo_b = dscr("wdo_b", [4, 128, 16, 512])
    wout_b = dscr("wout_b", [4, 128, 16, 512])
    wup_b = dscr("wup_b", [16, 128, 16, 512])
    wdn_b = dscr("wdn_b", [16, 128, 16, 512])

    def xrows(tt):
        t0 = tt * 128
        return x_s[t0:t0 + 128, :] if t0 < TS else x_p[t0 - TS:t0 - TS + 128, :]

    def yrows(tt):
        t0 = tt * 128
        return y_s[t0:t0 + 128, :] if t0 < TS else y_p[t0 - TS:t0 - TS + 128, :]

    with ExitStack() as top:
        fw = FW(nc, top)

        uniq = [0]

        def sbt(st, name, shape, dt=F32):
            uniq[0] += 1
            return st.enter_context(nc.sbuf_tensor("%s_%d" % (name, uniq[0]), shape, dt))

        def pst(st, name, dt=F32):
            uniq[0] += 1
            return st.enter_context(nc.psum_tensor("%s_%d" % (name, uniq[0]), [128, 512 if dt == F32 else 1024], dt))

        C = sbt(top, "C", [128, NCONST])
        identf = C[:, 0:128]
        Umask = C[:, 128:256]
        Lmask = C[:, 256:384]
        relF = C[:, 384:512]
        relB = C[:, 512:640]
        iota1 = C[:, 640:768]
        iotaB = C[:, 768:896]
        colA = C[:, 896:897]
        colB = C[:, 897:898]
        identb = sbt(top, "identb", [128, 128], BF16)
        onesb = sbt(top, "onesb", [128, 128], BF16)
        onesf = sbt(top, "onesf", [128, 128])
        Pmat = sbt(top, "Pmat", [128, 128], BF16)
        modF = sbt(top, "modF", [128, 2, 96])
        vF = sbt(top, "vF", [128, 64])
        A1 = sbt(top, "A1", [128, 2, 16])
        A2 = sbt(top, "A2", [128, 2, 16])
        maskT = sbt(top, "maskT", [128, H, 128])
        XIF = sbt(top, "XIF", [128, H, 128])
        XIB = sbt(top, "XIB", [128, H, 128])
        LG = sbt(top, "LG", [128, 16])
        ZF = sbt(top, "ZF", [128, H])
        ZB = sbt(top, "ZB", [128, H])
        GF = sbt(top, "GF", [128, H])
        GB = sbt(top, "GB", [128, H])
        neglam = sbt(top, "neglam", [128, 1])
        GS = sbt(top, "GS", [128, 2])
        RP = Res()

        with ExitStack() as ph:
            ps0 = pst(ph, "ps0")
            ps1 = pst(ph, "ps1")
            rps0, rps1 = Res(True), Res(True)
            fw.dma("sync", C[:], consts, writes=[RP])
            fw.op("vector", lambda e: e.tensor_copy(out=identb[:], in_=identf), reads=[RP], writes=[RP])
            fw.op("vector", lambda e: e.memset(onesb[:], 1.0), writes=[RP])
            fw.op("vector", lambda e: e.memset(onesf[:], 1.0), writes=[RP])
            fw.op("vector", lambda e: e.tensor_copy(out=Pmat[:], in_=C[:, 898:1026]), reads=[RP], writes=[RP])
            v64 = sbt(ph, "v64", [64, 128])
            rv64 = Res()
            fw.dma("sync", v64[:], cvec.rearrange("r (k p) -> (r k) p", p=128), writes=[rv64])
            fw.op("tensor", lambda e: e.transpose(out=ps0[:, 0:64], in_=v64[:], identity=identf[0:64, 0:64]),
                  reads=[rv64, RP], writes=[rps0])
            fw.op("vector", lambda e: e.tensor_copy(out=vF[:], in_=ps0[:, 0:64]), reads=[rps0], writes=[RP])
            silF = sbt(ph, "silF", [128, 32])
            rsil = Res()
            fw.op("scalar", lambda e: e.activation(out=silF[:], in_=vF[:, 0:32], func=AF.Silu), reads=[RP], writes=[rsil])
            mrow = sbt(ph, "mrow", [2, 6 * D])
            rm = Res()
            fw.dma("sync", mrow[:], b_ada.partition_broadcast(2), writes=[rm])
            wa = [sbt(ph, "wa%d" % i, [128, KC, 512]) for i in range(2)]
            rwa = [Res(), Res()]
            for nb in range(24):
                b = nb % 2
                fw.dma("sync", wa[b][:], w_ada[:, nb * 512:(nb + 1) * 512].rearrange("(k p) n -> p k n", p=128), writes=[rwa[b]])
                for k in range(KC):
                    fw.op("tensor", lambda e, k=k, b=b: e.matmul(ps1[0:2, :], lhsT=silF[:, k:32:16], rhs=wa[b][:, k, :],
                                                                 start=(k == 0), stop=(k == KC - 1)),
                          reads=[rsil, rwa[b]], writes=[rps1], inc=(k == KC - 1))
                fw.op("vector", lambda e, nb=nb: e.tensor_tensor(out=mrow[:, nb * 512:(nb + 1) * 512], in0=ps1[0:2, :],
                                                                 in1=mrow[:, nb * 512:(nb + 1) * 512], op=ALU.add),
                      reads=[rps1], writes=[rm])
            rms = Res()
            fw.dma("sync", mod_scr, mrow[:], reads=[rm], writes=[rms])
            gp = sbt(ph, "gp", [2, 2, D])
            rgp = Res()
            fw.dma("sync", gp[:, 0, :], gpost[0:1, :].partition_broadcast(2), writes=[rgp])
            fw.dma("sync", gp[:, 1, :], gpost[1:2, :].partition_broadcast(2), writes=[rgp])
            fw.op("vector", lambda e: e.tensor_tensor(out=gp[:, 0, :], in0=gp[:, 0, :], in1=mrow[:, 2 * D:3 * D], op=ALU.mult),
                  reads=[rm], writes=[rgp])
            fw.op("vector", lambda e: e.tensor_tensor(out=gp[:, 1, :], in0=gp[:, 1, :], in1=mrow[:, 5 * D:6 * D], op=ALU.mult),
                  reads=[rm], writes=[rgp])
            fw.dma("sync", gt_scr, gp[:], reads=[rgp])
            m96 = sbt(ph, "m96", [96, 2, 128])
            rm96 = Res()
            for c in range(2):
                fw.dma("sync", m96[:, c, :], mod_scr[c:c + 1, :].rearrange("o (k p) -> (o k) p", p=128), reads=[rms], writes=[rm96])
            for c in range(2):
                fw.op("tensor", lambda e, c=c: e.transpose(out=ps0[:, c * 96:(c + 1) * 96], in_=m96[:, c, :], identity=identf[0:96, 0:96]),
                      reads=[rm96, RP], writes=[rps0])
            fw.op("vector", lambda e: e.tensor_copy(out=modF[:].rearrange("p c f -> p (c f)"), in_=ps0[:, 0:192]), reads=[rps0], writes=[RP])
            for c in range(2):
                fw.op("vector", lambda e, c=c: e.scalar_tensor_tensor(out=A1[:, c, :], in0=modF[:, c, 16:32], scalar=1.0, in1=vF[:, 32:48],
                                                                      op0=ALU.add, op1=ALU.mult), reads=[RP], writes=[RP])
                fw.op("vector", lambda e, c=c: e.scalar_tensor_tensor(out=A2[:, c, :], in0=modF[:, c, 64:80], scalar=1.0, in1=vF[:, 48:64],
                                                                      op0=ALU.add, op1=ALU.mult), reads=[RP], writes=[RP])
            l4 = sbt(ph, "l4", [128, 512])
            rl4 = Res()
            fw.dma("sync", l4[:], lam4.partition_broadcast(128), writes=[rl4])
            lp = sbt(ph, "lp", [128, 2, 128])
            ls = sbt(ph, "ls", [128, 2])
            fw.op("vector", lambda e: e.tensor_tensor(out=lp[:, 0, :], in0=l4[:, 0:128], in1=l4[:, 128:256], op=ALU.mult), reads=[rl4], writes=[rl4])
            fw.op("vector", lambda e: e.tensor_tensor(out=lp[:, 1, :], in0=l4[:, 256:384], in1=l4[:, 384:512], op=ALU.mult), reads=[rl4], writes=[rl4])
            fw.op("vector", lambda e: e.reduce_sum(out=ls[:], in_=lp[:], axis=AX.X), reads=[rl4], writes=[rl4])
            fw.op("scalar", lambda e: e.activation(out=ls[:], in_=ls[:], func=AF.Exp), reads=[rl4], writes=[rl4])
            fw.op("vector", lambda e: e.tensor_tensor(out=neglam[:], in0=ls[:, 1:2], in1=ls[:, 0:1], op=ALU.subtract), reads=[rl4], writes=[RP])
            fw.op("vector", lambda e: e.tensor_scalar_add(out=neglam[:], in0=neglam[:], scalar1=-LAM_INIT), reads=[RP], writes=[RP])
            fw.dma("sync", LG[:], glog.partition_broadcast(128), writes=[RP])
            fw.op("scalar", lambda e: e.activation(out=LG[:], in_=LG[:], func=AF.Sigmoid), reads=[RP], writes=[RP])
            fw.op("scalar", lambda e: e.activation(out=LG[:], in_=LG[:], func=AF.Ln), reads=[RP], writes=[RP])
            tz = sbt(ph, "tz", [128, 4, H])
            fw.op("vector", lambda e: e.tensor_scalar(out=tz[:, 0, :], in0=LG[:, 0:8], scalar1=colA, scalar2=None, op0=ALU.mult), reads=[RP], writes=[RP])
            fw.op("vector", lambda e: e.tensor_scalar(out=tz[:, 1, :], in0=LG[:, 8:16], scalar1=colB, scalar2=None, op0=ALU.mult), reads=[RP], writes=[RP])
            fw.op("vector", lambda e: e.tensor_scalar(out=tz[:, 2, :], in0=LG[:, 0:8], scalar1=128.0, scalar2=None, op0=ALU.mult), reads=[RP], writes=[RP])
            fw.op("vector", lambda e: e.tensor_scalar(out=tz[:, 3, :], in0=LG[:, 8:16], scalar1=128.0, scalar2=None, op0=ALU.mult), reads=[RP], writes=[RP])
            for i, dst in enumerate([ZF, ZB, GF, GB]):
                fw.op("scalar", lambda e, i=i, dst=dst: e.activation(out=dst[:], in_=tz[:, i, :], func=AF.Exp), reads=[RP], writes=[RP])
            mtmp = sbt(ph, "mtmp", [128, 2, 128])
            for h in range(H):
                fw.op("scalar", lambda e, h=h: e.activation(out=mtmp[:, 0, :], in_=relF, func=AF.Exp, scale=LG[:, h:h + 1]), reads=[RP], writes=[RP])
                fw.op("scalar", lambda e, h=h: e.activation(out=mtmp[:, 1, :], in_=relB, func=AF.Exp, scale=LG[:, 8 + h:9 + h]), reads=[RP], writes=[RP])
                fw.op("vector", lambda e: e.tensor_tensor(out=mtmp[:, 0, :], in0=mtmp[:, 0, :], in1=Umask, op=ALU.mult), reads=[RP], writes=[RP])
                fw.op("vector", lambda e: e.tensor_tensor(out=mtmp[:, 1, :], in0=mtmp[:, 1, :], in1=Lmask, op=ALU.mult), reads=[RP], writes=[RP])
                fw.op("vector", lambda e, h=h: e.tensor_tensor(out=maskT[:, h, :], in0=mtmp[:, 0, :], in1=mtmp[:, 1, :], op=ALU.add), reads=[RP], writes=[RP])
                fw.op("scalar", lambda e, h=h: e.activation(out=XIF[:, h, :], in_=iota1, func=AF.Exp, scale=LG[:, h:h + 1]), reads=[RP], writes=[RP])
                fw.op("scalar", lambda e, h=h: e.activation(out=XIB[:, h, :], in_=iotaB, func=AF.Exp, scale=LG[:, 8 + h:9 + h]), reads=[RP], writes=[RP])
            g2 = sbt(ph, "g2", [2, 128])
            rg2 = Res()
            fw.dma("sync", g2[:], g_sub.rearrange("o (h p) -> (o h) p", p=128), writes=[rg2])
            fw.op("tensor", lambda e: e.transpose(out=ps1[:, 0:2], in_=g2[:], identity=identf[0:2, 0:2]), reads=[rg2, RP, rps1], writes=[rps1])
            fw.op("vector", lambda e: e.tensor_scalar(out=GS[:], in0=ps1[:, 0:2], scalar1=1.0 - LAM_INIT, scalar2=None, op0=ALU.mult),
                  reads=[rps1], writes=[RP])
            fw.replay()
        if os.environ.get("KSTOP") == "0":
            return nc

        def norm_mod_T(tmp, src, rsrc, cond, Amat, shift_off, dst, rdst, psT):
            st = norm_part1(tmp, src, rsrc, psT)
            norm_part2(st, cond, Amat, shift_off, dst, rdst)

        def norm_part1(tmp, src, rsrc, psT):
            ss, rss = tmp["ss"].next()
            junk, rj = tmp["junk"].next()
            xn, rxn = tmp["xn"].next()
            fw.op("scalar", lambda e: e.activation(out=junk[:], in_=src, func=AF.Square, scale=1.0 / math.sqrt(D), accum_out=ss[:, 0:1]),
                  reads=[rsrc], writes=[rj, rss])
            fw.op("scalar", lambda e: e.activation(out=ss[:, 0:1], in_=ss[:, 0:1], func=AF.Sqrt, bias=EPS, scale=1.0), reads=[rss], writes=[rss])
            fw.op("vector", lambda e: e.reciprocal(out=ss[:, 0:1], in_=ss[:, 0:1]), reads=[rss], writes=[rss])
            fw.op("scalar", lambda e: e.activation(out=xn[:], in_=src, func=AF.Copy, scale=ss[:, 0:1]), reads=[rsrc, rss], writes=[rxn])
            pts = []
            for half in range(2):
                pt, rpt = psT.next()
                for j in range(8):
                    k = half * 8 + j
                    fw.op("tensor", lambda e, k=k, j=j, pt=pt: e.transpose(out=pt[:, j * 128:(j + 1) * 128], in_=xn[:, k * 128:(k + 1) * 128], identity=identb[:]),
                          reads=[rxn], writes=[rpt], inc=(j == 7))
                pts.append((pt, rpt))
            return pts

        def norm_part2(pts, cond, Amat, shift_off, dst, rdst):
            for half in range(2):
                pt, rpt = pts[half]
                for j in range(8):
                    k = half * 8 + j
                    if j % 2 == 0:
                        fw.op("vector", lambda e, k=k, j=j, pt=pt: e.tensor_scalar(out=dst(k), in0=pt[:, j * 128:(j + 1) * 128], scalar1=Amat[:, cond, k:k + 1],
                                                                                   scalar2=modF[:, cond, shift_off + k:shift_off + k + 1], op0=ALU.mult, op1=ALU.add),
                              reads=[rpt], writes=[rdst])
                    else:
                        fw.op("scalar", lambda e, k=k, j=j, pt=pt: e.activation(out=dst(k), in_=pt[:, j * 128:(j + 1) * 128], func=AF.Identity,
                                                                                scale=Amat[:, cond, k:k + 1], bias=modF[:, cond, shift_off + k:shift_off + k + 1]),
                              reads=[rpt], writes=[rdst])

        with ExitStack() as phAB:
            hT = sbt(phAB, "hT", [128, KC, NTOK], BF16)
            rhT = Res()
            with ExitStack() as ph:
                xt = Rot([sbt(ph, "xt%d" % i, [128, D]) for i in range(3)])
                tmp = {"ss": Rot([sbt(ph, "ss%d" % i, [128, 1]) for i in range(4)]),
                       "junk": Rot([sbt(ph, "junk", [128, D], BF16)]),
                       "xn": Rot([sbt(ph, "xn%d" % i, [128, D], BF16) for i in range(2)])}
                psT = Rot([pst(ph, "psT%d" % i, BF16) for i in range(4)], x=True)
                def partA(tt):
                    x, rx = xt.next()
                    fw.dma("sync", x[:], xrows(tt), writes=[rx])
                    return norm_part1(tmp, x[:], rx, psT)

                stA = partA(0)
                for tt in range(NT):
                    stN = partA(tt + 1) if tt + 1 < NT else None
                    cond = 0 if tt < NTS else 1
                    norm_part2(stA, cond, A1, 0, lambda k, tt=tt: hT[:, k, tt * 128:(tt + 1) * 128], rhT)
                    stA = stN
                fw.replay()
            if os.environ.get("KSTOP") == "A":
                return nc

            with ExitStack() as ph:
                W = Rot([sbt(ph, "W%d" % i, [128, KC, 512], BF16) for i in range(2)])
                cosT = sbt(ph, "cosT", [128, TS])
                sinT = sbt(ph, "sinT", [128, TS])
                rrope = Res()
                fw.dma("sync", cosT[:], rope_cos, writes=[rrope])
                fw.dma("sync", sinT[:], rope_sin, writes=[rrope])
                stg = Rot([sbt(ph, "stg%d" % i, [128, 512], BF16) for i in range(4)])
                stf = Rot([sbt(ph, "stf%d" % i, [128, 512]) for i in range(2)])
                qraw = Rot([sbt(ph, "qraw%d" % i, [128, 512], BF16) for i in range(2)])
                t1s = Rot([sbt(ph, "t1_%d" % i, [128, 512]) for i in range(2)])
                t2s = Rot([sbt(ph, "t2_%d" % i, [128, 512]) for i in range(2)])
                pmain = Rot([pst(ph, "pm%d" % i) for i in range(5)], x=True)
                prot = Rot([pst(ph, "pr%d" % i) for i in range(2)], x=True)
                flip = [0]

                def evac_copy(dst, ps, rps, rdst, scale=None):
                    flip[0] ^= 1
                    if flip[0]:
                        if scale is None:
                            fw.op("vector", lambda e: e.tensor_copy(out=dst, in_=ps), reads=[rps], writes=[rdst])
                        else:
                            fw.op("vector", lambda e: e.tensor_scalar(out=dst, in0=ps, scalar1=scale, scalar2=None, op0=ALU.mult), reads=[rps], writes=[rdst])
                    else:
                        fw.op("scalar", lambda e: e.activation(out=dst, in_=ps, func=AF.Copy, scale=(1.0 if scale is None else scale)),
                              reads=[rps], writes=[rdst])

                wbl = range(INW // 512)
                if os.environ.get("KWB"):
                    wbl = [int(v) for v in os.environ["KWB"].split(",")]
                for wb in wbl:
                    col0 = wb * 512
                    Wt, rW = W.next()
                    fw.dma("gpsimd", Wt[:], w_in[:, col0:col0 + 512].rearrange("(k p) n -> p k n", p=128), writes=[rW])
                    if wb < 4:
                        kind, fdst, fbase = "rq", rqT, 0
                    elif wb < 8:
                        kind, fdst, fbase = "rk", rkT, 2048
                    elif wb < 16:
                        kind = "rv"
                    elif wb < 24:
                        kind = "rg"
                    elif wb < 28:
                        kind, fdst, fbase = "dq", dqT, 12288
                    elif wb < 32:
                        kind, fdst, fbase = "dk", dkT, 14336
                    elif wb < 36:
                        kind = "dv"
                    else:
                        kind, fdst, fbase = "gate", gatesT, 18432
                    if kind in ("rq", "rk", "dq", "dk", "gate"):
                        for sub in range(4):
                            fci = (col0 + sub * 128 - fbase) // 128
                            for gi, (t0, tn) in enumerate(groups):
                                ps, rps = pmain.next()
                                for k in range(KC):
                                    fw.op("tensor", lambda e, k=k, ps=ps, Wt=Wt, sub=sub, t0=t0, tn=tn: e.matmul(
                                        ps[:, :tn], lhsT=Wt[:, k, sub * 128:(sub + 1) * 128], rhs=hT[:, k, t0:t0 + tn], start=(k == 0), stop=(k == KC - 1)),
                                        reads=[rW], writes=[rps], inc=(k == KC - 1))
                                so, rso = stg.next()
                                if kind in ("dq", "dk") and t0 < TS:
                                    qr, rqr = qraw.next()
                                    fw.op("scalar", lambda e, qr=qr, ps=ps, tn=tn: e.activation(out=qr[:, :tn], in_=ps[:, :tn], func=AF.Copy), reads=[rps], writes=[rqr])
                                    pr, rpr = prot.next()
                                    fw.op("tensor", lambda e, pr=pr, qr=qr, tn=tn: e.matmul(pr[:, :tn], lhsT=Pmat[:], rhs=qr[:, :tn], start=True, stop=True),
                                          reads=[rqr], writes=[rpr])
                                    t1, rt1 = t1s.next()
                                    t2, rt2 = t2s.next()
                                    fw.op("vector", lambda e, t1=t1, ps=ps, t0=t0, tn=tn: e.tensor_tensor(out=t1[:, :tn], in0=ps[:, :tn], in1=cosT[:, t0:t0 + tn], op=ALU.mult),
                                          reads=[rps, rrope], writes=[rt1])
                                    fw.op("vector", lambda e, t2=t2, pr=pr, t0=t0, tn=tn: e.tensor_tensor(out=t2[:, :tn], in0=pr[:, :tn], in1=sinT[:, t0:t0 + tn], op=ALU.mult),
                                          reads=[rpr, rrope], writes=[rt2])
                                    fw.op("vector", lambda e, so=so, t1=t1, t2=t2, tn=tn: e.tensor_tensor(out=so[:, :tn], in0=t1[:, :tn], in1=t2[:, :tn], op=ALU.add),
                                          reads=[rt1, rt2], writes=[rso])
                                elif kind == "gate":
                                    fw.op("scalar", lambda e, so=so, ps=ps, tn=tn: e.activation(out=so[:, :tn], in_=ps[:, :tn], func=AF.Sigmoid), reads=[rps], writes=[rso])
                                else:
                                    evac_copy(so[:, :tn], ps[:, :tn], rps, rso, scale=(0.0625 if kind == "rk" else None))
                                fw.dma("sync", fdst[fci, :, t0:t0 + tn], so[:, :tn], reads=[rso])
                    if kind in ("rk", "rv", "rg", "dv", "dk"):
                        tiles = range(NTS, NT) if kind == "dk" else range(NT)
                        for tt in tiles:
                            ps, rps = pmain.next()
                            for k in range(KC):
                                fw.op("tensor", lambda e, k=k, ps=ps, Wt=Wt, tt=tt: e.matmul(
                                    ps[:, :], lhsT=hT[:, k, tt * 128:(tt + 1) * 128], rhs=Wt[:, k, :], start=(k == 0), stop=(k == KC - 1)),
                                    reads=[rW], writes=[rps], inc=(k == KC - 1))
                            r0 = tt * 128
                            if kind != "dk":
                                so, rso = stg.next()
                                evac_copy(so[:], ps[:], rps, rso, scale=(0.0625 if kind == "rk" else None))
                                if kind == "rk":
                                    dd = rk_tok[r0:r0 + 128, col0 - 2048:col0 - 2048 + 512]
                                elif kind == "rv":
                                    dd = rv_tok[r0:r0 + 128, col0 - 4096:col0 - 4096 + 512]
                                elif kind == "rg":
                                    dd = rg_tok[r0:r0 + 128, col0 - 8192:col0 - 8192 + 512]
                                else:
                                    dd = dv_tok[r0:r0 + 128, col0 - 16384:col0 - 16384 + 512]
                                fw.dma("sync", dd, so[:], reads=[rso])
                            if kind in ("dk", "dv") and tt >= NTS:
                                sf, rsf = stf.next()
                                fw.op("scalar", lambda e, sf=sf, ps=ps: e.activation(out=sf[:], in_=ps[:], func=AF.Copy), reads=[rps], writes=[rsf])
                                pr0 = r0 - TS
                                if kind == "dk":
                                    dd = nk[pr0:pr0 + 128, col0 - 14336:col0 - 14336 + 512]
                                else:
                                    dd = nv[pr0:pr0 + 128, col0 - 16384:col0 - 16384 + 512]
                                fw.dma("sync", dd, sf[:], reads=[rsf])
                fw.replay()

        if os.environ.get("KSTOP") == "B":
            return nc
        with ExitStack() as ph:
            TM = TS
            nmax = TM // 128
            qTs = Rot([sbt(ph, "qT%d" % i, [128, 2, TM], BF16) for i in range(2)])
            kTs = Rot([sbt(ph, "kT%d" % i, [128, 2, TM], BF16) for i in range(2)])
            kts = Rot([sbt(ph, "kt%d" % i, [128, nmax, 256], BF16) for i in range(1)])
            vs = Rot([sbt(ph, "v%d" % i, [128, nmax, 512], BF16) for i in range(1)])
            gs = Rot([sbt(ph, "g%d" % i, [128, nmax, 512], BF16) for i in range(1)])
            qf = sbt(ph, "qf", [128, 2, TM], BF16)
            qb = sbt(ph, "qb", [128, 2, TM], BF16)
            rqf, rqb = Res(), Res()
            SbA = sbt(ph, "SbA", [128, nmax, 1024], BF16)
            rSbA = [Res() for _ in range(nmax)]
            Sb = sbt(ph, "Sb", [128, 2, 512])
            Sf = sbt(ph, "Sf", [128, 2, 512])
            rSb, rSf = Res(), Res()
            SfA = sbt(ph, "SfA", [128, nmax, 1024], BF16)
            rSfA = [Res() for _ in range(nmax)]
            goTh = sbt(ph, "goTh", [128, 4, TM], BF16)
            rgoTh = Res()
            aTs = Rot([sbt(ph, "aT%d" % i, [128, 128], BF16) for i in range(3)])
            kzs = Rot([sbt(ph, "kz%d" % i, [128, 256], BF16) for i in range(4)])
            gos = Rot([sbt(ph, "go%d" % i, [128, 512], BF16) for i in range(3)])
            junks = Rot([sbt(ph, "jk%d" % i, [128, 512], BF16) for i in range(1)])
            sss = Rot([sbt(ph, "rss%d" % i, [128, 1]) for i in range(4)])
            p_s = Rot([pst(ph, "p_s%d" % i) for i in range(2)], x=True)
            p_o = Rot([pst(ph, "p_o%d" % i) for i in range(2)], x=True)
            p_kv = Rot([pst(ph, "p_kv%d" % i) for i in range(3)], x=True)
            p_T = Rot([pst(ph, "p_T", BF16)], x=True)

            def ret_head(t0, T, is_s, pidx, h, shared=None):
                    n = T // 128
                    if shared is None:
                        qT, rqT_ = qTs.next()
                        kT, rkT_ = kTs.next()
                        kt, rkt = kts.next()
                        v, rv = vs.next()
                        g, rg = gs.next()
                        fw.dma("sync", qT[:, :, :T], rqT[2 * h:2 * h + 2, :, t0:t0 + T].rearrange("c p t -> p c t"), writes=[rqT_])
                        fw.dma("sync", kT[:, :, :T], rkT[2 * h:2 * h + 2, :, t0:t0 + T].rearrange("c p t -> p c t"), writes=[rkT_])
                        fw.dma("sync", kt[:, :n, :], rk_tok[t0:t0 + T, h * 256:(h + 1) * 256].rearrange("(c p) d -> p c d", p=128), writes=[rkt])
                        fw.dma("sync", v[:, :n, :], rv_tok[t0:t0 + T, h * 512:(h + 1) * 512].rearrange("(c p) d -> p c d", p=128), writes=[rv])
                        fw.dma("sync", g[:, :n, :], rg_tok[t0:t0 + T, h * 512:(h + 1) * 512].rearrange("(c p) d -> p c d", p=128), writes=[rg])
                        fw.op("scalar", lambda e, g=g, n=n: e.activation(out=g[:, :n, :], in_=g[:, :n, :], func=AF.Silu), reads=[rg], writes=[rg])
                        vc = lambda c: v[:, c, :]
                        gc = lambda c: g[:, c, :]
                        ktc = lambda c: kt[:, c, :]
                        qTd = lambda dc: qT[:, dc, :T]
                        kTd = lambda dc: kT[:, dc, :T]
                    else:
                        (qT, rqT_), (kT, rkT_), (kt, rkt), (v, rv), (g, rg) = shared
                        qTflat = qT[:].rearrange("p a t -> p (a t)")
                        kTflat = kT[:].rearrange("p a t -> p (a t)")
                        vc = lambda c: v[:, c * H + h, :]
                        gc = lambda c: g[:, c * H + h, :]
                        ktc = lambda c: kt[:, c * H + h, :]
                        qTd = lambda dc: qTflat[:, (2 * h + dc) * T:(2 * h + dc + 1) * T]
                        kTd = lambda dc: kTflat[:, (2 * h + dc) * T:(2 * h + dc + 1) * T]
                    if is_s:
                        fw.dma("sync", Sb[:], st_b[h].rearrange("(c p) e -> p c e", p=128), writes=[rSb])
                        fw.dma("sync", Sf[:], st_f[h].rearrange("(c p) e -> p c e", p=128), writes=[rSf])
                    else:
                        fw.op("gpsimd", lambda e: e.memset(Sb[:], 0.0), writes=[rSb])
                        fw.op("gpsimd", lambda e: e.memset(Sf[:], 0.0), writes=[rSf])
                    for dc in range(2):
                        fw.op("vector", lambda e, dc=dc, n=n, T=T, h=h: e.tensor_tensor(
                            out=qf[:, dc, :T].rearrange("p (c i) -> p c i", i=128), in0=qTd(dc).rearrange("p (c i) -> p c i", i=128),
                            in1=XIF[:, h:h + 1, :].broadcast_to([128, n, 128]), op=ALU.mult), reads=[rqT_], writes=[rqf])
                        fw.op("vector", lambda e, dc=dc, n=n, T=T, h=h: e.tensor_tensor(
                            out=qb[:, dc, :T].rearrange("p (c i) -> p c i", i=128), in0=qTd(dc).rearrange("p (c i) -> p c i", i=128),
                            in1=XIB[:, h:h + 1, :].broadcast_to([128, n, 128]), op=ALU.mult), reads=[rqT_], writes=[rqb])

                    steps = []
                    for i in range(n):
                        cb = n - 1 - i
                        if not (cb == 0 and is_s):
                            steps.append(("b", cb))
                        if i < n - 1 or not is_s:
                            steps.append(("f", i))

                    def emit_kz(st):
                        d, c = st
                        Zt = ZB if d == "b" else ZF
                        kz, rkz = kzs.next()
                        fw.op("scalar", lambda e, kz=kz, c=c, Zt=Zt: e.activation(out=kz[:], in_=ktc(c), func=AF.Copy, scale=Zt[:, h:h + 1]),
                              reads=[rkt], writes=[rkz])
                        return kz, rkz

                    fw.op("scalar", lambda e, n=n: e.activation(out=SbA[:, n - 1, :], in_=Sb[:].rearrange("p c e -> p (c e)"), func=AF.Copy),
                          reads=[rSb], writes=[rSbA[n - 1]])
                    fw.op("scalar", lambda e: e.activation(out=SfA[:, 0, :], in_=Sf[:].rearrange("p c e -> p (c e)"), func=AF.Copy),
                          reads=[rSf], writes=[rSfA[0]])
                    nxt = emit_kz(steps[0]) if steps else None
                    for si, (d, c) in enumerate(steps):
                        kz, rkz = nxt
                        nxt = emit_kz(steps[si + 1]) if si + 1 < len(steps) else None
                        S, rS, Gt = (Sb, rSb, GB) if d == "b" else (Sf, rSf, GF)
                        pks = []
                        for dc in range(2):
                            pk, rpk = p_kv.next()
                            fw.op("tensor", lambda e, pk=pk, kz=kz, dc=dc, c=c: e.matmul(pk[:], lhsT=kz[:, dc * 128:(dc + 1) * 128], rhs=vc(c), start=True, stop=True),
                                  reads=[rkz, rv], writes=[rpk])
                            pks.append((pk, rpk))
                        for dc in range(2):
                            pk, rpk = pks[dc]
                            fw.op("vector", lambda e, pk=pk, dc=dc, S=S, Gt=Gt: e.scalar_tensor_tensor(out=S[:, dc, :], in0=S[:, dc, :], scalar=Gt[:, h:h + 1], in1=pk[:],
                                                                                                       op0=ALU.mult, op1=ALU.add), reads=[rpk], writes=[rS])
                        if d == "b" and c > 0:
                            fw.op("scalar", lambda e, c=c: e.activation(out=SbA[:, c - 1, :], in_=Sb[:].rearrange("p c e -> p (c e)"), func=AF.Copy),
                                  reads=[rSb], writes=[rSbA[c - 1]])
                        if d == "f" and c < n - 1:
                            fw.op("scalar", lambda e, c=c: e.activation(out=SfA[:, c + 1, :], in_=Sf[:].rearrange("p c e -> p (c e)"), func=AF.Copy),
                                  reads=[rSf], writes=[rSfA[c + 1]])
                    if not is_s:
                        fw.dma("sync", nsb[pidx, h].rearrange("(c p) e -> p c e", p=128), Sb[:], reads=[rSb])
                        fw.dma("sync", nsf[pidx, h].rearrange("(c p) e -> p c e", p=128), Sf[:], reads=[rSf])

                    def stageA(c):
                        cs = slice(c * 128, (c + 1) * 128)
                        pss, rpss = p_s.next()
                        for dc in range(2):
                            fw.op("tensor", lambda e, pss=pss, dc=dc, cs=cs: e.matmul(pss[:, 0:128], lhsT=kTd(dc)[:, cs], rhs=qTd(dc)[:, cs], start=(dc == 0), stop=(dc == 1)),
                                  reads=[rkT_, rqT_], writes=[rpss], inc=(dc == 1))
                        aT, raT = aTs.next()
                        fw.op("vector", lambda e, aT=aT, pss=pss: e.tensor_tensor(out=aT[:], in0=pss[:, 0:128], in1=maskT[:, h, :], op=ALU.mult), reads=[rpss], writes=[raT])
                        return aT, raT

                    def stageB(c, aT, raT):
                        cs = slice(c * 128, (c + 1) * 128)
                        po, rpo = p_o.next()
                        fw.op("tensor", lambda e, po=po, aT=aT, c=c: e.matmul(po[:], lhsT=aT[:], rhs=vc(c), start=True, stop=False), reads=[raT, rv], writes=[rpo], inc=False)
                        for dc in range(2):
                            fw.op("tensor", lambda e, po=po, dc=dc, cs=cs, c=c: e.matmul(po[:], lhsT=qf[:, dc, cs], rhs=SfA[:, c, dc * 512:(dc + 1) * 512], start=False, stop=False),
                                  reads=[rqf, rSfA[c]], writes=[rpo], inc=False)
                        for dc in range(2):
                            fw.op("tensor", lambda e, po=po, dc=dc, cs=cs, c=c: e.matmul(po[:], lhsT=qb[:, dc, cs], rhs=SbA[:, c, dc * 512:(dc + 1) * 512], start=False, stop=(dc == 1)),
                                  reads=[rqb, rSbA[c]], writes=[rpo], inc=(dc == 1))
                        ss, rss = sss.next()
                        jk, rjk = junks.next()
                        fw.op("scalar", lambda e, jk=jk, po=po, ss=ss: e.activation(out=jk[:], in_=po[:], func=AF.Square, scale=1.0 / math.sqrt(512.0), accum_out=ss[:]),
                              reads=[rpo], writes=[rjk, rss])
                        fw.op("scalar", lambda e, ss=ss: e.activation(out=ss[:], in_=ss[:], func=AF.Sqrt, bias=EPS, scale=1.0), reads=[rss], writes=[rss])
                        fw.op("vector", lambda e, ss=ss: e.reciprocal(out=ss[:], in_=ss[:]), reads=[rss], writes=[rss])
                        go, rgo = gos.next()
                        fw.op("vector", lambda e, go=go, po=po, ss=ss, c=c: e.scalar_tensor_tensor(out=go[:], in0=po[:], scalar=ss[:], in1=gc(c), op0=ALU.mult, op1=ALU.mult),
                              reads=[rpo, rss, rg], writes=[rgo])
                        return go, rgo, cs

                    def stageC(go, rgo, cs):
                        pT, rpT = p_T.next()
                        for j in range(4):
                            fw.op("tensor", lambda e, pT=pT, go=go, j=j: e.transpose(out=pT[:, j * 128:(j + 1) * 128], in_=go[:, j * 128:(j + 1) * 128], identity=identb[:]),
                                  reads=[rgo], writes=[rpT], inc=(j == 3))
                        fw.op("vector", lambda e, pT=pT, cs=cs: e.tensor_copy(out=goTh[:, :, cs], in_=pT[:, 0:512].rearrange("p (j i) -> p j i", i=128)),
                              reads=[rpT], writes=[rgoTh])

                    a_cur = stageA(0)
                    prevB = None
                    for c in range(n):
                        a_nxt = stageA(c + 1) if c + 1 < n else None
                        b_cur = stageB(c, *a_cur)
                        if prevB is not None:
                            stageC(*prevB)
                        prevB = b_cur
                        a_cur = a_nxt
                    stageC(*prevB)
                    fw.dma("sync", goT[4 * h:4 * h + 4, :, t0:t0 + T].rearrange("c p t -> p c t"), goTh[:, :, :T], reads=[rgoTh])

            for b in range(4):
                fw.dma("gpsimd", wro_b[b], w_ret_o[:, b * 512:(b + 1) * 512].rearrange("(k p) n -> p k n", p=128))
                fw.dma("gpsimd", wdo_b[b], w_diff_o[:, b * 512:(b + 1) * 512].rearrange("(k p) n -> p k n", p=128))
                fw.dma("gpsimd", wout_b[b], w_out[:, b * 512:(b + 1) * 512].rearrange("(k p) n -> p k n", p=128))
            for b in range(16):
                fw.dma("gpsimd", wup_b[b], w_up[:, b * 512:(b + 1) * 512].rearrange("(k p) n -> p k n", p=128))
            for fb in range(4):
                for cb in range(4):
                    fw.dma("gpsimd", wdn_b[fb * 4 + cb], w_down[fb * 2048:(fb + 1) * 2048, cb * 512:(cb + 1) * 512].rearrange("(k p) n -> p k n", p=128))
            for (t0, T, is_s, pidx) in seqs:
                if is_s or T * H > TM:
                    for h in range(H):
                        ret_head(t0, T, is_s, pidx, h)
                else:
                    n = T // 128
                    sh = [qTs.next(), kTs.next(), kts.next(), vs.next(), gs.next()]
                    (qT, rqT_), (kT, rkT_), (kt, rkt), (v, rv), (g, rg) = sh
                    fw.dma("sync", qT[:].rearrange("p a t -> p (a t)")[:, 0:16 * T].rearrange("p (c t) -> p c t", t=T),
                           rqT[:, :, t0:t0 + T].rearrange("c p t -> p c t"), writes=[rqT_])
                    fw.dma("sync", kT[:].rearrange("p a t -> p (a t)")[:, 0:16 * T].rearrange("p (c t) -> p c t", t=T),
                           rkT[:, :, t0:t0 + T].rearrange("c p t -> p c t"), writes=[rkT_])
                    fw.dma("sync", kt[:, 0:n * H, :].rearrange("p (c h) d -> p c (h d)", h=H), rk_tok[t0:t0 + T, :].rearrange("(c p) d -> p c d", p=128), writes=[rkt])
                    fw.dma("sync", v[:, 0:n * H, :].rearrange("p (c h) d -> p c (h d)", h=H), rv_tok[t0:t0 + T, :].rearrange("(c p) d -> p c d", p=128), writes=[rv])
                    fw.dma("sync", g[:, 0:n * H, :].rearrange("p (c h) d -> p c (h d)", h=H), rg_tok[t0:t0 + T, :].rearrange("(c p) d -> p c d", p=128), writes=[rg])
                    fw.op("scalar", lambda e, g=g, n=n: e.activation(out=g[:, 0:n * H, :], in_=g[:, 0:n * H, :], func=AF.Silu), reads=[rg], writes=[rg])
                    for h in range(H):
                        ret_head(t0, T, is_s, pidx, h, shared=sh)
            fw.replay()

        if os.environ.get("KSTOP") == "C":
            return nc
        with ExitStack() as ph:
            TKM = TS + PAST
            NP = PAST // 128
            q2s = Rot([sbt(ph, "q2_%d" % i, [128, 2, TS], BF16) for i in range(2)])
            k2s = Rot([sbt(ph, "k2_%d" % i, [128, 2, TKM], BF16) for i in range(2)])
            v2s = Rot([sbt(ph, "v2_%d" % i, [128, TKM // 128, 256], BF16) for i in range(2)])
            ckf = sbt(ph, "ckf", [128, NP, 256])
            ckb = sbt(ph, "ckb", [128, NP, 256], BF16)
            rckf, rckb = Res(), Res()
            eTs = Rot([sbt(ph, "eT%d" % i, [128, 512], BF16) for i in range(3)])
            o1s = Rot([sbt(ph, "o1_%d" % i, [128, 2, 512]) for i in range(2)])
            esums = Rot([sbt(ph, "esum%d" % i, [128, 2, 512]) for i in range(2)])
            esres = {id(r): (Res(), Res()) for (_, r) in esums.tiles}
            r1 = sbt(ph, "r1", [128, 512])
            rr1 = Res()
            tmpd = Rot([sbt(ph, "tmpd%d" % i, [128, 512]) for i in range(2)])
            sq = sbt(ph, "sq", [128, 2, 512], BF16)
            rsq = Res()
            sd = sbt(ph, "sd", [128, 512])
            rsd = Res()
            dst_ = Rot([sbt(ph, "dst%d" % i, [128, 2, 512], BF16) for i in range(2)])
            p_sT = Rot([pst(ph, "p_sT%d" % i) for i in range(2)], x=True)
            accs = [[(pst(ph, "acc%d_%d" % (m, j)), Res(True)) for j in range(3)] for m in range(2)]
            p_ck = p_sT
            scale = 1.0 / math.sqrt(128.0)

            def attn_head(t0, T, is_s, pidx, h):
                    Tk = T + (PAST if is_s else 0)
                    nk_ = Tk // 128
                    q2, rq2 = q2s.next()
                    k2, rk2 = k2s.next()
                    v2, rv2 = v2s.next()
                    fw.dma("sync", q2[:, :, :T], dqT[2 * h:2 * h + 2, :, t0:t0 + T].rearrange("c p t -> p c t"), writes=[rq2])
                    fw.dma("sync", k2[:, :, :T], dkT[2 * h:2 * h + 2, :, t0:t0 + T].rearrange("c p t -> p c t"), writes=[rk2])
                    fw.dma("sync", v2[:, :T // 128, :], dv_tok[t0:t0 + T, h * 256:(h + 1) * 256].rearrange("(c p) d -> p c d", p=128), writes=[rv2])
                    if is_s:
                        fw.dma("gpsimd", v2[:, T // 128:nk_, :], cache_v[:, h, :].rearrange("(c p) d -> p c d", p=128), writes=[rv2])
                        fw.dma("sync", ckf[:], cache_k[:, h, :, :].rearrange("(c p) m d -> p c (m d)", p=128), writes=[rckf])
                        fw.op("vector", lambda e: e.tensor_copy(out=ckb[:], in_=ckf[:]), reads=[rckf], writes=[rckb])
                        for c in range(NP):
                            pc, rpc = p_ck.next()
                            pcb = pc[:].bitcast(BF16)
                            for m in range(2):
                                fw.op("tensor", lambda e, pcb=pcb, c=c, m=m: e.transpose(out=pcb[:, m * 128:(m + 1) * 128], in_=ckb[:, c, m * 128:(m + 1) * 128], identity=identb[:]),
                                      reads=[rckb], writes=[rpc], inc=(m == 1))
                            fw.op("vector", lambda e, pcb=pcb, c=c, T=T, k2=k2: e.tensor_copy(out=k2[:, :, T + c * 128:T + (c + 1) * 128],
                                                                                               in_=pcb[:, 0:256].rearrange("p (m i) -> p m i", i=128)),
                                  reads=[rpc], writes=[rk2])
                    pend_epi = [None]

                    def epilogue(o1, ro1, q0, qn):
                        fw.op("scalar", lambda e: e.activation(out=sq[:, :, :qn], in_=o1[:, :, :qn], func=AF.Square, scale=1.0 / 16.0), reads=[ro1], writes=[rsq])
                        ps, rps = p_sT.next()
                        for half in range(2):
                            fw.op("tensor", lambda e, half=half: e.matmul(ps[:, :qn], lhsT=onesb[:], rhs=sq[:, half, :qn], start=(half == 0), stop=(half == 1)),
                                  reads=[rsq], writes=[rps], inc=(half == 1))
                        fw.op("scalar", lambda e: e.activation(out=sd[:, :qn], in_=ps[:, :qn], func=AF.Sqrt, bias=SUBLN_EPS, scale=1.0), reads=[rps], writes=[rsd])
                        fw.op("vector", lambda e: e.reciprocal(out=sd[:, :qn], in_=sd[:, :qn]), reads=[rsd], writes=[rsd])
                        ds, rds = dst_.next()
                        for half in range(2):
                            fw.op("vector", lambda e, half=half: e.scalar_tensor_tensor(out=ds[:, half, :qn], in0=o1[:, half, :qn], scalar=GS[:, half:half + 1],
                                                                                      in1=sd[:, :qn], op0=ALU.mult, op1=ALU.mult),
                                  reads=[ro1, rsd], writes=[rds])
                        fw.dma("sync", doT[2 * h:2 * h + 2, :, t0 + q0:t0 + q0 + qn].rearrange("c p t -> p c t"), ds[:, :, :qn], reads=[rds])

                    for q0 in range(0, T, 512):
                        qn = min(512, T - q0)
                        o1, ro1 = o1s.next()
                        for m in range(2):
                            (alo, ralo), (ahi, rahi), (asum, rasum) = accs[m]
                            pend = None
                            es, res_ = esums.next()
                            resh = esres[id(res_)]
                            for kk in range(nk_ + 1):
                                cur = None
                                if kk < nk_:
                                    ps, rps = p_sT.next()
                                    fw.op("tensor", lambda e, ps=ps, m=m, kt=kk, q0=q0, qn=qn, k2=k2, q2=q2: e.matmul(
                                        ps[:, :qn], lhsT=k2[:, m, kt * 128:(kt + 1) * 128], rhs=q2[:, m, q0:q0 + qn], start=True, stop=True),
                                        reads=[rk2, rq2], writes=[rps])
                                    cur = (ps, rps, kk)
                                if pend is None:
                                    pend = cur
                                    continue
                                ps, rps, kt = pend
                                pend = cur
                                eT, reT = eTs.next()
                                fw.op("scalar", lambda e, eT=eT, ps=ps, qn=qn: e.activation(out=eT[:, :qn], in_=ps[:, :qn], func=AF.Exp, scale=scale), reads=[rps], writes=[reT])
                                st_, sp_ = (kt == 0), (kt == nk_ - 1)
                                fw.op("tensor", lambda e, alo=alo, v2=v2, kt=kt, eT=eT, qn=qn, st_=st_, sp_=sp_: e.matmul(
                                    alo[:, :qn], lhsT=v2[:, kt, 0:128], rhs=eT[:, :qn], start=st_, stop=sp_), reads=[rv2, reT], writes=[ralo], inc=False)
                                fw.op("tensor", lambda e, ahi=ahi, v2=v2, kt=kt, eT=eT, qn=qn, st_=st_, sp_=sp_: e.matmul(
                                    ahi[:, :qn], lhsT=v2[:, kt, 128:256], rhs=eT[:, :qn], start=st_, stop=sp_), reads=[rv2, reT], writes=[rahi], inc=True)
                                eng_ = "vector" if kt % 2 == 0 else "gpsimd"
                                if kt < 2:
                                    fw.op(eng_, lambda e, es=es, eT=eT, qn=qn, kt=kt: e.tensor_copy(out=es[:, kt % 2, :qn], in_=eT[:, :qn]), reads=[reT], writes=[resh[kt % 2]])
                                else:
                                    fw.op(eng_, lambda e, es=es, eT=eT, qn=qn, kt=kt: e.tensor_tensor(out=es[:, kt % 2, :qn], in0=es[:, kt % 2, :qn], in1=eT[:, :qn], op=ALU.add),
                                          reads=[reT], writes=[resh[kt % 2]])
                            nh_ = min(2, nk_)
                            for half in range(nh_):
                                fw.op("tensor", lambda e, asum=asum, es=es, qn=qn, half=half, nh_=nh_: e.matmul(
                                    asum[:, :qn], lhsT=onesf[:], rhs=es[:, half, :qn], start=(half == 0), stop=(half == nh_ - 1)), reads=[resh[half]], writes=[rasum], inc=(half == nh_ - 1))
                            if m == 0:
                                fw.op("vector", lambda e, asum=asum, qn=qn: e.reciprocal(out=r1[:, :qn], in_=asum[:, :qn]), reads=[rasum], writes=[rr1])
                                fw.op("vector", lambda e, alo=alo, qn=qn, o1=o1: e.tensor_tensor(out=o1[:, 0, :qn], in0=alo[:, :qn], in1=r1[:, :qn], op=ALU.mult),
                                      reads=[ralo, rr1], writes=[ro1])
                                fw.op("vector", lambda e, ahi=ahi, qn=qn, o1=o1: e.tensor_tensor(out=o1[:, 1, :qn], in0=ahi[:, :qn], in1=r1[:, :qn], op=ALU.mult),
                                      reads=[rahi, rr1], writes=[ro1])
                                if pend_epi[0] is not None:
                                    epilogue(*pend_epi[0])
                                    pend_epi[0] = None
                            else:
                                fw.op("vector", lambda e, asum=asum, qn=qn: e.reciprocal(out=r1[:, :qn], in_=asum[:, :qn]), reads=[rasum], writes=[rr1])
                                fw.op("vector", lambda e, qn=qn: e.tensor_scalar(out=r1[:, :qn], in0=r1[:, :qn], scalar1=neglam[:, 0:1], scalar2=None, op0=ALU.mult),
                                      reads=[rr1], writes=[rr1])
                                for half, (aa, raa) in enumerate([(alo, ralo), (ahi, rahi)]):
                                    td, rtd = tmpd.next()
                                    fw.op("vector", lambda e, td=td, aa=aa, qn=qn: e.tensor_tensor(out=td[:, :qn], in0=aa[:, :qn], in1=r1[:, :qn], op=ALU.mult),
                                          reads=[raa, rr1], writes=[rtd])
                                    fw.op("vector", lambda e, td=td, half=half, qn=qn, o1=o1: e.tensor_tensor(out=o1[:, half, :qn], in0=o1[:, half, :qn], in1=td[:, :qn], op=ALU.add),
                                          reads=[rtd], writes=[ro1])
                        pend_epi[0] = (o1, ro1, q0, qn)
                    epilogue(*pend_epi[0])

            for (t0, T, is_s, pidx) in seqs:
                for h in range(H):
                    attn_head(t0, T, is_s, pidx, h)
            fw.replay()

        if os.environ.get("KSTOP") == "D":
            return nc
        with ExitStack() as phEF:
            actT = sbt(phEF, "actT", [128, KC, 512], BF16)
            ract = Res()
            def ef_group(t0, tn):
                cond = 0 if t0 < TS else 1
                ntl = tn // 128
                with ExitStack() as ph:
                    gd = sbt(ph, "gd", [128, 48, 512], BF16)
                    rgd = Res()
                    fw.dma("sync", gd[:, 0:32, :tn], goT[:, :, t0:t0 + tn].rearrange("c p t -> p c t"), writes=[rgd])
                    fw.dma("sync", gd[:, 32:48, :tn], doT[:, :, t0:t0 + tn].rearrange("c p t -> p c t"), writes=[rgd])
                    Wr = Rot([sbt(ph, "Wr%d" % i, [128, 32, 512], BF16) for i in range(2)])
                    Wd = Rot([sbt(ph, "Wd%d" % i, [128, 16, 512], BF16) for i in range(2)])
                    gts = Rot([sbt(ph, "gts%d" % i, [128, 2, 512], BF16) for i in range(2)])
                    m1s = Rot([sbt(ph, "m1_%d" % i, [128, 512]) for i in range(2)])
                    m2s = Rot([sbt(ph, "m2_%d" % i, [128, 512]) for i in range(2)])
                    p_r = Rot([pst(ph, "p_r%d" % i) for i in range(2)], x=True)
                    p_d = Rot([pst(ph, "p_d%d" % i) for i in range(2)], x=True)
                    for oc in range(KC):
                        sub = oc % 4
                        if sub == 0:
                            wr, rwr = Wr.next()
                            wd, rwd = Wd.next()
                            fw.dma("gpsimd", wr[:], wro_b[oc // 4], writes=[rwr])
                            fw.dma("gpsimd", wd[:], wdo_b[oc // 4], writes=[rwd])
                        gt_, rgt = gts.next()
                        fw.dma("sync", gt_[:, 0, :tn], gatesT[oc, :, t0:t0 + tn], writes=[rgt])
                        fw.dma("sync", gt_[:, 1, :tn], gatesT[16 + oc, :, t0:t0 + tn], writes=[rgt])
                        pr, rpr = p_r.next()
                        pd, rpd = p_d.next()
                        for k in range(32):
                            fw.op("tensor", lambda e, pr=pr, wr=wr, k=k, sub=sub: e.matmul(pr[:, :tn], lhsT=wr[:, k, sub * 128:(sub + 1) * 128], rhs=gd[:, k, :tn], start=(k == 0), stop=(k == 31)),
                                  reads=[rwr, rgd], writes=[rpr], inc=(k == 31))
                        for k in range(16):
                            fw.op("tensor", lambda e, pd=pd, wd=wd, k=k, sub=sub: e.matmul(pd[:, :tn], lhsT=wd[:, k, sub * 128:(sub + 1) * 128], rhs=gd[:, 32 + k, :tn], start=(k == 0), stop=(k == 15)),
                                  reads=[rwd, rgd], writes=[rpd], inc=(k == 15))
                        m1, rm1 = m1s.next()
                        m2, rm2 = m2s.next()
                        fw.op("vector", lambda e, m1=m1, pr=pr, gt_=gt_: e.tensor_tensor(out=m1[:, :tn], in0=pr[:, :tn], in1=gt_[:, 0, :tn], op=ALU.mult),
                              reads=[rpr, rgt], writes=[rm1])
                        fw.op("vector", lambda e, m2=m2, pd=pd, gt_=gt_: e.tensor_tensor(out=m2[:, :tn], in0=pd[:, :tn], in1=gt_[:, 1, :tn], op=ALU.mult),
                              reads=[rpd, rgt], writes=[rm2])
                        fw.op("vector", lambda e, m1=m1, m2=m2, oc=oc: e.tensor_tensor(out=actT[:, oc, :tn], in0=m1[:, :tn], in1=m2[:, :tn], op=ALU.add),
                              reads=[rm1, rm2], writes=[ract])
                    fw.replay()

                with ExitStack() as ph:
                    yz = sbt(ph, "yz", [128, 4, D])
                    ryz = [Res() for _ in range(4)]
                    h2T = sbt(ph, "h2T", [128, KC, 512], BF16)
                    rh2 = Res()
                    W = Rot([sbt(ph, "Wf%d" % i, [128, KC, 512], BF16) for i in range(2)])
                    Wo2 = sbt(ph, "Wo2", [128, KC, 512], BF16)
                    Wo3 = sbt(ph, "Wo3", [128, KC, 512], BF16)
                    GT = sbt(ph, "GT", [128, D])
                    rGT = Res()
                    xl = Rot([sbt(ph, "xl%d" % i, [128, D]) for i in range(2)])
                    tmp = {"ss": Rot([sbt(ph, "ss%d" % i, [128, 8]) for i in range(4)]),
                           "junk": Rot([sbt(ph, "junk", [128, D], BF16)]),
                           "xn": Rot([sbt(ph, "xn%d" % i, [128, D], BF16) for i in range(1)])}
                    rl = Rot([sbt(ph, "rl%d" % i, [128, 512]) for i in range(2)])
                    psT = Rot([pst(ph, "psT%d" % i, BF16) for i in range(2)], x=True)
                    pm = Rot([pst(ph, "pm%d" % i) for i in range(6)], x=True)
                    rx1 = [Res() for _ in range(4)]

                    def post_norm_residual(tt, which, src_rows, rsrc, dst_dram, rdst, keep=None):
                        ss, rss = tmp["ss"].next()
                        jk, rjk = tmp["junk"].next()
                        fw.op("scalar", lambda e: e.activation(out=jk[:], in_=yz[:, tt, :], func=AF.Square, scale=1.0 / math.sqrt(D), accum_out=ss[:, 0:1]),
                              reads=[ryz[tt]], writes=[rjk, rss])
                        fw.op("scalar", lambda e: e.activation(out=ss[:, 0:1], in_=ss[:, 0:1], func=AF.Sqrt, bias=EPS, scale=1.0), reads=[rss], writes=[rss])
                        fw.op("vector", lambda e: e.reciprocal(out=ss[:, 0:1], in_=ss[:, 0:1]), reads=[rss], writes=[rss])
                        x, rx = xl.next()
                        fw.dma("sync", x[:], src_rows, reads=rsrc, writes=[rx])
                        fw.op("vector", lambda e: e.scalar_tensor_tensor(out=yz[:, tt, :], in0=yz[:, tt, :], scalar=ss[:, 0:1], in1=GT[:], op0=ALU.mult, op1=ALU.mult),
                              reads=[rss, rGT], writes=[ryz[tt]])
                        fw.op("vector", lambda e: e.tensor_tensor(out=yz[:, tt, :], in0=yz[:, tt, :], in1=x[:], op=ALU.add), reads=[rx], writes=[ryz[tt]])
                        fw.dma("sync", dst_dram, yz[:, tt, :], reads=[ryz[tt]], writes=rdst)

                    fw.dma("sync", GT[:], gt_scr[cond, 0:1, :].partition_broadcast(128), writes=[rGT])
                    Wo = [W.tiles[0], W.tiles[1], (Wo2, Res()), (Wo3, Res())]
                    for cb in range(4):
                        fw.dma("gpsimd", Wo[cb][0][:], wout_b[cb], writes=[Wo[cb][1]])

                    def mm_wout(tt):
                        for cb in range(4):
                            Wt, rW = Wo[cb]
                            ps, rps = pm.next()
                            for k in range(KC):
                                fw.op("tensor", lambda e, ps=ps, Wt=Wt, k=k, tt=tt: e.matmul(ps[:], lhsT=actT[:, k, tt * 128:(tt + 1) * 128], rhs=Wt[:, k, :],
                                                                                            start=(k == 0), stop=(k == KC - 1)),
                                      reads=[rW, ract], writes=[rps], inc=(k == KC - 1))
                            fw.op("scalar", lambda e, ps=ps, tt=tt, cb=cb: e.activation(out=yz[:, tt, cb * 512:(cb + 1) * 512], in_=ps[:], func=AF.Copy),
                                  reads=[rps], writes=[ryz[tt]])

                    mm_wout(0)
                    for tt in range(ntl):
                        if tt + 1 < ntl:
                            mm_wout(tt + 1)
                        gtt = t0 // 128 + tt
                        post_norm_residual(tt, 0, xrows(gtt), [], x1_scr[gtt * 128:(gtt + 1) * 128, :], [rx1[tt]])
                        norm_mod_T(tmp, yz[:, tt, :], ryz[tt], cond, A2, 48, lambda k, tt=tt: h2T[:, k, tt * 128:(tt + 1) * 128], rh2, psT)
                    fw.dma("sync", GT[:], gt_scr[cond, 1:2, :].partition_broadcast(128), writes=[rGT])
                    for fb in range(4):
                        for ub in range(4):
                            Wt, rW = W.next()
                            c0 = fb * 2048 + ub * 512
                            fw.dma("gpsimd", Wt[:], wup_b[fb * 4 + ub], writes=[rW])
                            for sub in range(4):
                                ps, rps = pm.next()
                                for k in range(KC):
                                    fw.op("tensor", lambda e, ps=ps, Wt=Wt, k=k, sub=sub: e.matmul(ps[:, :tn], lhsT=Wt[:, k, sub * 128:(sub + 1) * 128], rhs=h2T[:, k, :tn],
                                                                                                  start=(k == 0), stop=(k == KC - 1)),
                                          reads=[rW, rh2], writes=[rps], inc=(k == KC - 1))
                                r_, rr_ = rl.next()
                                fw.op("scalar", lambda e, r_=r_, ps=ps: e.activation(out=r_[:, :tn], in_=ps[:, :tn], func=AF.Relu), reads=[rps], writes=[rr_])
                                fc = ub * 4 + sub
                                fw.op("vector", lambda e, r_=r_, fc=fc: e.tensor_tensor(out=actT[:, fc, :tn], in0=r_[:, :tn], in1=r_[:, :tn], op=ALU.mult),
                                      reads=[rr_], writes=[ract])
                        for cb in range(4):
                            Wt, rW = W.next()
                            fw.dma("gpsimd", Wt[:], wdn_b[fb * 4 + cb], writes=[rW])
                            for tt in range(ntl):
                                ps, rps = pm.next()
                                for k in range(KC):
                                    fw.op("tensor", lambda e, ps=ps, Wt=Wt, k=k, tt=tt: e.matmul(ps[:], lhsT=actT[:, k, tt * 128:(tt + 1) * 128], rhs=Wt[:, k, :],
                                                                                                start=(k == 0), stop=(k == KC - 1)),
                                          reads=[rW, ract], writes=[rps], inc=(k == KC - 1))
                                zs = yz[:, tt, cb * 512:(cb + 1) * 512]
                                if fb == 0:
                                    fw.op("scalar", lambda e, ps=ps, zs=zs: e.activation(out=zs, in_=ps[:], func=AF.Copy), reads=[rps], writes=[ryz[tt]])
                                else:
                                    fw.op("vector", lambda e, ps=ps, zs=zs: e.tensor_tensor(out=zs, in0=ps[:], in1=zs, op=ALU.add), reads=[rps], writes=[ryz[tt]])
                    for tt in range(ntl):
                        gtt = t0 // 128 + tt
                        post_norm_residual(tt, 1, x1_scr[gtt * 128:(gtt + 1) * 128, :], [rx1[tt]], yrows(gtt), [])
                    fw.replay()

            for (t0, tn) in groups:
                ef_group(t0, tn)
    return nc


def _consts(TS):
    c = np.zeros((128, NCONST), np.float32)
    p = np.arange(128, dtype=np.float32)[:, None]
    i = np.arange(128, dtype=np.float32)[None, :]
    c[:, 0:128] = np.eye(128, dtype=np.float32)
    c[:, 128:256] = (i >= p)
    c[:, 256:384] = (p >= i)
    c[:, 384:512] = np.maximum(i - p, 0)
    c[:, 512:640] = np.maximum(p - i, 0)
    c[:, 640:768] = i + 1.0
    c[:, 768:896] = 128.0 - i
    c[:, 896] = 127.0 - p[:, 0]
    c[:, 897] = p[:, 0]
    f = np.arange(128)
    partner = np.where((f % 64) < 32, f + 32, f - 32)
    P = np.zeros((128, 128), np.float32)
    P[partner, f] = 1.0
    c[:, 898:1026] = P
    t = np.arange(TS)
    row = (t // 64).astype(np.float32)
    col = (t % 64).astype(np.float32)
    inv = (10000.0 ** (-np.arange(0, 64, 2, dtype=np.float32) / 64.0)).astype(np.float32)
    ang = np.where((f[:, None] < 64), row[None, :] * inv[f % 32][:, None], col[None, :] * inv[f % 32][:, None]).astype(np.float32)
    sgn = np.where((f % 64) < 32, -1.0, 1.0).astype(np.float32)[:, None]
    cos = np.cos(ang).astype(np.float32)
    sin = (np.sin(ang) * sgn).astype(np.float32)
    return c, cos, sin


_NC_CACHE = {}


def run(inputs, TS, TP, PAST):
    key = (TS, TP, PAST)
    if key not in _NC_CACHE:
        _NC_CACHE[key] = build(TS, TP, PAST)
    nc = _NC_CACHE[key]
    f = lambda a: np.ascontiguousarray(np.asarray(a, dtype=np.float32))
    cst, cos, sin = _consts(TS)
    shared = {
        "w_ada": f(inputs["w_ada"][0]), "b_ada": f(inputs["b_ada"]), "w_in": f(inputs["w_in"][0]),
        "glog": f(np.concatenate([inputs["ret_gamma_logit_fwd"][0], inputs["ret_gamma_logit_bwd"][0]])[None, :]),
        "w_ret_o": f(inputs["w_ret_o"][0]),
        "lam4": f(np.concatenate([inputs["lambda_q1"][0], inputs["lambda_k1"][0], inputs["lambda_q2"][0], inputs["lambda_k2"][0]])[None, :]),
        "g_sub": f(inputs["g_diff_subln"]), "w_diff_o": f(inputs["w_diff_o"][0]), "w_out": f(inputs["w_out"][0]),
        "w_up": f(inputs["w_mlp_up"][0]), "w_down": f(inputs["w_mlp_down"][0]),
        "gpost": f(np.stack([inputs["g_mix_post"][0], inputs["g_mlp_post"][0]])),
        "consts": cst, "rope_cos": cos, "rope_sin": sin,
    }
    in_maps = []
    for c in range(8):
        m = dict(shared)
        m["x_s"] = f(inputs["x_sample"][c])
        m["x_p"] = f(np.asarray(inputs["x_prompt"][2 * c:2 * c + 2]).reshape(2 * TP, D))
        m["cache_k"] = f(inputs["cache_k"][c, 0])
        m["cache_v"] = f(inputs["cache_v"][c, 0])
        m["st_f"] = f(inputs["state_ret_fwd"][c, 0])
        m["st_b"] = f(inputs["state_ret_bwd"][c, 0])
        m["cvec"] = f(np.stack([inputs["c"][c], inputs["c_ctx"], inputs["g_mix_pre"][0], inputs["g_mlp_pre"][0]]))
        in_maps.append(m)
    res = run_bass_kernel_spmd(nc, in_maps, core_ids=list(range(8)))
    R = res.results
    y_prompt = np.concatenate([R[c]["y_p"].reshape(2, TP, D) for c in range(8)], axis=0)
    y_sample = np.stack([R[c]["y_s"] for c in range(8)], axis=0)
    new_k = np.concatenate([R[c]["nk"].reshape(2, 1, TP, H, 2, 128) for c in range(8)], axis=0)
    new_v = np.concatenate([R[c]["nv"].reshape(2, 1, TP, H, 256) for c in range(8)], axis=0)
    new_sf = np.concatenate([R[c]["nsf"].reshape(2, 1, H, 256, 512) for c in range(8)], axis=0)
    new_sb = np.concatenate([R[c]["nsb"].reshape(2, 1, H, 256, 512) for c in range(8)], axis=0)
    return tuple(np.asarray(a, dtype=np.float32) for a in (y_prompt, y_sample, new_k, new_v, new_sf, new_sb))


def kernel(**inputs):
    return run(inputs, 2048, 256, 256)
```

```python
import math
import os
from contextlib import ExitStack

import numpy as np
import concourse.bass as bass
import concourse.mybir as mybir
from concourse.bass_utils import run_bass_kernel_spmd

F32 = mybir.dt.float32
BF16 = mybir.dt.bfloat16
AF = mybir.ActivationFunctionType
ALU = mybir.AluOpType
AX = mybir.AxisListType

D = 2048
KC = 16
H = 8
DFF = 8192
INW = 22528
EPS = 1e-6
SUBLN_EPS = 1e-5
LAM_INIT = 0.8 - 0.6 * math.exp(-0.3 * 0)
NCONST = 1154


class Res:
    __slots__ = ("w", "r", "x")

    def __init__(self, x=False):
        self.w = None
        self.r = []
        self.x = x


class FW:
    def __init__(self, nc, stack):
        self.nc = nc
        self.names = ["tensor", "vector", "scalar", "gpsimd", "sync"]
        self.prog = {n: [] for n in self.names}
        self.cnt = {n: 0 for n in self.names}
        self.waited = {n: {} for n in self.names}
        self.sem = {n: stack.enter_context(nc.semaphore("s_" + n)) for n in self.names}
        self.NDS = 8
        self.dma_sems = {}
        self.dma_cnt = {}
        self.dma_i = {n: 0 for n in self.names}
        for q in ["sync", "gpsimd"]:
            for i in range(self.NDS):
                self.dma_sems[(q, i)] = stack.enter_context(nc.semaphore("d_%s%d" % (q, i)))
                self.dma_cnt[(q, i)] = 0

    def _wait(self, eng, tok):
        if tok is None:
            return
        sem, val = tok
        if sem is self.sem.get(eng):
            if eng == "tensor" or val > self.cnt[eng]:
                return
        key = id(sem)
        if self.waited[eng].get(key, 0) >= val:
            return
        self.waited[eng][key] = val
        self.prog[eng].append(("wait", sem, val))

    def _deps(self, eng, reads, writes):
        for r in reads:
            self._wait(eng, r.w)
        for w in writes:
            self._wait(eng, w.w)
            for t in w.r:
                self._wait(eng, t)

    def _upd(self, tok, reads, writes):
        for r in reads:
            r.r.append(tok)
        for w in writes:
            w.w = tok
            w.r = []

    def op(self, eng, fn, reads=(), writes=(), inc=True):
        xr = [r for r in reads if r.x]
        if xr:
            reads = [r for r in reads if not r.x]
            writes = list(writes) + xr
        self._deps(eng, reads, writes)
        if inc:
            self.cnt[eng] += 1
            tok = (self.sem[eng], self.cnt[eng])
            self.prog[eng].append(("inst", fn, self.sem[eng], 1))
        else:
            tok = (self.sem[eng], self.cnt[eng] + 1)
            self.prog[eng].append(("inst", fn, None, 0))
        self._upd(tok, reads, writes)
        return tok

    def dma(self, q, out, in_, reads=(), writes=()):
        i = self.dma_i[q] % self.NDS
        self.dma_i[q] += 1
        k = (q, i)
        sem = self.dma_sems[k]
        if self.dma_cnt[k] > 0:
            self._wait(q, (sem, self.dma_cnt[k]))
        self._deps(q, reads, writes)
        self.dma_cnt[k] += 16
        tok = (sem, self.dma_cnt[k])
        self.prog[q].append(("inst", lambda e, o=out, i_=in_: e.dma_start(out=o, in_=i_), sem, 16))
        self._upd(tok, reads, writes)
        return tok

    def barrier(self):
        toks = [(self.sem[n], self.cnt[n]) for n in self.names if self.cnt[n] > 0]
        toks += [(self.dma_sems[k], v) for k, v in self.dma_cnt.items() if v > 0]
        for e in self.names:
            for t in toks:
                self._wait(e, t)

    def replay(self):
        self.barrier()
        with self.nc.Block() as block:
            def mk(name):
                items = self.prog[name]

                def body(e):
                    for it in items:
                        if it[0] == "wait":
                            e.wait_ge(it[1], it[2])
                        else:
                            ins = it[1](e)
                            if it[2] is not None:
                                ins.then_inc(it[2], it[3])
                return body
            block.tensor(mk("tensor"))
            block.vector(mk("vector"))
            block.scalar(mk("scalar"))
            block.gpsimd(mk("gpsimd"))
            block.sync(mk("sync"))
        self.prog = {n: [] for n in self.names}


class Rot:
    def __init__(self, tiles, x=False):
        self.tiles = [(t, Res(x)) for t in tiles]
        self.i = 0

    def next(self):
        t = self.tiles[self.i % len(self.tiles)]
        self.i += 1
        return t


def build(TS, TP, PAST):
    NTOK = TS + 2 * TP
    NT = NTOK // 128
    NTS = TS // 128
    PG = 2 * TP
    assert TS % 512 == 0 and PG <= 512 and PG % 128 == 0 and PAST % 128 == 0
    groups = [(g * 512, 512) for g in range(TS // 512)] + [(TS, PG)]
    seqs = [(0, TS, True, 0), (TS, TP, False, 0), (TS + TP, TP, False, 1)]

    nc = bass.Bass("TRN2", target_bir_lowering=False)

    def din(name, shape, dt=F32):
        return nc.dram_tensor(name, shape, dt, kind="ExternalInput").ap()

    def dout(name, shape):
        return nc.dram_tensor(name, shape, F32, kind="ExternalOutput").ap()

    def dscr(name, shape, dt=BF16):
        return nc.dram_tensor(name, shape, dt).ap()

    x_s = din("x_s", [TS, D])
    x_p = din("x_p", [PG, D])
    cache_k = din("cache_k", [PAST, H, 2, 128])
    cache_v = din("cache_v", [PAST, H, 256])
    st_f = din("st_f", [H, 256, 512])
    st_b = din("st_b", [H, 256, 512])
    cvec = din("cvec", [4, D])
    gpost = din("gpost", [2, D])
    w_ada = din("w_ada", [D, 6 * D])
    b_ada = din("b_ada", [1, 6 * D])
    w_in = din("w_in", [D, INW])
    glog = din("glog", [1, 16])
    w_ret_o = din("w_ret_o", [4096, D])
    lam4 = din("lam4", [1, 512])
    g_sub = din("g_sub", [1, 256])
    w_diff_o = din("w_diff_o", [D, D])
    w_out = din("w_out", [D, D])
    w_up = din("w_up", [D, DFF])
    w_down = din("w_down", [DFF, D])
    consts = din("consts", [128, NCONST])
    rope_cos = din("rope_cos", [128, TS])
    rope_sin = din("rope_sin", [128, TS])

    y_s = dout("y_s", [TS, D])
    y_p = dout("y_p", [PG, D])
    nk = dout("nk", [PG, D])
    nv = dout("nv", [PG, D])
    nsf = dout("nsf", [2, H, 256, 512])
    nsb = dout("nsb", [2, H, 256, 512])

    mod_scr = dscr("mod_scr", [2, 6 * D], F32)
    gt_scr = dscr("gt_scr", [2, 2, D], F32)
    rqT = dscr("rqT", [16, 128, NTOK])
    rkT = dscr("rkT", [16, 128, NTOK])
    rk_tok = dscr("rk_tok", [NTOK, 2048])
    rv_tok = dscr("rv_tok", [NTOK, 4096])
    rg_tok = dscr("rg_tok", [NTOK, 4096])
    dqT = dscr("dqT", [16, 128, NTOK])
    dkT = dscr("dkT", [16, 128, NTOK])
    dv_tok = dscr("dv_tok", [NTOK, 2048])
    gatesT = dscr("gatesT", [32, 128, NTOK])
    goT = dscr("goT", [32, 128, NTOK])
    doT = dscr("doT", [16, 128, NTOK])
    x1_scr = dscr("x1_scr", [NTOK, D], F32)
    wro_b = dscr("wro_b", [4, 128, 32, 512])
    wdo_b = dscr("wdo_b", [4, 128, 16, 512])
    wout_b = dscr("wout_b", [4, 128, 16, 512])
    wup_b = dscr("wup_b", [16, 128, 16, 512])
    wdn_b = dscr("wdn_b", [16, 128, 16, 512])

    def xrows(tt):
        t0 = tt * 128
        return x_s[t0:t0 + 128, :] if t0 < TS else x_p[t0 - TS:t0 - TS + 128, :]

    def yrows(tt):
        t0 = tt * 128
        return y_s[t0:t0 + 128, :] if t0 < TS else y_p[t0 - TS:t0 - TS + 128, :]

    with ExitStack() as top:
        fw = FW(nc, top)

        uniq = [0]

        def sbt(st, name, shape, dt=F32):
            uniq[0] += 1
            return st.enter_context(nc.sbuf_tensor("%s_%d" % (name, uniq[0]), shape, dt))

        def pst(st, name, dt=F32):
            uniq[0] += 1
            return st.enter_context(nc.psum_tensor("%s_%d" % (name, uniq[0]), [128, 512 if dt == F32 else 1024], dt))

        C = sbt(top, "C", [128, NCONST])
        identf = C[:, 0:128]
        Umask = C[:, 128:256]
        Lmask = C[:, 256:384]
        relF = C[:, 384:512]
        relB = C[:, 512:640]
        iota1 = C[:, 640:768]
        iotaB = C[:, 768:896]
        colA = C[:, 896:897]
        colB = C[:, 897:898]
        identb = sbt(top, "identb", [128, 128], BF16)
        onesb = sbt(top, "onesb", [128, 128], BF16)
        Pmat = sbt(top, "Pmat", [128, 128], BF16)
        modF = sbt(top, "modF", [128, 2, 96])
        vF = sbt(top, "vF", [128, 64])
        A1 = sbt(top, "A1", [128, 2, 16])
        A2 = sbt(top, "A2", [128, 2, 16])
        maskT = sbt(top, "maskT", [128, H, 128])
        XIF = sbt(top, "XIF", [128, H, 128])
        XIB = sbt(top, "XIB", [128, H, 128])
        LG = sbt(top, "LG", [128, 16])
        ZF = sbt(top, "ZF", [128, H])
        ZB = sbt(top, "ZB", [128, H])
        GF = sbt(top, "GF", [128, H])
        GB = sbt(top, "GB", [128, H])
        neglam = sbt(top, "neglam", [128, 1])
        GS = sbt(top, "GS", [128, 2])
        RP = Res()

        with ExitStack() as ph:
            ps0 = pst(ph, "ps0")
            ps1 = pst(ph, "ps1")
            rps0, rps1 = Res(True), Res(True)
            fw.dma("sync", C[:], consts, writes=[RP])
            fw.op("vector", lambda e: e.tensor_copy(out=identb[:], in_=identf), reads=[RP], writes=[RP])
            fw.op("vector", lambda e: e.memset(onesb[:], 1.0), writes=[RP])
            fw.op("vector", lambda e: e.tensor_copy(out=Pmat[:], in_=C[:, 898:1026]), reads=[RP], writes=[RP])
            v64 = sbt(ph, "v64", [64, 128])
            rv64 = Res()
            fw.dma("sync", v64[:], cvec.rearrange("r (k p) -> (r k) p", p=128), writes=[rv64])
            fw.op("tensor", lambda e: e.transpose(out=ps0[:, 0:64], in_=v64[:], identity=identf[0:64, 0:64]),
                  reads=[rv64, RP], writes=[rps0])
            fw.op("vector", lambda e: e.tensor_copy(out=vF[:], in_=ps0[:, 0:64]), reads=[rps0], writes=[RP])
            silF = sbt(ph, "silF", [128, 32])
            rsil = Res()
            fw.op("scalar", lambda e: e.activation(out=silF[:], in_=vF[:, 0:32], func=AF.Silu), reads=[RP], writes=[rsil])
            mrow = sbt(ph, "mrow", [2, 6 * D])
            rm = Res()
            fw.dma("sync", mrow[:], b_ada.partition_broadcast(2), writes=[rm])
            wa = [sbt(ph, "wa%d" % i, [128, KC, 512]) for i in range(2)]
            rwa = [Res(), Res()]
            for nb in range(24):
                b = nb % 2
                fw.dma("sync", wa[b][:], w_ada[:, nb * 512:(nb + 1) * 512].rearrange("(k p) n -> p k n", p=128), writes=[rwa[b]])
                for k in range(KC):
                    fw.op("tensor", lambda e, k=k, b=b: e.matmul(ps1[0:2, :], lhsT=silF[:, k:32:16], rhs=wa[b][:, k, :],
                                                                 start=(k == 0), stop=(k == KC - 1)),
                          reads=[rsil, rwa[b]], writes=[rps1], inc=(k == KC - 1))
                fw.op("vector", lambda e, nb=nb: e.tensor_tensor(out=mrow[:, nb * 512:(nb + 1) * 512], in0=ps1[0:2, :],
                                                                 in1=mrow[:, nb * 512:(nb + 1) * 512], op=ALU.add),
                      reads=[rps1], writes=[rm])
            rms = Res()
            fw.dma("sync", mod_scr, mrow[:], reads=[rm], writes=[rms])
            gp = sbt(ph, "gp", [2, 2, D])
            rgp = Res()
            fw.dma("sync", gp[:, 0, :], gpost[0:1, :].partition_broadcast(2), writes=[rgp])
            fw.dma("sync", gp[:, 1, :], gpost[1:2, :].partition_broadcast(2), writes=[rgp])
            fw.op("vector", lambda e: e.tensor_tensor(out=gp[:, 0, :], in0=gp[:, 0, :], in1=mrow[:, 2 * D:3 * D], op=ALU.mult),
                  reads=[rm], writes=[rgp])
            fw.op("vector", lambda e: e.tensor_tensor(out=gp[:, 1, :], in0=gp[:, 1, :], in1=mrow[:, 5 * D:6 * D], op=ALU.mult),
                  reads=[rm], writes=[rgp])
            fw.dma("sync", gt_scr, gp[:], reads=[rgp])
            m96 = sbt(ph, "m96", [96, 2, 128])
            rm96 = Res()
            for c in range(2):
                fw.dma("sync", m96[:, c, :], mod_scr[c:c + 1, :].rearrange("o (k p) -> (o k) p", p=128), reads=[rms], writes=[rm96])
            for c in range(2):
                fw.op("tensor", lambda e, c=c: e.transpose(out=ps0[:, c * 96:(c + 1) * 96], in_=m96[:, c, :], identity=identf[0:96, 0:96]),
                      reads=[rm96, RP], writes=[rps0])
            fw.op("vector", lambda e: e.tensor_copy(out=modF[:].rearrange("p c f -> p (c f)"), in_=ps0[:, 0:192]), reads=[rps0], writes=[RP])
            for c in range(2):
                fw.op("vector", lambda e, c=c: e.scalar_tensor_tensor(out=A1[:, c, :], in0=modF[:, c, 16:32], scalar=1.0, in1=vF[:, 32:48],
                                                                      op0=ALU.add, op1=ALU.mult), reads=[RP], writes=[RP])
                fw.op("vector", lambda e, c=c: e.scalar_tensor_tensor(out=A2[:, c, :], in0=modF[:, c, 64:80], scalar=1.0, in1=vF[:, 48:64],
                                                                      op0=ALU.add, op1=ALU.mult), reads=[RP], writes=[RP])
            l4 = sbt(ph, "l4", [128, 512])
            rl4 = Res()
            fw.dma("sync", l4[:], lam4.partition_broadcast(128), writes=[rl4])
            lp = sbt(ph, "lp", [128, 2, 128])
            ls = sbt(ph, "ls", [128, 2])
            fw.op("vector", lambda e: e.tensor_tensor(out=lp[:, 0, :], in0=l4[:, 0:128], in1=l4[:, 128:256], op=ALU.mult), reads=[rl4], writes=[rl4])
            fw.op("vector", lambda e: e.tensor_tensor(out=lp[:, 1, :], in0=l4[:, 256:384], in1=l4[:, 384:512], op=ALU.mult), reads=[rl4], writes=[rl4])
            fw.op("vector", lambda e: e.reduce_sum(out=ls[:], in_=lp[:], axis=AX.X), reads=[rl4], writes=[rl4])
            fw.op("scalar", lambda e: e.activation(out=ls[:], in_=ls[:], func=AF.Exp), reads=[rl4], writes=[rl4])
            fw.op("vector", lambda e: e.tensor_tensor(out=neglam[:], in0=ls[:, 1:2], in1=ls[:, 0:1], op=ALU.subtract), reads=[rl4], writes=[RP])
            fw.op("vector", lambda e: e.tensor_scalar_add(out=neglam[:], in0=neglam[:], scalar1=-LAM_INIT), reads=[RP], writes=[RP])
            fw.dma("sync", LG[:], glog.partition_broadcast(128), writes=[RP])
            fw.op("scalar", lambda e: e.activation(out=LG[:], in_=LG[:], func=AF.Sigmoid), reads=[RP], writes=[RP])
            fw.op("scalar", lambda e: e.activation(out=LG[:], in_=LG[:], func=AF.Ln), reads=[RP], writes=[RP])
            tz = sbt(ph, "tz", [128, 4, H])
            fw.op("vector", lambda e: e.tensor_scalar(out=tz[:, 0, :], in0=LG[:, 0:8], scalar1=colA, scalar2=None, op0=ALU.mult), reads=[RP], writes=[RP])
            fw.op("vector", lambda e: e.tensor_scalar(out=tz[:, 1, :], in0=LG[:, 8:16], scalar1=colB, scalar2=None, op0=ALU.mult), reads=[RP], writes=[RP])
            fw.op("vector", lambda e: e.tensor_scalar(out=tz[:, 2, :], in0=LG[:, 0:8], scalar1=128.0, scalar2=None, op0=ALU.mult), reads=[RP], writes=[RP])
            fw.op("vector", lambda e: e.tensor_scalar(out=tz[:, 3, :], in0=LG[:, 8:16], scalar1=128.0, scalar2=None, op0=ALU.mult), reads=[RP], writes=[RP])
            for i, dst in enumerate([ZF, ZB, GF, GB]):
                fw.op("scalar", lambda e, i=i, dst=dst: e.activation(out=dst[:], in_=tz[:, i, :], func=AF.Exp), reads=[RP], writes=[RP])
            mtmp = sbt(ph, "mtmp", [128, 2, 128])
            for h in range(H):
                fw.op("scalar", lambda e, h=h: e.activation(out=mtmp[:, 0, :], in_=relF, func=AF.Exp, scale=LG[:, h:h + 1]), reads=[RP], writes=[RP])
                fw.op("scalar", lambda e, h=h: e.activation(out=mtmp[:, 1, :], in_=relB, func=AF.Exp, scale=LG[:, 8 + h:9 + h]), reads=[RP], writes=[RP])
                fw.op("vector", lambda e: e.tensor_tensor(out=mtmp[:, 0, :], in0=mtmp[:, 0, :], in1=Umask, op=ALU.mult), reads=[RP], writes=[RP])
                fw.op("vector", lambda e: e.tensor_tensor(out=mtmp[:, 1, :], in0=mtmp[:, 1, :], in1=Lmask, op=ALU.mult), reads=[RP], writes=[RP])
                fw.op("vector", lambda e, h=h: e.tensor_tensor(out=maskT[:, h, :], in0=mtmp[:, 0, :], in1=mtmp[:, 1, :], op=ALU.add), reads=[RP], writes=[RP])
                fw.op("scalar", lambda e, h=h: e.activation(out=XIF[:, h, :], in_=iota1, func=AF.Exp, scale=LG[:, h:h + 1]), reads=[RP], writes=[RP])
                fw.op("scalar", lambda e, h=h: e.activation(out=XIB[:, h, :], in_=iotaB, func=AF.Exp, scale=LG[:, 8 + h:9 + h]), reads=[RP], writes=[RP])
            g2 = sbt(ph, "g2", [2, 128])
            rg2 = Res()
            fw.dma("sync", g2[:], g_sub.rearrange("o (h p) -> (o h) p", p=128), writes=[rg2])
            fw.op("tensor", lambda e: e.transpose(out=ps1[:, 0:2], in_=g2[:], identity=identf[0:2, 0:2]), reads=[rg2, RP, rps1], writes=[rps1])
            fw.op("vector", lambda e: e.tensor_scalar(out=GS[:], in0=ps1[:, 0:2], scalar1=1.0 - LAM_INIT, scalar2=None, op0=ALU.mult),
                  reads=[rps1], writes=[RP])
            fw.replay()
        if os.environ.get("KSTOP") == "0":
            return nc

        def norm_mod_T(tmp, src, rsrc, cond, Amat, shift_off, dst, rdst, psT):
            st = norm_part1(tmp, src, rsrc, psT)
            norm_part2(st, cond, Amat, shift_off, dst, rdst)

        def norm_part1(tmp, src, rsrc, psT):
            ss, rss = tmp["ss"].next()
            junk, rj = tmp["junk"].next()
            xn, rxn = tmp["xn"].next()
            fw.op("scalar", lambda e: e.activation(out=junk[:], in_=src, func=AF.Square, scale=1.0 / math.sqrt(D), accum_out=ss[:, 0:1]),
                  reads=[rsrc], writes=[rj, rss])
            fw.op("scalar", lambda e: e.activation(out=ss[:, 0:1], in_=ss[:, 0:1], func=AF.Sqrt, bias=EPS, scale=1.0), reads=[rss], writes=[rss])
            fw.op("vector", lambda e: e.reciprocal(out=ss[:, 0:1], in_=ss[:, 0:1]), reads=[rss], writes=[rss])
            fw.op("scalar", lambda e: e.activation(out=xn[:], in_=src, func=AF.Copy, scale=ss[:, 0:1]), reads=[rsrc, rss], writes=[rxn])
            pts = []
            for half in range(2):
                pt, rpt = psT.next()
                for j in range(8):
                    k = half * 8 + j
                    fw.op("tensor", lambda e, k=k, j=j, pt=pt: e.transpose(out=pt[:, j * 128:(j + 1) * 128], in_=xn[:, k * 128:(k + 1) * 128], identity=identb[:]),
                          reads=[rxn], writes=[rpt], inc=(j == 7))
                pts.append((pt, rpt))
            return pts

        def norm_part2(pts, cond, Amat, shift_off, dst, rdst):
            for half in range(2):
                pt, rpt = pts[half]
                for j in range(8):
                    k = half * 8 + j
                    if j % 2 == 0:
                        fw.op("vector", lambda e, k=k, j=j, pt=pt: e.tensor_scalar(out=dst(k), in0=pt[:, j * 128:(j + 1) * 128], scalar1=Amat[:, cond, k:k + 1],
                                                                                   scalar2=modF[:, cond, shift_off + k:shift_off + k + 1], op0=ALU.mult, op1=ALU.add),
                              reads=[rpt], writes=[rdst])
                    else:
                        fw.op("scalar", lambda e, k=k, j=j, pt=pt: e.activation(out=dst(k), in_=pt[:, j * 128:(j + 1) * 128], func=AF.Identity,
                                                                                scale=Amat[:, cond, k:k + 1], bias=modF[:, cond, shift_off + k:shift_off + k + 1]),
                              reads=[rpt], writes=[rdst])

        with ExitStack() as phAB:
            hT = sbt(phAB, "hT", [128, KC, NTOK], BF16)
            rhT = Res()
            with ExitStack() as ph:
                xt = Rot([sbt(ph, "xt%d" % i, [128, D]) for i in range(3)])
                tmp = {"ss": Rot([sbt(ph, "ss%d" % i, [128, 1]) for i in range(4)]),
                       "junk": Rot([sbt(ph, "junk", [128, D], BF16)]),
                       "xn": Rot([sbt(ph, "xn%d" % i, [128, D], BF16) for i in range(2)])}
                psT = Rot([pst(ph, "psT%d" % i, BF16) for i in range(4)], x=True)
                def partA(tt):
                    x, rx = xt.next()
                    fw.dma("sync", x[:], xrows(tt), writes=[rx])
                    return norm_part1(tmp, x[:], rx, psT)

                stA = partA(0)
                for tt in range(NT):
                    stN = partA(tt + 1) if tt + 1 < NT else None
                    cond = 0 if tt < NTS else 1
                    norm_part2(stA, cond, A1, 0, lambda k, tt=tt: hT[:, k, tt * 128:(tt + 1) * 128], rhT)
                    stA = stN
                fw.replay()
            if os.environ.get("KSTOP") == "A":
                return nc

            with ExitStack() as ph:
                W = Rot([sbt(ph, "W%d" % i, [128, KC, 512], BF16) for i in range(2)])
                cosT = sbt(ph, "cosT", [128, TS])
                sinT = sbt(ph, "sinT", [128, TS])
                rrope = Res()
                fw.dma("sync", cosT[:], rope_cos, writes=[rrope])
                fw.dma("sync", sinT[:], rope_sin, writes=[rrope])
                stg = Rot([sbt(ph, "stg%d" % i, [128, 512], BF16) for i in range(4)])
                stf = Rot([sbt(ph, "stf%d" % i, [128, 512]) for i in range(2)])
                qraw = Rot([sbt(ph, "qraw%d" % i, [128, 512], BF16) for i in range(2)])
                t1s = Rot([sbt(ph, "t1_%d" % i, [128, 512]) for i in range(2)])
                t2s = Rot([sbt(ph, "t2_%d" % i, [128, 512]) for i in range(2)])
                pmain = Rot([pst(ph, "pm%d" % i) for i in range(5)], x=True)
                prot = Rot([pst(ph, "pr%d" % i) for i in range(2)], x=True)
                flip = [0]

                def evac_copy(dst, ps, rps, rdst, scale=None):
                    flip[0] ^= 1
                    if flip[0]:
                        if scale is None:
                            fw.op("vector", lambda e: e.tensor_copy(out=dst, in_=ps), reads=[rps], writes=[rdst])
                        else:
                            fw.op("vector", lambda e: e.tensor_scalar(out=dst, in0=ps, scalar1=scale, scalar2=None, op0=ALU.mult), reads=[rps], writes=[rdst])
                    else:
                        fw.op("scalar", lambda e: e.activation(out=dst, in_=ps, func=AF.Copy, scale=(1.0 if scale is None else scale)),
                              reads=[rps], writes=[rdst])

                wbl = range(INW // 512)
                if os.environ.get("KWB"):
                    wbl = [int(v) for v in os.environ["KWB"].split(",")]
                for wb in wbl:
                    col0 = wb * 512
                    Wt, rW = W.next()
                    fw.dma("gpsimd", Wt[:], w_in[:, col0:col0 + 512].rearrange("(k p) n -> p k n", p=128), writes=[rW])
                    if wb < 4:
                        kind, fdst, fbase = "rq", rqT, 0
                    elif wb < 8:
                        kind, fdst, fbase = "rk", rkT, 2048
                    elif wb < 16:
                        kind = "rv"
                    elif wb < 24:
                        kind = "rg"
                    elif wb < 28:
                        kind, fdst, fbase = "dq", dqT, 12288
                    elif wb < 32:
                        kind, fdst, fbase = "dk", dkT, 14336
                    elif wb < 36:
                        kind = "dv"
                    else:
                        kind, fdst, fbase = "gate", gatesT, 18432
                    pend_tail = [None]
                    if kind in ("rq", "rk", "dq", "dk", "gate"):
                        for sub in range(4):
                            fci = (col0 + sub * 128 - fbase) // 128
                            for gi, (t0, tn) in enumerate(groups):
                                ps, rps = pmain.next()
                                for k in range(KC):
                                    fw.op("tensor", lambda e, k=k, ps=ps, Wt=Wt, sub=sub, t0=t0, tn=tn: e.matmul(
                                        ps[:, :tn], lhsT=Wt[:, k, sub * 128:(sub + 1) * 128], rhs=hT[:, k, t0:t0 + tn], start=(k == 0), stop=(k == KC - 1)),
                                        reads=[rW], writes=[rps], inc=(k == KC - 1))
                                if kind in ("dq", "dk") and t0 < TS:
                                    qr, rqr = qraw.next()
                                    fw.op("scalar", lambda e, qr=qr, ps=ps, tn=tn: e.activation(out=qr[:, :tn], in_=ps[:, :tn], func=AF.Copy), reads=[rps], writes=[rqr])

                                    def rope_tail(ps=ps, rps=rps, qr=qr, rqr=rqr, t0=t0, tn=tn, fci=fci, fdst=fdst):
                                        so, rso = stg.next()
                                        pr, rpr = prot.next()
                                        fw.op("tensor", lambda e: e.matmul(pr[:, :tn], lhsT=Pmat[:], rhs=qr[:, :tn], start=True, stop=True), reads=[rqr], writes=[rpr])
                                        t1, rt1 = t1s.next()
                                        t2, rt2 = t2s.next()
                                        fw.op("vector", lambda e: e.tensor_tensor(out=t1[:, :tn], in0=ps[:, :tn], in1=cosT[:, t0:t0 + tn], op=ALU.mult),
                                              reads=[rps, rrope], writes=[rt1])
                                        fw.op("vector", lambda e: e.tensor_tensor(out=t2[:, :tn], in0=pr[:, :tn], in1=sinT[:, t0:t0 + tn], op=ALU.mult),
                                              reads=[rpr, rrope], writes=[rt2])
                                        fw.op("vector", lambda e: e.tensor_tensor(out=so[:, :tn], in0=t1[:, :tn], in1=t2[:, :tn], op=ALU.add),
                                              reads=[rt1, rt2], writes=[rso])
                                        fw.dma("sync", fdst[fci, :, t0:t0 + tn], so[:, :tn], reads=[rso])

                                    if pend_tail[0] is not None:
                                        pend_tail[0]()
                                    pend_tail[0] = rope_tail
                                    continue
                                so, rso = stg.next()
                                if kind == "gate":
                                    fw.op("scalar", lambda e, so=so, ps=ps, tn=tn: e.activation(out=so[:, :tn], in_=ps[:, :tn], func=AF.Sigmoid), reads=[rps], writes=[rso])
                                else:
                                    evac_copy(so[:, :tn], ps[:, :tn], rps, rso, scale=(0.0625 if kind == "rk" else None))
                                fw.dma("sync", fdst[fci, :, t0:t0 + tn], so[:, :tn], reads=[rso])
                    if pend_tail[0] is not None:
                        pend_tail[0]()
                        pend_tail[0] = None
                    if kind in ("rk", "rv", "rg", "dv", "dk"):
                        tiles = range(NTS, NT) if kind == "dk" else range(NT)
                        for tt in tiles:
                            ps, rps = pmain.next()
                            for k in range(KC):
                                fw.op("tensor", lambda e, k=k, ps=ps, Wt=Wt, tt=tt: e.matmul(
                                    ps[:, :], lhsT=hT[:, k, tt * 128:(tt + 1) * 128], rhs=Wt[:, k, :], start=(k == 0), stop=(k == KC - 1)),
                                    reads=[rW], writes=[rps], inc=(k == KC - 1))
                            r0 = tt * 128
                            if kind != "dk":
                                so, rso = stg.next()
                                evac_copy(so[:], ps[:], rps, rso, scale=(0.0625 if kind == "rk" else None))
                                if kind == "rk":
                                    dd = rk_tok[r0:r0 + 128, col0 - 2048:col0 - 2048 + 512]
                                elif kind == "rv":
                                    dd = rv_tok[r0:r0 + 128, col0 - 4096:col0 - 4096 + 512]
                                elif kind == "rg":
                                    dd = rg_tok[r0:r0 + 128, col0 - 8192:col0 - 8192 + 512]
                                else:
                                    dd = dv_tok[r0:r0 + 128, col0 - 16384:col0 - 16384 + 512]
                                fw.dma("sync", dd, so[:], reads=[rso])
                            if kind in ("dk", "dv") and tt >= NTS:
                                sf, rsf = stf.next()
                                fw.op("scalar", lambda e, sf=sf, ps=ps: e.activation(out=sf[:], in_=ps[:], func=AF.Copy), reads=[rps], writes=[rsf])
                                pr0 = r0 - TS
                                if kind == "dk":
                                    dd = nk[pr0:pr0 + 128, col0 - 14336:col0 - 14336 + 512]
                                else:
                                    dd = nv[pr0:pr0 + 128, col0 - 16384:col0 - 16384 + 512]
                                fw.dma("sync", dd, sf[:], reads=[rsf])
                fw.replay()

        if os.environ.get("KSTOP") == "B":
            return nc
        with ExitStack() as ph:
            TM = TS
            nmax = TM // 128
            qTs = Rot([sbt(ph, "qT%d" % i, [128, 2, TM], BF16) for i in range(2)])
            kTs = Rot([sbt(ph, "kT%d" % i, [128, 2, TM], BF16) for i in range(2)])
            kts = Rot([sbt(ph, "kt%d" % i, [128, nmax, 256], BF16) for i in range(1)])
            vs = Rot([sbt(ph, "v%d" % i, [128, nmax, 512], BF16) for i in range(1)])
            gs = Rot([sbt(ph, "g%d" % i, [128, nmax, 512], BF16) for i in range(1)])
            qf = sbt(ph, "qf", [128, 2, TM], BF16)
            qb = sbt(ph, "qb", [128, 2, TM], BF16)
            rqf, rqb = Res(), Res()
            SbA = sbt(ph, "SbA", [128, nmax, 1024], BF16)
            rSbA = [Res() for _ in range(nmax)]
            Sb = sbt(ph, "Sb", [128, 2, 512])
            Sf = sbt(ph, "Sf", [128, 2, 512])
            rSb, rSf = Res(), Res()
            SfA = sbt(ph, "SfA", [128, nmax, 1024], BF16)
            rSfA = [Res() for _ in range(nmax)]
            goTh = sbt(ph, "goTh", [128, 4, TM], BF16)
            rgoTh = Res()
            aTs = Rot([sbt(ph, "aT%d" % i, [128, 128], BF16) for i in range(3)])
            kzs = Rot([sbt(ph, "kz%d" % i, [128, 256], BF16) for i in range(4)])
            gos = Rot([sbt(ph, "go%d" % i, [128, 512], BF16) for i in range(3)])
            junks = Rot([sbt(ph, "jk%d" % i, [128, 512], BF16) for i in range(1)])
            sss = Rot([sbt(ph, "rss%d" % i, [128, 1]) for i in range(4)])
            p_s = Rot([pst(ph, "p_s%d" % i) for i in range(2)], x=True)
            p_o = Rot([pst(ph, "p_o%d" % i) for i in range(2)], x=True)
            p_kv = Rot([pst(ph, "p_kv%d" % i) for i in range(3)], x=True)
            p_T = Rot([pst(ph, "p_T", BF16)], x=True)

            def ret_head(t0, T, is_s, pidx, h, shared=None):
                    n = T // 128
                    if shared is None:
                        qT, rqT_ = qTs.next()
                        kT, rkT_ = kTs.next()
                        kt, rkt = kts.next()
                        v, rv = vs.next()
                        g, rg = gs.next()
                        fw.dma("sync", qT[:, :, :T], rqT[2 * h:2 * h + 2, :, t0:t0 + T].rearrange("c p t -> p c t"), writes=[rqT_])
                        fw.dma("sync", kT[:, :, :T], rkT[2 * h:2 * h + 2, :, t0:t0 + T].rearrange("c p t -> p c t"), writes=[rkT_])
                        fw.dma("sync", kt[:, :n, :], rk_tok[t0:t0 + T, h * 256:(h + 1) * 256].rearrange("(c p) d -> p c d", p=128), writes=[rkt])
                        fw.dma("sync", v[:, :n, :], rv_tok[t0:t0 + T, h * 512:(h + 1) * 512].rearrange("(c p) d -> p c d", p=128), writes=[rv])
                        fw.dma("sync", g[:, :n, :], rg_tok[t0:t0 + T, h * 512:(h + 1) * 512].rearrange("(c p) d -> p c d", p=128), writes=[rg])
                        fw.op("scalar", lambda e, g=g, n=n: e.activation(out=g[:, :n, :], in_=g[:, :n, :], func=AF.Silu), reads=[rg], writes=[rg])
                        vc = lambda c: v[:, c, :]
                        gc = lambda c: g[:, c, :]
                        ktc = lambda c: kt[:, c, :]
                        qTd = lambda dc: qT[:, dc, :T]
                        kTd = lambda dc: kT[:, dc, :T]
                    else:
                        (qT, rqT_), (kT, rkT_), (kt, rkt), (v, rv), (g, rg) = shared
                        qTflat = qT[:].rearrange("p a t -> p (a t)")
                        kTflat = kT[:].rearrange("p a t -> p (a t)")
                        vc = lambda c: v[:, c * H + h, :]
                        gc = lambda c: g[:, c * H + h, :]
                        ktc = lambda c: kt[:, c * H + h, :]
                        qTd = lambda dc: qTflat[:, (2 * h + dc) * T:(2 * h + dc + 1) * T]
                        kTd = lambda dc: kTflat[:, (2 * h + dc) * T:(2 * h + dc + 1) * T]
                    if is_s:
                        fw.dma("sync", Sb[:], st_b[h].rearrange("(c p) e -> p c e", p=128), writes=[rSb])
                        fw.dma("sync", Sf[:], st_f[h].rearrange("(c p) e -> p c e", p=128), writes=[rSf])
                    else:
                        fw.op("gpsimd", lambda e: e.memset(Sb[:], 0.0), writes=[rSb])
                        fw.op("gpsimd", lambda e: e.memset(Sf[:], 0.0), writes=[rSf])
                    for dc in range(2):
                        fw.op("vector", lambda e, dc=dc, n=n, T=T, h=h: e.tensor_tensor(
                            out=qf[:, dc, :T].rearrange("p (c i) -> p c i", i=128), in0=qTd(dc).rearrange("p (c i) -> p c i", i=128),
                            in1=XIF[:, h:h + 1, :].broadcast_to([128, n, 128]), op=ALU.mult), reads=[rqT_], writes=[rqf])
                        fw.op("vector", lambda e, dc=dc, n=n, T=T, h=h: e.tensor_tensor(
                            out=qb[:, dc, :T].rearrange("p (c i) -> p c i", i=128), in0=qTd(dc).rearrange("p (c i) -> p c i", i=128),
                            in1=XIB[:, h:h + 1, :].broadcast_to([128, n, 128]), op=ALU.mult), reads=[rqT_], writes=[rqb])

                    steps = []
                    for i in range(n):
                        cb = n - 1 - i
                        if not (cb == 0 and is_s):
                            steps.append(("b", cb))
                        if i < n - 1 or not is_s:
                            steps.append(("f", i))

                    def emit_kz(st):
                        d, c = st
                        Zt = ZB if d == "b" else ZF
                        kz, rkz = kzs.next()
                        fw.op("scalar", lambda e, kz=kz, c=c, Zt=Zt: e.activation(out=kz[:], in_=ktc(c), func=AF.Copy, scale=Zt[:, h:h + 1]),
                              reads=[rkt], writes=[rkz])
                        return kz, rkz

                    fw.op("scalar", lambda e, n=n: e.activation(out=SbA[:, n - 1, :], in_=Sb[:].rearrange("p c e -> p (c e)"), func=AF.Copy),
                          reads=[rSb], writes=[rSbA[n - 1]])
                    fw.op("scalar", lambda e: e.activation(out=SfA[:, 0, :], in_=Sf[:].rearrange("p c e -> p (c e)"), func=AF.Copy),
                          reads=[rSf], writes=[rSfA[0]])
                    nxt = emit_kz(steps[0]) if steps else None
                    for si, (d, c) in enumerate(steps):
                        kz, rkz = nxt
                        nxt = emit_kz(steps[si + 1]) if si + 1 < len(steps) else None
                        S, rS, Gt = (Sb, rSb, GB) if d == "b" else (Sf, rSf, GF)
                        pks = []
                        for dc in range(2):
                            pk, rpk = p_kv.next()
                            fw.op("tensor", lambda e, pk=pk, kz=kz, dc=dc, c=c: e.matmul(pk[:], lhsT=kz[:, dc * 128:(dc + 1) * 128], rhs=vc(c), start=True, stop=True),
                                  reads=[rkz, rv], writes=[rpk])
                            pks.append((pk, rpk))
                        for dc in range(2):
                            pk, rpk = pks[dc]
                            fw.op("vector", lambda e, pk=pk, dc=dc, S=S, Gt=Gt: e.scalar_tensor_tensor(out=S[:, dc, :], in0=S[:, dc, :], scalar=Gt[:, h:h + 1], in1=pk[:],
                                                                                                       op0=ALU.mult, op1=ALU.add), reads=[rpk], writes=[rS])
                        if d == "b" and c > 0:
                            fw.op("scalar", lambda e, c=c: e.activation(out=SbA[:, c - 1, :], in_=Sb[:].rearrange("p c e -> p (c e)"), func=AF.Copy),
                                  reads=[rSb], writes=[rSbA[c - 1]])
                        if d == "f" and c < n - 1:
                            fw.op("scalar", lambda e, c=c: e.activation(out=SfA[:, c + 1, :], in_=Sf[:].rearrange("p c e -> p (c e)"), func=AF.Copy),
                                  reads=[rSf], writes=[rSfA[c + 1]])
                    if not is_s:
                        fw.dma("sync", nsb[pidx, h].rearrange("(c p) e -> p c e", p=128), Sb[:], reads=[rSb])
                        fw.dma("sync", nsf[pidx, h].rearrange("(c p) e -> p c e", p=128), Sf[:], reads=[rSf])

                    def stageA(c):
                        cs = slice(c * 128, (c + 1) * 128)
                        pss, rpss = p_s.next()
                        for dc in range(2):
                            fw.op("tensor", lambda e, pss=pss, dc=dc, cs=cs: e.matmul(pss[:, 0:128], lhsT=kTd(dc)[:, cs], rhs=qTd(dc)[:, cs], start=(dc == 0), stop=(dc == 1)),
                                  reads=[rkT_, rqT_], writes=[rpss], inc=(dc == 1))
                        aT, raT = aTs.next()
                        fw.op("vector", lambda e, aT=aT, pss=pss: e.tensor_tensor(out=aT[:], in0=pss[:, 0:128], in1=maskT[:, h, :], op=ALU.mult), reads=[rpss], writes=[raT])
                        return aT, raT

                    def stageB(c, aT, raT):
                        cs = slice(c * 128, (c + 1) * 128)
                        po, rpo = p_o.next()
                        fw.op("tensor", lambda e, po=po, aT=aT, c=c: e.matmul(po[:], lhsT=aT[:], rhs=vc(c), start=True, stop=False), reads=[raT, rv], writes=[rpo], inc=False)
                        for dc in range(2):
                            fw.op("tensor", lambda e, po=po, dc=dc, cs=cs, c=c: e.matmul(po[:], lhsT=qf[:, dc, cs], rhs=SfA[:, c, dc * 512:(dc + 1) * 512], start=False, stop=False),
                                  reads=[rqf, rSfA[c]], writes=[rpo], inc=False)
                        for dc in range(2):
                            fw.op("tensor", lambda e, po=po, dc=dc, cs=cs, c=c: e.matmul(po[:], lhsT=qb[:, dc, cs], rhs=SbA[:, c, dc * 512:(dc + 1) * 512], start=False, stop=(dc == 1)),
                                  reads=[rqb, rSbA[c]], writes=[rpo], inc=(dc == 1))
                        ss, rss = sss.next()
                        jk, rjk = junks.next()
                        fw.op("scalar", lambda e, jk=jk, po=po, ss=ss: e.activation(out=jk[:], in_=po[:], func=AF.Square, scale=1.0 / math.sqrt(512.0), accum_out=ss[:]),
                              reads=[rpo], writes=[rjk, rss])
                        fw.op("scalar", lambda e, ss=ss: e.activation(out=ss[:], in_=ss[:], func=AF.Sqrt, bias=EPS, scale=1.0), reads=[rss], writes=[rss])
                        fw.op("vector", lambda e, ss=ss: e.reciprocal(out=ss[:], in_=ss[:]), reads=[rss], writes=[rss])
                        go, rgo = gos.next()
                        fw.op("vector", lambda e, go=go, po=po, ss=ss, c=c: e.scalar_tensor_tensor(out=go[:], in0=po[:], scalar=ss[:], in1=gc(c), op0=ALU.mult, op1=ALU.mult),
                              reads=[rpo, rss, rg], writes=[rgo])
                        return go, rgo, cs

                    def stageC(go, rgo, cs):
                        pT, rpT = p_T.next()
                        for j in range(4):
                            fw.op("tensor", lambda e, pT=pT, go=go, j=j: e.transpose(out=pT[:, j * 128:(j + 1) * 128], in_=go[:, j * 128:(j + 1) * 128], identity=identb[:]),
                                  reads=[rgo], writes=[rpT], inc=(j == 3))
                        fw.op("vector", lambda e, pT=pT, cs=cs: e.tensor_copy(out=goTh[:, :, cs], in_=pT[:, 0:512].rearrange("p (j i) -> p j i", i=128)),
                              reads=[rpT], writes=[rgoTh])

                    a_cur = stageA(0)
                    prevB = None
                    for c in range(n):
                        a_nxt = stageA(c + 1) if c + 1 < n else None
                        b_cur = stageB(c, *a_cur)
                        if prevB is not None:
                            stageC(*prevB)
                        prevB = b_cur
                        a_cur = a_nxt
                    stageC(*prevB)
                    fw.dma("sync", goT[4 * h:4 * h + 4, :, t0:t0 + T].rearrange("c p t -> p c t"), goTh[:, :, :T], reads=[rgoTh])

            for b in range(4):
                fw.dma("gpsimd", wro_b[b], w_ret_o[:, b * 512:(b + 1) * 512].rearrange("(k p) n -> p k n", p=128))
                fw.dma("gpsimd", wdo_b[b], w_diff_o[:, b * 512:(b + 1) * 512].rearrange("(k p) n -> p k n", p=128))
                fw.dma("gpsimd", wout_b[b], w_out[:, b * 512:(b + 1) * 512].rearrange("(k p) n -> p k n", p=128))
            for b in range(16):
                fw.dma("gpsimd", wup_b[b], w_up[:, b * 512:(b + 1) * 512].rearrange("(k p) n -> p k n", p=128))
            for fb in range(4):
                for cb in range(4):
                    fw.dma("gpsimd", wdn_b[fb * 4 + cb], w_down[fb * 2048:(fb + 1) * 2048, cb * 512:(cb + 1) * 512].rearrange("(k p) n -> p k n", p=128))
            for (t0, T, is_s, pidx) in seqs:
                if is_s or T * H > TM:
                    for h in range(H):
                        ret_head(t0, T, is_s, pidx, h)
                else:
                    n = T // 128
                    sh = [qTs.next(), kTs.next(), kts.next(), vs.next(), gs.next()]
                    (qT, rqT_), (kT, rkT_), (kt, rkt), (v, rv), (g, rg) = sh
                    fw.dma("sync", qT[:].rearrange("p a t -> p (a t)")[:, 0:16 * T].rearrange("p (c t) -> p c t", t=T),
                           rqT[:, :, t0:t0 + T].rearrange("c p t -> p c t"), writes=[rqT_])
                    fw.dma("sync", kT[:].rearrange("p a t -> p (a t)")[:, 0:16 * T].rearrange("p (c t) -> p c t", t=T),
                           rkT[:, :, t0:t0 + T].rearrange("c p t -> p c t"), writes=[rkT_])
                    fw.dma("sync", kt[:, 0:n * H, :].rearrange("p (c h) d -> p c (h d)", h=H), rk_tok[t0:t0 + T, :].rearrange("(c p) d -> p c d", p=128), writes=[rkt])
                    fw.dma("sync", v[:, 0:n * H, :].rearrange("p (c h) d -> p c (h d)", h=H), rv_tok[t0:t0 + T, :].rearrange("(c p) d -> p c d", p=128), writes=[rv])
                    fw.dma("sync", g[:, 0:n * H, :].rearrange("p (c h) d -> p c (h d)", h=H), rg_tok[t0:t0 + T, :].rearrange("(c p) d -> p c d", p=128), writes=[rg])
                    fw.op("scalar", lambda e, g=g, n=n: e.activation(out=g[:, 0:n * H, :], in_=g[:, 0:n * H, :], func=AF.Silu), reads=[rg], writes=[rg])
                    for h in range(H):
                        ret_head(t0, T, is_s, pidx, h, shared=sh)
            fw.replay()

        if os.environ.get("KSTOP") == "C":
            return nc
        with ExitStack() as ph:
            TKM = TS + PAST
            NP = PAST // 128
            q2s = Rot([sbt(ph, "q2_%d" % i, [128, 2, TS], BF16) for i in range(2)])
            k2s = Rot([sbt(ph, "k2_%d" % i, [128, 2, TKM], BF16) for i in range(2)])
            v2s = Rot([sbt(ph, "v2_%d" % i, [128, TKM // 128, 256], BF16) for i in range(2)])
            ckf = sbt(ph, "ckf", [128, NP, 256])
            ckb = sbt(ph, "ckb", [128, NP, 256], BF16)
            rckf, rckb = Res(), Res()
            eTs = Rot([sbt(ph, "eT%d" % i, [128, 512], BF16) for i in range(3)])
            o1s = Rot([sbt(ph, "o1_%d" % i, [128, 2, 512]) for i in range(2)])
            r1 = sbt(ph, "r1", [128, 512])
            rr1 = Res()
            tmpd = Rot([sbt(ph, "tmpd%d" % i, [128, 512]) for i in range(2)])
            sq = sbt(ph, "sq", [128, 2, 512], BF16)
            rsq = Res()
            sd = sbt(ph, "sd", [128, 512])
            rsd = Res()
            dst_ = Rot([sbt(ph, "dst%d" % i, [128, 2, 512], BF16) for i in range(2)])
            p_sT = Rot([pst(ph, "p_sT%d" % i) for i in range(2)], x=True)
            accs = [[(pst(ph, "acc%d_%d" % (m, j)), Res(True)) for j in range(3)] for m in range(2)]
            p_ck = p_sT
            scale = 1.0 / math.sqrt(128.0)

            def attn_head(t0, T, is_s, pidx, h):
                    Tk = T + (PAST if is_s else 0)
                    nk_ = Tk // 128
                    q2, rq2 = q2s.next()
                    k2, rk2 = k2s.next()
                    v2, rv2 = v2s.next()
                    fw.dma("sync", q2[:, :, :T], dqT[2 * h:2 * h + 2, :, t0:t0 + T].rearrange("c p t -> p c t"), writes=[rq2])
                    fw.dma("sync", k2[:, :, :T], dkT[2 * h:2 * h + 2, :, t0:t0 + T].rearrange("c p t -> p c t"), writes=[rk2])
                    fw.dma("sync", v2[:, :T // 128, :], dv_tok[t0:t0 + T, h * 256:(h + 1) * 256].rearrange("(c p) d -> p c d", p=128), writes=[rv2])
                    if is_s:
                        fw.dma("gpsimd", v2[:, T // 128:nk_, :], cache_v[:, h, :].rearrange("(c p) d -> p c d", p=128), writes=[rv2])
                        fw.dma("sync", ckf[:], cache_k[:, h, :, :].rearrange("(c p) m d -> p c (m d)", p=128), writes=[rckf])
                        fw.op("vector", lambda e: e.tensor_copy(out=ckb[:], in_=ckf[:]), reads=[rckf], writes=[rckb])
                        for c in range(NP):
                            pc, rpc = p_ck.next()
                            pcb = pc[:].bitcast(BF16)
                            for m in range(2):
                                fw.op("tensor", lambda e, pcb=pcb, c=c, m=m: e.transpose(out=pcb[:, m * 128:(m + 1) * 128], in_=ckb[:, c, m * 128:(m + 1) * 128], identity=identb[:]),
                                      reads=[rckb], writes=[rpc], inc=(m == 1))
                            fw.op("vector", lambda e, pcb=pcb, c=c, T=T, k2=k2: e.tensor_copy(out=k2[:, :, T + c * 128:T + (c + 1) * 128],
                                                                                               in_=pcb[:, 0:256].rearrange("p (m i) -> p m i", i=128)),
                                  reads=[rpc], writes=[rk2])
                    pend_epi = [None]

                    def epilogue(o1, ro1, q0, qn):
                        fw.op("scalar", lambda e: e.activation(out=sq[:, :, :qn], in_=o1[:, :, :qn], func=AF.Square, scale=1.0 / 16.0), reads=[ro1], writes=[rsq])
                        ps, rps = p_sT.next()
                        for half in range(2):
                            fw.op("tensor", lambda e, half=half: e.matmul(ps[:, :qn], lhsT=onesb[:], rhs=sq[:, half, :qn], start=(half == 0), stop=(half == 1)),
                                  reads=[rsq], writes=[rps], inc=(half == 1))
                        fw.op("scalar", lambda e: e.activation(out=sd[:, :qn], in_=ps[:, :qn], func=AF.Sqrt, bias=SUBLN_EPS, scale=1.0), reads=[rps], writes=[rsd])
                        fw.op("vector", lambda e: e.reciprocal(out=sd[:, :qn], in_=sd[:, :qn]), reads=[rsd], writes=[rsd])
                        ds, rds = dst_.next()
                        for half in range(2):
                            fw.op("vector", lambda e, half=half: e.scalar_tensor_tensor(out=ds[:, half, :qn], in0=o1[:, half, :qn], scalar=GS[:, half:half + 1],
                                                                                      in1=sd[:, :qn], op0=ALU.mult, op1=ALU.mult),
                                  reads=[ro1, rsd], writes=[rds])
                        fw.dma("sync", doT[2 * h:2 * h + 2, :, t0 + q0:t0 + q0 + qn].rearrange("c p t -> p c t"), ds[:, :, :qn], reads=[rds])

                    for q0 in range(0, T, 512):
                        qn = min(512, T - q0)
                        o1, ro1 = o1s.next()
                        for m in range(2):
                            (alo, ralo), (ahi, rahi), (asum, rasum) = accs[m]
                            pend = None
                            for kk in range(nk_ + 1):
                                cur = None
                                if kk < nk_:
                                    ps, rps = p_sT.next()
                                    fw.op("tensor", lambda e, ps=ps, m=m, kt=kk, q0=q0, qn=qn, k2=k2, q2=q2: e.matmul(
                                        ps[:, :qn], lhsT=k2[:, m, kt * 128:(kt + 1) * 128], rhs=q2[:, m, q0:q0 + qn], start=True, stop=True),
                                        reads=[rk2, rq2], writes=[rps])
                                    cur = (ps, rps, kk)
                                if pend is None:
                                    pend = cur
                                    continue
                                ps, rps, kt = pend
                                pend = cur
                                eT, reT = eTs.next()
                                fw.op("scalar", lambda e, eT=eT, ps=ps, qn=qn: e.activation(out=eT[:, :qn], in_=ps[:, :qn], func=AF.Exp, scale=scale), reads=[rps], writes=[reT])
                                st_, sp_ = (kt == 0), (kt == nk_ - 1)
                                fw.op("tensor", lambda e, alo=alo, v2=v2, kt=kt, eT=eT, qn=qn, st_=st_, sp_=sp_: e.matmul(
                                    alo[:, :qn], lhsT=v2[:, kt, 0:128], rhs=eT[:, :qn], start=st_, stop=sp_), reads=[rv2, reT], writes=[ralo], inc=False)
                                fw.op("tensor", lambda e, ahi=ahi, v2=v2, kt=kt, eT=eT, qn=qn, st_=st_, sp_=sp_: e.matmul(
                                    ahi[:, :qn], lhsT=v2[:, kt, 128:256], rhs=eT[:, :qn], start=st_, stop=sp_), reads=[rv2, reT], writes=[rahi], inc=False)
                                fw.op("tensor", lambda e, asum=asum, eT=eT, qn=qn, st_=st_, sp_=sp_: e.matmul(
                                    asum[:, :qn], lhsT=onesb[:], rhs=eT[:, :qn], start=st_, stop=sp_), reads=[reT], writes=[rasum], inc=True)
                            if m == 0:
                                fw.op("vector", lambda e, asum=asum, qn=qn: e.reciprocal(out=r1[:, :qn], in_=asum[:, :qn]), reads=[rasum], writes=[rr1])
                                fw.op("vector", lambda e, alo=alo, qn=qn, o1=o1: e.tensor_tensor(out=o1[:, 0, :qn], in0=alo[:, :qn], in1=r1[:, :qn], op=ALU.mult),
                                      reads=[ralo, rr1], writes=[ro1])
                                fw.op("vector", lambda e, ahi=ahi, qn=qn, o1=o1: e.tensor_tensor(out=o1[:, 1, :qn], in0=ahi[:, :qn], in1=r1[:, :qn], op=ALU.mult),
                                      reads=[rahi, rr1], writes=[ro1])
                                if pend_epi[0] is not None:
                                    epilogue(*pend_epi[0])
                                    pend_epi[0] = None
                            else:
                                fw.op("vector", lambda e, asum=asum, qn=qn: e.reciprocal(out=r1[:, :qn], in_=asum[:, :qn]), reads=[rasum], writes=[rr1])
                                fw.op("vector", lambda e, qn=qn: e.tensor_scalar(out=r1[:, :qn], in0=r1[:, :qn], scalar1=neglam[:, 0:1], scalar2=None, op0=ALU.mult),
                                      reads=[rr1], writes=[rr1])
                                for half, (aa, raa) in enumerate([(alo, ralo), (ahi, rahi)]):
                                    td, rtd = tmpd.next()
                                    fw.op("vector", lambda e, td=td, aa=aa, qn=qn: e.tensor_tensor(out=td[:, :qn], in0=aa[:, :qn], in1=r1[:, :qn], op=ALU.mult),
                                          reads=[raa, rr1], writes=[rtd])
                                    fw.op("vector", lambda e, td=td, half=half, qn=qn, o1=o1: e.tensor_tensor(out=o1[:, half, :qn], in0=o1[:, half, :qn], in1=td[:, :qn], op=ALU.add),
                                          reads=[rtd], writes=[ro1])
                        pend_epi[0] = (o1, ro1, q0, qn)
                    epilogue(*pend_epi[0])

            for (t0, T, is_s, pidx) in seqs:
                for h in range(H):
                    attn_head(t0, T, is_s, pidx, h)
            fw.replay()

        if os.environ.get("KSTOP") == "D":
            return nc
        with ExitStack() as phEF:
            actT = sbt(phEF, "actT", [128, KC, 512], BF16)
            ract = Res()
            def ef_group(t0, tn):
                cond = 0 if t0 < TS else 1
                ntl = tn // 128
                with ExitStack() as ph:
                    gd = sbt(ph, "gd", [128, 48, 512], BF16)
                    rgd = Res()
                    fw.dma("sync", gd[:, 0:32, :tn], goT[:, :, t0:t0 + tn].rearrange("c p t -> p c t"), writes=[rgd])
                    fw.dma("sync", gd[:, 32:48, :tn], doT[:, :, t0:t0 + tn].rearrange("c p t -> p c t"), writes=[rgd])
                    Wr = Rot([sbt(ph, "Wr%d" % i, [128, 32, 512], BF16) for i in range(2)])
                    Wd = Rot([sbt(ph, "Wd%d" % i, [128, 16, 512], BF16) for i in range(2)])
                    gts = Rot([sbt(ph, "gts%d" % i, [128, 2, 512], BF16) for i in range(2)])
                    m1s = Rot([sbt(ph, "m1_%d" % i, [128, 512]) for i in range(2)])
                    m2s = Rot([sbt(ph, "m2_%d" % i, [128, 512]) for i in range(2)])
                    p_r = Rot([pst(ph, "p_r%d" % i) for i in range(2)], x=True)
                    p_d = Rot([pst(ph, "p_d%d" % i) for i in range(2)], x=True)
                    for oc in range(KC):
                        sub = oc % 4
                        if sub == 0:
                            wr, rwr = Wr.next()
                            wd, rwd = Wd.next()
                            fw.dma("gpsimd", wr[:], wro_b[oc // 4], writes=[rwr])
                            fw.dma("gpsimd", wd[:], wdo_b[oc // 4], writes=[rwd])
                        gt_, rgt = gts.next()
                        fw.dma("sync", gt_[:, 0, :tn], gatesT[oc, :, t0:t0 + tn], writes=[rgt])
                        fw.dma("sync", gt_[:, 1, :tn], gatesT[16 + oc, :, t0:t0 + tn], writes=[rgt])
                        pr, rpr = p_r.next()
                        pd, rpd = p_d.next()
                        for k in range(32):
                            fw.op("tensor", lambda e, pr=pr, wr=wr, k=k, sub=sub: e.matmul(pr[:, :tn], lhsT=wr[:, k, sub * 128:(sub + 1) * 128], rhs=gd[:, k, :tn], start=(k == 0), stop=(k == 31)),
                                  reads=[rwr, rgd], writes=[rpr], inc=(k == 31))
                        for k in range(16):
                            fw.op("tensor", lambda e, pd=pd, wd=wd, k=k, sub=sub: e.matmul(pd[:, :tn], lhsT=wd[:, k, sub * 128:(sub + 1) * 128], rhs=gd[:, 32 + k, :tn], start=(k == 0), stop=(k == 15)),
                                  reads=[rwd, rgd], writes=[rpd], inc=(k == 15))
                        m1, rm1 = m1s.next()
                        m2, rm2 = m2s.next()
                        fw.op("vector", lambda e, m1=m1, pr=pr, gt_=gt_: e.tensor_tensor(out=m1[:, :tn], in0=pr[:, :tn], in1=gt_[:, 0, :tn], op=ALU.mult),
                              reads=[rpr, rgt], writes=[rm1])
                        fw.op("vector", lambda e, m2=m2, pd=pd, gt_=gt_: e.tensor_tensor(out=m2[:, :tn], in0=pd[:, :tn], in1=gt_[:, 1, :tn], op=ALU.mult),
                              reads=[rpd, rgt], writes=[rm2])
                        fw.op("vector", lambda e, m1=m1, m2=m2, oc=oc: e.tensor_tensor(out=actT[:, oc, :tn], in0=m1[:, :tn], in1=m2[:, :tn], op=ALU.add),
                              reads=[rm1, rm2], writes=[ract])
                    fw.replay()

                with ExitStack() as ph:
                    yz = sbt(ph, "yz", [128, 4, D])
                    ryz = [Res() for _ in range(4)]
                    h2T = sbt(ph, "h2T", [128, KC, 512], BF16)
                    rh2 = Res()
                    W = Rot([sbt(ph, "Wf%d" % i, [128, KC, 512], BF16) for i in range(2)])
                    Wo2 = sbt(ph, "Wo2", [128, KC, 512], BF16)
                    Wo3 = sbt(ph, "Wo3", [128, KC, 512], BF16)
                    GT = sbt(ph, "GT", [128, D])
                    rGT = Res()
                    xl = Rot([sbt(ph, "xl%d" % i, [128, D]) for i in range(2)])
                    tmp = {"ss": Rot([sbt(ph, "ss%d" % i, [128, 8]) for i in range(4)]),
                           "junk": Rot([sbt(ph, "junk", [128, D], BF16)]),
                           "xn": Rot([sbt(ph, "xn%d" % i, [128, D], BF16) for i in range(1)])}
                    rl = Rot([sbt(ph, "rl%d" % i, [128, 512]) for i in range(2)])
                    psT = Rot([pst(ph, "psT%d" % i, BF16) for i in range(2)], x=True)
                    pm = Rot([pst(ph, "pm%d" % i) for i in range(6)], x=True)
                    rx1 = [Res() for _ in range(4)]

                    def post_norm_residual(tt, which, src_rows, rsrc, dst_dram, rdst, keep=None):
                        ss, rss = tmp["ss"].next()
                        jk, rjk = tmp["junk"].next()
                        fw.op("scalar", lambda e: e.activation(out=jk[:], in_=yz[:, tt, :], func=AF.Square, scale=1.0 / math.sqrt(D), accum_out=ss[:, 0:1]),
                              reads=[ryz[tt]], writes=[rjk, rss])
                        fw.op("scalar", lambda e: e.activation(out=ss[:, 0:1], in_=ss[:, 0:1], func=AF.Sqrt, bias=EPS, scale=1.0), reads=[rss], writes=[rss])
                        fw.op("vector", lambda e: e.reciprocal(out=ss[:, 0:1], in_=ss[:, 0:1]), reads=[rss], writes=[rss])
                        x, rx = xl.next()
                        fw.dma("sync", x[:], src_rows, reads=rsrc, writes=[rx])
                        fw.op("vector", lambda e: e.scalar_tensor_tensor(out=yz[:, tt, :], in0=yz[:, tt, :], scalar=ss[:, 0:1], in1=GT[:], op0=ALU.mult, op1=ALU.mult),
                              reads=[rss, rGT], writes=[ryz[tt]])
                        fw.op("vector", lambda e: e.tensor_tensor(out=yz[:, tt, :], in0=yz[:, tt, :], in1=x[:], op=ALU.add), reads=[rx], writes=[ryz[tt]])
                        fw.dma("sync", dst_dram, yz[:, tt, :], reads=[ryz[tt]], writes=rdst)

                    fw.dma("sync", GT[:], gt_scr[cond, 0:1, :].partition_broadcast(128), writes=[rGT])
                    Wo = [W.tiles[0], W.tiles[1], (Wo2, Res()), (Wo3, Res())]
                    for cb in range(4):
                        fw.dma("gpsimd", Wo[cb][0][:], wout_b[cb], writes=[Wo[cb][1]])

                    def mm_wout(tt):
                        for cb in range(4):
                            Wt, rW = Wo[cb]
                            ps, rps = pm.next()
                            for k in range(KC):
                                fw.op("tensor", lambda e, ps=ps, Wt=Wt, k=k, tt=tt: e.matmul(ps[:], lhsT=actT[:, k, tt * 128:(tt + 1) * 128], rhs=Wt[:, k, :],
                                                                                            start=(k == 0), stop=(k == KC - 1)),
                                      reads=[rW, ract], writes=[rps], inc=(k == KC - 1))
                            fw.op("scalar", lambda e, ps=ps, tt=tt, cb=cb: e.activation(out=yz[:, tt, cb * 512:(cb + 1) * 512], in_=ps[:], func=AF.Copy),
                                  reads=[rps], writes=[ryz[tt]])

                    mm_wout(0)
                    for tt in range(ntl):
                        if tt + 1 < ntl:
                            mm_wout(tt + 1)
                        gtt = t0 // 128 + tt
                        post_norm_residual(tt, 0, xrows(gtt), [], x1_scr[gtt * 128:(gtt + 1) * 128, :], [rx1[tt]])
                        norm_mod_T(tmp, yz[:, tt, :], ryz[tt], cond, A2, 48, lambda k, tt=tt: h2T[:, k, tt * 128:(tt + 1) * 128], rh2, psT)
                    fw.dma("sync", GT[:], gt_scr[cond, 1:2, :].partition_broadcast(128), writes=[rGT])
                    for fb in range(4):
                        for ub in range(4):
                            Wt, rW = W.next()
                            c0 = fb * 2048 + ub * 512
                            fw.dma("gpsimd", Wt[:], wup_b[fb * 4 + ub], writes=[rW])
                            for sub in range(4):
                                ps, rps = pm.next()
                                for k in range(KC):
                                    fw.op("tensor", lambda e, ps=ps, Wt=Wt, k=k, sub=sub: e.matmul(ps[:, :tn], lhsT=Wt[:, k, sub * 128:(sub + 1) * 128], rhs=h2T[:, k, :tn],
                                                                                                  start=(k == 0), stop=(k == KC - 1)),
                                          reads=[rW, rh2], writes=[rps], inc=(k == KC - 1))
                                r_, rr_ = rl.next()
                                fw.op("scalar", lambda e, r_=r_, ps=ps: e.activation(out=r_[:, :tn], in_=ps[:, :tn], func=AF.Relu), reads=[rps], writes=[rr_])
                                fc = ub * 4 + sub
                                fw.op("vector", lambda e, r_=r_, fc=fc: e.tensor_tensor(out=actT[:, fc, :tn], in0=r_[:, :tn], in1=r_[:, :tn], op=ALU.mult),
                                      reads=[rr_], writes=[ract])
                        for cb in range(4):
                            Wt, rW = W.next()
                            fw.dma("gpsimd", Wt[:], wdn_b[fb * 4 + cb], writes=[rW])
                            for tt in range(ntl):
                                ps, rps = pm.next()
                                for k in range(KC):
                                    fw.op("tensor", lambda e, ps=ps, Wt=Wt, k=k, tt=tt: e.matmul(ps[:], lhsT=actT[:, k, tt * 128:(tt + 1) * 128], rhs=Wt[:, k, :],
                                                                                                start=(k == 0), stop=(k == KC - 1)),
                                          reads=[rW, ract], writes=[rps], inc=(k == KC - 1))
                                zs = yz[:, tt, cb * 512:(cb + 1) * 512]
                                if fb == 0:
                                    fw.op("scalar", lambda e, ps=ps, zs=zs: e.activation(out=zs, in_=ps[:], func=AF.Copy), reads=[rps], writes=[ryz[tt]])
                                else:
                                    fw.op("vector", lambda e, ps=ps, zs=zs: e.tensor_tensor(out=zs, in0=ps[:], in1=zs, op=ALU.add), reads=[rps], writes=[ryz[tt]])
                    for tt in range(ntl):
                        gtt = t0 // 128 + tt
                        post_norm_residual(tt, 1, x1_scr[gtt * 128:(gtt + 1) * 128, :], [rx1[tt]], yrows(gtt), [])
                    fw.replay()

            for (t0, tn) in groups:
                ef_group(t0, tn)
    return nc


def _consts(TS):
    c = np.zeros((128, NCONST), np.float32)
    p = np.arange(128, dtype=np.float32)[:, None]
    i = np.arange(128, dtype=np.float32)[None, :]
    c[:, 0:128] = np.eye(128, dtype=np.float32)
    c[:, 128:256] = (i >= p)
    c[:, 256:384] = (p >= i)
    c[:, 384:512] = np.maximum(i - p, 0)
    c[:, 512:640] = np.maximum(p - i, 0)
    c[:, 640:768] = i + 1.0
    c[:, 768:896] = 128.0 - i
    c[:, 896] = 127.0 - p[:, 0]
    c[:, 897] = p[:, 0]
    f = np.arange(128)
    partner = np.where((f % 64) < 32, f + 32, f - 32)
    P = np.zeros((128, 128), np.float32)
    P[partner, f] = 1.0
    c[:, 898:1026] = P
    t = np.arange(TS)
    row = (t // 64).astype(np.float32)
    col = (t % 64).astype(np.float32)
    inv = (10000.0 ** (-np.arange(0, 64, 2, dtype=np.float32) / 64.0)).astype(np.float32)
    ang = np.where((f[:, None] < 64), row[None, :] * inv[f % 32][:, None], col[None, :] * inv[f % 32][:, None]).astype(np.float32)
    sgn = np.where((f % 64) < 32, -1.0, 1.0).astype(np.float32)[:, None]
    cos = np.cos(ang).astype(np.float32)
    sin = (np.sin(ang) * sgn).astype(np.float32)
    return c, cos, sin


_NC_CACHE = {}


def run(inputs, TS, TP, PAST):
    key = (TS, TP, PAST)
    if key not in _NC_CACHE:
        _NC_CACHE[key] = build(TS, TP, PAST)
    nc = _NC_CACHE[key]
    f = lambda a: np.ascontiguousarray(np.asarray(a, dtype=np.float32))
    cst, cos, sin = _consts(TS)
    shared = {
        "w_ada": f(inputs["w_ada"][0]), "b_ada": f(inputs["b_ada"]), "w_in": f(inputs["w_in"][0]),
        "glog": f(np.concatenate([inputs["ret_gamma_logit_fwd"][0], inputs["ret_gamma_logit_bwd"][0]])[None, :]),
        "w_ret_o": f(inputs["w_ret_o"][0]),
        "lam4": f(np.concatenate([inputs["lambda_q1"][0], inputs["lambda_k1"][0], inputs["lambda_q2"][0], inputs["lambda_k2"][0]])[None, :]),
        "g_sub": f(inputs["g_diff_subln"]), "w_diff_o": f(inputs["w_diff_o"][0]), "w_out": f(inputs["w_out"][0]),
        "w_up": f(inputs["w_mlp_up"][0]), "w_down": f(inputs["w_mlp_down"][0]),
        "gpost": f(np.stack([inputs["g_mix_post"][0], inputs["g_mlp_post"][0]])),
        "consts": cst, "rope_cos": cos, "rope_sin": sin,
    }
    in_maps = []
    for c in range(8):
        m = dict(shared)
        m["x_s"] = f(inputs["x_sample"][c])
        m["x_p"] = f(np.asarray(inputs["x_prompt"][2 * c:2 * c + 2]).reshape(2 * TP, D))
        m["cache_k"] = f(inputs["cache_k"][c, 0])
        m["cache_v"] = f(inputs["cache_v"][c, 0])
        m["st_f"] = f(inputs["state_ret_fwd"][c, 0])
        m["st_b"] = f(inputs["state_ret_bwd"][c, 0])
        m["cvec"] = f(np.stack([inputs["c"][c], inputs["c_ctx"], inputs["g_mix_pre"][0], inputs["g_mlp_pre"][0]]))
        in_maps.append(m)
    res = run_bass_kernel_spmd(nc, in_maps, core_ids=list(range(8)))
    R = res.results
    y_prompt = np.concatenate([R[c]["y_p"].reshape(2, TP, D) for c in range(8)], axis=0)
    y_sample = np.stack([R[c]["y_s"] for c in range(8)], axis=0)
    new_k = np.concatenate([R[c]["nk"].reshape(2, 1, TP, H, 2, 128) for c in range(8)], axis=0)
    new_v = np.concatenate([R[c]["nv"].reshape(2, 1, TP, H, 256) for c in range(8)], axis=0)
    new_sf = np.concatenate([R[c]["nsf"].reshape(2, 1, H, 256, 512) for c in range(8)], axis=0)
    new_sb = np.concatenate([R[c]["nsb"].reshape(2, 1, H, 256, 512) for c in range(8)], axis=0)
    return tuple(np.asarray(a, dtype=np.float32) for a in (y_prompt, y_sample, new_k, new_v, new_sf, new_sb))


def kernel(**inputs):
    return run(inputs, 2048, 256, 256)
```

```python
import math
import os
from contextlib import ExitStack

import numpy as np
import concourse.bass as bass
import concourse.mybir as mybir
from concourse.bass_utils import run_bass_kernel_spmd

F32 = mybir.dt.float32
BF16 = mybir.dt.bfloat16
AF = mybir.ActivationFunctionType
ALU = mybir.AluOpType
AX = mybir.AxisListType

D = 2048
KC = 16
H = 8
DFF = 8192
INW = 22528
EPS = 1e-6
SUBLN_EPS = 1e-5
LAM_INIT = 0.8 - 0.6 * math.exp(-0.3 * 0)
NCONST = 1154


class Res:
    __slots__ = ("w", "r", "x")

    def __init__(self, x=False):
        self.w = None
        self.r = []
        self.x = x


class FW:
    def __init__(self, nc, stack):
        self.nc = nc
        self.names = ["tensor", "vector", "scalar", "gpsimd", "sync"]
        self.prog = {n: [] for n in self.names}
        self.cnt = {n: 0 for n in self.names}
        self.waited = {n: {} for n in self.names}
        self.sem = {n: stack.enter_context(nc.semaphore("s_" + n)) for n in self.names}
        self.NDS = 8
        self.dma_sems = {}
        self.dma_cnt = {}
        self.dma_i = {n: 0 for n in self.names}
        for q in ["sync", "gpsimd"]:
            for i in range(self.NDS):
                self.dma_sems[(q, i)] = stack.enter_context(nc.semaphore("d_%s%d" % (q, i)))
                self.dma_cnt[(q, i)] = 0

    def _wait(self, eng, tok):
        if tok is None:
            return
        sem, val = tok
        if sem is self.sem.get(eng):
            if eng == "tensor" or val > self.cnt[eng]:
                return
        key = id(sem)
        if self.waited[eng].get(key, 0) >= val:
            return
        self.waited[eng][key] = val
        self.prog[eng].append(("wait", sem, val))

    def _deps(self, eng, reads, writes):
        for r in reads:
            self._wait(eng, r.w)
        for w in writes:
            self._wait(eng, w.w)
            for t in w.r:
                self._wait(eng, t)

    def _upd(self, tok, reads, writes):
        for r in reads:
            r.r.append(tok)
        for w in writes:
            w.w = tok
            w.r = []

    def op(self, eng, fn, reads=(), writes=(), inc=True):
        xr = [r for r in reads if r.x]
        if xr:
            reads = [r for r in reads if not r.x]
            writes = list(writes) + xr
        self._deps(eng, reads, writes)
        if inc:
            self.cnt[eng] += 1
            tok = (self.sem[eng], self.cnt[eng])
            self.prog[eng].append(("inst", fn, self.sem[eng], 1))
        else:
            tok = (self.sem[eng], self.cnt[eng] + 1)
            self.prog[eng].append(("inst", fn, None, 0))
        self._upd(tok, reads, writes)
        return tok

    def dma(self, q, out, in_, reads=(), writes=()):
        i = self.dma_i[q] % self.NDS
        self.dma_i[q] += 1
        k = (q, i)
        sem = self.dma_sems[k]
        if self.dma_cnt[k] > 0:
            self._wait(q, (sem, self.dma_cnt[k]))
        self._deps(q, reads, writes)
        self.dma_cnt[k] += 16
        tok = (sem, self.dma_cnt[k])
        self.prog[q].append(("inst", lambda e, o=out, i_=in_: e.dma_start(out=o, in_=i_), sem, 16))
        self._upd(tok, reads, writes)
        return tok

    def barrier(self):
        toks = [(self.sem[n], self.cnt[n]) for n in self.names if self.cnt[n] > 0]
        toks += [(self.dma_sems[k], v) for k, v in self.dma_cnt.items() if v > 0]
        for e in self.names:
            for t in toks:
                self._wait(e, t)

    def replay(self):
        self.barrier()
        with self.nc.Block() as block:
            def mk(name):
                items = self.prog[name]

                def body(e):
                    for it in items:
                        if it[0] == "wait":
                            e.wait_ge(it[1], it[2])
                        else:
                            ins = it[1](e)
                            if it[2] is not None:
                                ins.then_inc(it[2], it[3])
                return body
            block.tensor(mk("tensor"))
            block.vector(mk("vector"))
            block.scalar(mk("scalar"))
            block.gpsimd(mk("gpsimd"))
            block.sync(mk("sync"))
        self.prog = {n: [] for n in self.names}


class Rot:
    def __init__(self, tiles, x=False):
        self.tiles = [(t, Res(x)) for t in tiles]
        self.i = 0

    def next(self):
        t = self.tiles[self.i % len(self.tiles)]
        self.i += 1
        return t


def build(TS, TP, PAST):
    NTOK = TS + 2 * TP
    NT = NTOK // 128
    NTS = TS // 128
    PG = 2 * TP
    assert TS % 512 == 0 and PG <= 512 and PG % 128 == 0 and PAST % 128 == 0
    groups = [(g * 512, 512) for g in range(TS // 512)] + [(TS, PG)]
    seqs = [(0, TS, True, 0), (TS, TP, False, 0), (TS + TP, TP, False, 1)]

    nc = bass.Bass("TRN2", target_bir_lowering=False)

    def din(name, shape, dt=F32):
        return nc.dram_tensor(name, shape, dt, kind="ExternalInput").ap()

    def dout(name, shape):
        return nc.dram_tensor(name, shape, F32, kind="ExternalOutput").ap()

    def dscr(name, shape, dt=BF16):
        return nc.dram_tensor(name, shape, dt).ap()

    x_s = din("x_s", [TS, D])
    x_p = din("x_p", [PG, D])
    cache_k = din("cache_k", [PAST, H, 2, 128])
    cache_v = din("cache_v", [PAST, H, 256])
    st_f = din("st_f", [H, 256, 512])
    st_b = din("st_b", [H, 256, 512])
    cvec = din("cvec", [4, D])
    gpost = din("gpost", [2, D])
    w_ada = din("w_ada", [D, 6 * D])
    b_ada = din("b_ada", [1, 6 * D])
    w_in = din("w_in", [D, INW])
    glog = din("glog", [1, 16])
    w_ret_o = din("w_ret_o", [4096, D])
    lam4 = din("lam4", [1, 512])
    g_sub = din("g_sub", [1, 256])
    w_diff_o = din("w_diff_o", [D, D])
    w_out = din("w_out", [D, D])
    w_up = din("w_up", [D, DFF])
    w_down = din("w_down", [DFF, D])
    consts = din("consts", [128, NCONST])
    rope_cos = din("rope_cos", [128, TS])
    rope_sin = din("rope_sin", [128, TS])

    y_s = dout("y_s", [TS, D])
    y_p = dout("y_p", [PG, D])
    nk = dout("nk", [PG, D])
    nv = dout("nv", [PG, D])
    nsf = dout("nsf", [2, H, 256, 512])
    nsb = dout("nsb", [2, H, 256, 512])

    mod_scr = dscr("mod_scr", [2, 6 * D], F32)
    gt_scr = dscr("gt_scr", [2, 2, D], F32)
    rqT = dscr("rqT", [16, 128, NTOK])
    rkT = dscr("rkT", [16, 128, NTOK])
    rk_tok = dscr("rk_tok", [NTOK, 2048])
    rv_tok = dscr("rv_tok", [NTOK, 4096])
    rg_tok = dscr("rg_tok", [NTOK, 4096])
    dqT = dscr("dqT", [16, 128, NTOK])
    dkT = dscr("dkT", [16, 128, NTOK])
    dv_tok = dscr("dv_tok", [NTOK, 2048])
    gatesT = dscr("gatesT", [32, 128, NTOK])
    goT = dscr("goT", [32, 128, NTOK])
    doT = dscr("doT", [16, 128, NTOK])
    x1_scr = dscr("x1_scr", [NTOK, D], F32)
    wro_b = dscr("wro_b", [4, 128, 32, 512])
    wdo_b = dscr("wdo_b", [4, 128, 16, 512])
    wout_b = dscr("wout_b", [4, 128, 16, 512])
    wup_b = dscr("wup_b", [16, 128, 16, 512])
    wdn_b = dscr("wdn_b", [16, 128, 16, 512])

    def xrows(tt):
        t0 = tt * 128
        return x_s[t0:t0 + 128, :] if t0 < TS else x_p[t0 - TS:t0 - TS + 128, :]

    def yrows(tt):
        t0 = tt * 128
        return y_s[t0:t0 + 128, :] if t0 < TS else y_p[t0 - TS:t0 - TS + 128, :]

    with ExitStack() as top:
        fw = FW(nc, top)

        uniq = [0]

        def sbt(st, name, shape, dt=F32):
            uniq[0] += 1
            return st.enter_context(nc.sbuf_tensor("%s_%d" % (name, uniq[0]), shape, dt))

        def pst(st, name, dt=F32):
            uniq[0] += 1
            return st.enter_context(nc.psum_tensor("%s_%d" % (name, uniq[0]), [128, 512 if dt == F32 else 1024], dt))

        C = sbt(top, "C", [128, NCONST])
        identf = C[:, 0:128]
        Umask = C[:, 128:256]
        Lmask = C[:, 256:384]
        relF = C[:, 384:512]
        relB = C[:, 512:640]
        iota1 = C[:, 640:768]
        iotaB = C[:, 768:896]
        colA = C[:, 896:897]
        colB = C[:, 897:898]
        identb = sbt(top, "identb", [128, 128], BF16)
        onesb = sbt(top, "onesb", [128, 128], BF16)
        Pmat = sbt(top, "Pmat", [128, 128], BF16)
        modF = sbt(top, "modF", [128, 2, 96])
        vF = sbt(top, "vF", [128, 64])
        A1 = sbt(top, "A1", [128, 2, 16])
        A2 = sbt(top, "A2", [128, 2, 16])
        maskT = sbt(top, "maskT", [128, H, 128])
        XIF = sbt(top, "XIF", [128, H, 128])
        XIB = sbt(top, "XIB", [128, H, 128])
        LG = sbt(top, "LG", [128, 16])
        ZF = sbt(top, "ZF", [128, H])
        ZB = sbt(top, "ZB", [128, H])
        GF = sbt(top, "GF", [128, H])
        GB = sbt(top, "GB", [128, H])
        neglam = sbt(top, "neglam", [128, 1])
        GS = sbt(top, "GS", [128, 2])
        RP = Res()

        with ExitStack() as ph:
            ps0 = pst(ph, "ps0")
            ps1 = pst(ph, "ps1")
            rps0, rps1 = Res(True), Res(True)
            fw.dma("sync", C[:], consts, writes=[RP])
            fw.op("vector", lambda e: e.tensor_copy(out=identb[:], in_=identf), reads=[RP], writes=[RP])
            fw.op("vector", lambda e: e.memset(onesb[:], 1.0), writes=[RP])
            fw.op("vector", lambda e: e.tensor_copy(out=Pmat[:], in_=C[:, 898:1026]), reads=[RP], writes=[RP])
            v64 = sbt(ph, "v64", [64, 128])
            rv64 = Res()
            fw.dma("sync", v64[:], cvec.rearrange("r (k p) -> (r k) p", p=128), writes=[rv64])
            fw.op("tensor", lambda e: e.transpose(out=ps0[:, 0:64], in_=v64[:], identity=identf[0:64, 0:64]),
                  reads=[rv64, RP], writes=[rps0])
            fw.op("vector", lambda e: e.tensor_copy(out=vF[:], in_=ps0[:, 0:64]), reads=[rps0], writes=[RP])
            silF = sbt(ph, "silF", [128, 32])
            rsil = Res()
            fw.op("scalar", lambda e: e.activation(out=silF[:], in_=vF[:, 0:32], func=AF.Silu), reads=[RP], writes=[rsil])
            mrow = sbt(ph, "mrow", [2, 6 * D])
            rm = Res()
            fw.dma("sync", mrow[:], b_ada.partition_broadcast(2), writes=[rm])
            wa = [sbt(ph, "wa%d" % i, [128, KC, 512]) for i in range(2)]
            rwa = [Res(), Res()]
            for nb in range(24):
                b = nb % 2
                fw.dma("sync", wa[b][:], w_ada[:, nb * 512:(nb + 1) * 512].rearrange("(k p) n -> p k n", p=128), writes=[rwa[b]])
                for k in range(KC):
                    fw.op("tensor", lambda e, k=k, b=b: e.matmul(ps1[0:2, :], lhsT=silF[:, k:32:16], rhs=wa[b][:, k, :],
                                                                 start=(k == 0), stop=(k == KC - 1)),
                          reads=[rsil, rwa[b]], writes=[rps1], inc=(k == KC - 1))
                fw.op("vector", lambda e, nb=nb: e.tensor_tensor(out=mrow[:, nb * 512:(nb + 1) * 512], in0=ps1[0:2, :],
                                                                 in1=mrow[:, nb * 512:(nb + 1) * 512], op=ALU.add),
                      reads=[rps1], writes=[rm])
            rms = Res()
            fw.dma("sync", mod_scr, mrow[:], reads=[rm], writes=[rms])
            gp = sbt(ph, "gp", [2, 2, D])
            rgp = Res()
            fw.dma("sync", gp[:, 0, :], gpost[0:1, :].partition_broadcast(2), writes=[rgp])
            fw.dma("sync", gp[:, 1, :], gpost[1:2, :].partition_broadcast(2), writes=[rgp])
            fw.op("vector", lambda e: e.tensor_tensor(out=gp[:, 0, :], in0=gp[:, 0, :], in1=mrow[:, 2 * D:3 * D], op=ALU.mult),
                  reads=[rm], writes=[rgp])
            fw.op("vector", lambda e: e.tensor_tensor(out=gp[:, 1, :], in0=gp[:, 1, :], in1=mrow[:, 5 * D:6 * D], op=ALU.mult),
                  reads=[rm], writes=[rgp])
            fw.dma("sync", gt_scr, gp[:], reads=[rgp])
            m96 = sbt(ph, "m96", [96, 2, 128])
            rm96 = Res()
            for c in range(2):
                fw.dma("sync", m96[:, c, :], mod_scr[c:c + 1, :].rearrange("o (k p) -> (o k) p", p=128), reads=[rms], writes=[rm96])
            for c in range(2):
                fw.op("tensor", lambda e, c=c: e.transpose(out=ps0[:, c * 96:(c + 1) * 96], in_=m96[:, c, :], identity=identf[0:96, 0:96]),
                      reads=[rm96, RP], writes=[rps0])
            fw.op("vector", lambda e: e.tensor_copy(out=modF[:].rearrange("p c f -> p (c f)"), in_=ps0[:, 0:192]), reads=[rps0], writes=[RP])
            for c in range(2):
                fw.op("vector", lambda e, c=c: e.scalar_tensor_tensor(out=A1[:, c, :], in0=modF[:, c, 16:32], scalar=1.0, in1=vF[:, 32:48],
                                                                      op0=ALU.add, op1=ALU.mult), reads=[RP], writes=[RP])
                fw.op("vector", lambda e, c=c: e.scalar_tensor_tensor(out=A2[:, c, :], in0=modF[:, c, 64:80], scalar=1.0, in1=vF[:, 48:64],
                                                                      op0=ALU.add, op1=ALU.mult), reads=[RP], writes=[RP])
            l4 = sbt(ph, "l4", [128, 512])
            rl4 = Res()
            fw.dma("sync", l4[:], lam4.partition_broadcast(128), writes=[rl4])
            lp = sbt(ph, "lp", [128, 2, 128])
            ls = sbt(ph, "ls", [128, 2])
            fw.op("vector", lambda e: e.tensor_tensor(out=lp[:, 0, :], in0=l4[:, 0:128], in1=l4[:, 128:256], op=ALU.mult), reads=[rl4], writes=[rl4])
            fw.op("vector", lambda e: e.tensor_tensor(out=lp[:, 1, :], in0=l4[:, 256:384], in1=l4[:, 384:512], op=ALU.mult), reads=[rl4], writes=[rl4])
            fw.op("vector", lambda e: e.reduce_sum(out=ls[:], in_=lp[:], axis=AX.X), reads=[rl4], writes=[rl4])
            fw.op("scalar", lambda e: e.activation(out=ls[:], in_=ls[:], func=AF.Exp), reads=[rl4], writes=[rl4])
            fw.op("vector", lambda e: e.tensor_tensor(out=neglam[:], in0=ls[:, 1:2], in1=ls[:, 0:1], op=ALU.subtract), reads=[rl4], writes=[RP])
            fw.op("vector", lambda e: e.tensor_scalar_add(out=neglam[:], in0=neglam[:], scalar1=-LAM_INIT), reads=[RP], writes=[RP])
            fw.dma("sync", LG[:], glog.partition_broadcast(128), writes=[RP])
            fw.op("scalar", lambda e: e.activation(out=LG[:], in_=LG[:], func=AF.Sigmoid), reads=[RP], writes=[RP])
            fw.op("scalar", lambda e: e.activation(out=LG[:], in_=LG[:], func=AF.Ln), reads=[RP], writes=[RP])
            tz = sbt(ph, "tz", [128, 4, H])
            fw.op("vector", lambda e: e.tensor_scalar(out=tz[:, 0, :], in0=LG[:, 0:8], scalar1=colA, scalar2=None, op0=ALU.mult), reads=[RP], writes=[RP])
            fw.op("vector", lambda e: e.tensor_scalar(out=tz[:, 1, :], in0=LG[:, 8:16], scalar1=colB, scalar2=None, op0=ALU.mult), reads=[RP], writes=[RP])
            fw.op("vector", lambda e: e.tensor_scalar(out=tz[:, 2, :], in0=LG[:, 0:8], scalar1=128.0, scalar2=None, op0=ALU.mult), reads=[RP], writes=[RP])
            fw.op("vector", lambda e: e.tensor_scalar(out=tz[:, 3, :], in0=LG[:, 8:16], scalar1=128.0, scalar2=None, op0=ALU.mult), reads=[RP], writes=[RP])
            for i, dst in enumerate([ZF, ZB, GF, GB]):
                fw.op("scalar", lambda e, i=i, dst=dst: e.activation(out=dst[:], in_=tz[:, i, :], func=AF.Exp), reads=[RP], writes=[RP])
            mtmp = sbt(ph, "mtmp", [128, 2, 128])
            for h in range(H):
                fw.op("scalar", lambda e, h=h: e.activation(out=mtmp[:, 0, :], in_=relF, func=AF.Exp, scale=LG[:, h:h + 1]), reads=[RP], writes=[RP])
                fw.op("scalar", lambda e, h=h: e.activation(out=mtmp[:, 1, :], in_=relB, func=AF.Exp, scale=LG[:, 8 + h:9 + h]), reads=[RP], writes=[RP])
                fw.op("vector", lambda e: e.tensor_tensor(out=mtmp[:, 0, :], in0=mtmp[:, 0, :], in1=Umask, op=ALU.mult), reads=[RP], writes=[RP])
                fw.op("vector", lambda e: e.tensor_tensor(out=mtmp[:, 1, :], in0=mtmp[:, 1, :], in1=Lmask, op=ALU.mult), reads=[RP], writes=[RP])
                fw.op("vector", lambda e, h=h: e.tensor_tensor(out=maskT[:, h, :], in0=mtmp[:, 0, :], in1=mtmp[:, 1, :], op=ALU.add), reads=[RP], writes=[RP])
                fw.op("scalar", lambda e, h=h: e.activation(out=XIF[:, h, :], in_=iota1, func=AF.Exp, scale=LG[:, h:h + 1]), reads=[RP], writes=[RP])
                fw.op("scalar", lambda e, h=h: e.activation(out=XIB[:, h, :], in_=iotaB, func=AF.Exp, scale=LG[:, 8 + h:9 + h]), reads=[RP], writes=[RP])
            g2 = sbt(ph, "g2", [2, 128])
            rg2 = Res()
            fw.dma("sync", g2[:], g_sub.rearrange("o (h p) -> (o h) p", p=128), writes=[rg2])
            fw.op("tensor", lambda e: e.transpose(out=ps1[:, 0:2], in_=g2[:], identity=identf[0:2, 0:2]), reads=[rg2, RP, rps1], writes=[rps1])
            fw.op("vector", lambda e: e.tensor_scalar(out=GS[:], in0=ps1[:, 0:2], scalar1=1.0 - LAM_INIT, scalar2=None, op0=ALU.mult),
                  reads=[rps1], writes=[RP])
            fw.replay()
        if os.environ.get("KSTOP") == "0":
            return nc

        def norm_mod_T(tmp, src, rsrc, cond, Amat, shift_off, dst, rdst, psT):
            st = norm_part1(tmp, src, rsrc, psT)
            norm_part2(st, cond, Amat, shift_off, dst, rdst)

        def norm_part1(tmp, src, rsrc, psT):
            ss, rss = tmp["ss"].next()
            junk, rj = tmp["junk"].next()
            xn, rxn = tmp["xn"].next()
            fw.op("scalar", lambda e: e.activation(out=junk[:], in_=src, func=AF.Square, scale=1.0 / math.sqrt(D), accum_out=ss[:, 0:1]),
                  reads=[rsrc], writes=[rj, rss])
            fw.op("scalar", lambda e: e.activation(out=ss[:, 0:1], in_=ss[:, 0:1], func=AF.Sqrt, bias=EPS, scale=1.0), reads=[rss], writes=[rss])
            fw.op("vector", lambda e: e.reciprocal(out=ss[:, 0:1], in_=ss[:, 0:1]), reads=[rss], writes=[rss])
            fw.op("scalar", lambda e: e.activation(out=xn[:], in_=src, func=AF.Copy, scale=ss[:, 0:1]), reads=[rsrc, rss], writes=[rxn])
            pts = []
            for half in range(2):
                pt, rpt = psT.next()
                for j in range(8):
                    k = half * 8 + j
                    fw.op("tensor", lambda e, k=k, j=j, pt=pt: e.transpose(out=pt[:, j * 128:(j + 1) * 128], in_=xn[:, k * 128:(k + 1) * 128], identity=identb[:]),
                          reads=[rxn], writes=[rpt], inc=(j == 7))
                pts.append((pt, rpt))
            return pts

        def norm_part2(pts, cond, Amat, shift_off, dst, rdst):
            for half in range(2):
                pt, rpt = pts[half]
                for j in range(8):
                    k = half * 8 + j
                    if j % 2 == 0:
                        fw.op("vector", lambda e, k=k, j=j, pt=pt: e.tensor_scalar(out=dst(k), in0=pt[:, j * 128:(j + 1) * 128], scalar1=Amat[:, cond, k:k + 1],
                                                                                   scalar2=modF[:, cond, shift_off + k:shift_off + k + 1], op0=ALU.mult, op1=ALU.add),
                              reads=[rpt], writes=[rdst])
                    else:
                        fw.op("scalar", lambda e, k=k, j=j, pt=pt: e.activation(out=dst(k), in_=pt[:, j * 128:(j + 1) * 128], func=AF.Identity,
                                                                                scale=Amat[:, cond, k:k + 1], bias=modF[:, cond, shift_off + k:shift_off + k + 1]),
                              reads=[rpt], writes=[rdst])

        with ExitStack() as phAB:
            hT = sbt(phAB, "hT", [128, KC, NTOK], BF16)
            rhT = Res()
            with ExitStack() as ph:
                xt = Rot([sbt(ph, "xt%d" % i, [128, D]) for i in range(3)])
                tmp = {"ss": Rot([sbt(ph, "ss%d" % i, [128, 1]) for i in range(4)]),
                       "junk": Rot([sbt(ph, "junk", [128, D], BF16)]),
                       "xn": Rot([sbt(ph, "xn%d" % i, [128, D], BF16) for i in range(2)])}
                psT = Rot([pst(ph, "psT%d" % i, BF16) for i in range(4)], x=True)
                def partA(tt):
                    x, rx = xt.next()
                    fw.dma("sync", x[:], xrows(tt), writes=[rx])
                    return norm_part1(tmp, x[:], rx, psT)

                stA = partA(0)
                for tt in range(NT):
                    stN = partA(tt + 1) if tt + 1 < NT else None
                    cond = 0 if tt < NTS else 1
                    norm_part2(stA, cond, A1, 0, lambda k, tt=tt: hT[:, k, tt * 128:(tt + 1) * 128], rhT)
                    stA = stN
                fw.replay()
            if os.environ.get("KSTOP") == "A":
                return nc

            with ExitStack() as ph:
                W = Rot([sbt(ph, "W%d" % i, [128, KC, 512], BF16) for i in range(2)])
                cosT = sbt(ph, "cosT", [128, TS])
                sinT = sbt(ph, "sinT", [128, TS])
                rrope = Res()
                fw.dma("sync", cosT[:], rope_cos, writes=[rrope])
                fw.dma("sync", sinT[:], rope_sin, writes=[rrope])
                stg = Rot([sbt(ph, "stg%d" % i, [128, 512], BF16) for i in range(4)])
                stf = Rot([sbt(ph, "stf%d" % i, [128, 512]) for i in range(2)])
                qraw = Rot([sbt(ph, "qraw%d" % i, [128, 512], BF16) for i in range(2)])
                t1s = Rot([sbt(ph, "t1_%d" % i, [128, 512]) for i in range(2)])
                t2s = Rot([sbt(ph, "t2_%d" % i, [128, 512]) for i in range(2)])
                pmain = Rot([pst(ph, "pm%d" % i) for i in range(5)], x=True)
                prot = Rot([pst(ph, "pr%d" % i) for i in range(2)], x=True)
                flip = [0]

                def evac_copy(dst, ps, rps, rdst, scale=None):
                    flip[0] ^= 1
                    if flip[0]:
                        if scale is None:
                            fw.op("vector", lambda e: e.tensor_copy(out=dst, in_=ps), reads=[rps], writes=[rdst])
                        else:
                            fw.op("vector", lambda e: e.tensor_scalar(out=dst, in0=ps, scalar1=scale, scalar2=None, op0=ALU.mult), reads=[rps], writes=[rdst])
                    else:
                        fw.op("scalar", lambda e: e.activation(out=dst, in_=ps, func=AF.Copy, scale=(1.0 if scale is None else scale)),
                              reads=[rps], writes=[rdst])

                wbl = range(INW // 512)
                if os.environ.get("KWB"):
                    wbl = [int(v) for v in os.environ["KWB"].split(",")]
                for wb in wbl:
                    col0 = wb * 512
                    Wt, rW = W.next()
                    fw.dma("gpsimd", Wt[:], w_in[:, col0:col0 + 512].rearrange("(k p) n -> p k n", p=128), writes=[rW])
                    if wb < 4:
                        kind, fdst, fbase = "rq", rqT, 0
                    elif wb < 8:
                        kind, fdst, fbase = "rk", rkT, 2048
                    elif wb < 16:
                        kind = "rv"
                    elif wb < 24:
                        kind = "rg"
                    elif wb < 28:
                        kind, fdst, fbase = "dq", dqT, 12288
                    elif wb < 32:
                        kind, fdst, fbase = "dk", dkT, 14336
                    elif wb < 36:
                        kind = "dv"
                    else:
                        kind, fdst, fbase = "gate", gatesT, 18432
                    pend_tail = [None]
                    if kind in ("rq", "rk", "dq", "dk", "gate"):
                        for sub in range(4):
                            fci = (col0 + sub * 128 - fbase) // 128
                            for gi, (t0, tn) in enumerate(groups):
                                ps, rps = pmain.next()
                                for k in range(KC):
                                    fw.op("tensor", lambda e, k=k, ps=ps, Wt=Wt, sub=sub, t0=t0, tn=tn: e.matmul(
                                        ps[:, :tn], lhsT=Wt[:, k, sub * 128:(sub + 1) * 128], rhs=hT[:, k, t0:t0 + tn], start=(k == 0), stop=(k == KC - 1)),
                                        reads=[rW], writes=[rps], inc=(k == KC - 1))
                                if kind in ("dq", "dk") and t0 < TS:
                                    qr, rqr = qraw.next()
                                    fw.op("scalar", lambda e, qr=qr, ps=ps, tn=tn: e.activation(out=qr[:, :tn], in_=ps[:, :tn], func=AF.Copy), reads=[rps], writes=[rqr])

                                    def rope_tail(ps=ps, rps=rps, qr=qr, rqr=rqr, t0=t0, tn=tn, fci=fci, fdst=fdst):
                                        so, rso = stg.next()
                                        pr, rpr = prot.next()
                                        fw.op("tensor", lambda e: e.matmul(pr[:, :tn], lhsT=Pmat[:], rhs=qr[:, :tn], start=True, stop=True), reads=[rqr], writes=[rpr])
                                        t1, rt1 = t1s.next()
                                        t2, rt2 = t2s.next()
                                        fw.op("vector", lambda e: e.tensor_tensor(out=t1[:, :tn], in0=ps[:, :tn], in1=cosT[:, t0:t0 + tn], op=ALU.mult),
                                              reads=[rps, rrope], writes=[rt1])
                                        fw.op("vector", lambda e: e.tensor_tensor(out=t2[:, :tn], in0=pr[:, :tn], in1=sinT[:, t0:t0 + tn], op=ALU.mult),
                                              reads=[rpr, rrope], writes=[rt2])
                                        fw.op("vector", lambda e: e.tensor_tensor(out=so[:, :tn], in0=t1[:, :tn], in1=t2[:, :tn], op=ALU.add),
                                              reads=[rt1, rt2], writes=[rso])
                                        fw.dma("sync", fdst[fci, :, t0:t0 + tn], so[:, :tn], reads=[rso])

                                    if pend_tail[0] is not None:
                                        pend_tail[0]()
                                    pend_tail[0] = rope_tail
                                    continue
                                so, rso = stg.next()
                                if kind == "gate":
                                    fw.op("scalar", lambda e, so=so, ps=ps, tn=tn: e.activation(out=so[:, :tn], in_=ps[:, :tn], func=AF.Sigmoid), reads=[rps], writes=[rso])
                                else:
                                    evac_copy(so[:, :tn], ps[:, :tn], rps, rso, scale=(0.0625 if kind == "rk" else None))
                                fw.dma("sync", fdst[fci, :, t0:t0 + tn], so[:, :tn], reads=[rso])
                    if pend_tail[0] is not None:
                        pend_tail[0]()
                        pend_tail[0] = None
                    if kind in ("rk", "rv", "rg", "dv", "dk"):
                        tiles = range(NTS, NT) if kind == "dk" else range(NT)
                        for tt in tiles:
                            ps, rps = pmain.next()
                            for k in range(KC):
                                fw.op("tensor", lambda e, k=k, ps=ps, Wt=Wt, tt=tt: e.matmul(
                                    ps[:, :], lhsT=hT[:, k, tt * 128:(tt + 1) * 128], rhs=Wt[:, k, :], start=(k == 0), stop=(k == KC - 1)),
                                    reads=[rW], writes=[rps], inc=(k == KC - 1))
                            r0 = tt * 128
                            if kind != "dk":
                                so, rso = stg.next()
                                evac_copy(so[:], ps[:], rps, rso, scale=(0.0625 if kind == "rk" else None))
                                if kind == "rk":
                                    dd = rk_tok[r0:r0 + 128, col0 - 2048:col0 - 2048 + 512]
                                elif kind == "rv":
                                    dd = rv_tok[r0:r0 + 128, col0 - 4096:col0 - 4096 + 512]
                                elif kind == "rg":
                                    dd = rg_tok[r0:r0 + 128, col0 - 8192:col0 - 8192 + 512]
                                else:
                                    dd = dv_tok[r0:r0 + 128, col0 - 16384:col0 - 16384 + 512]
                                fw.dma("sync", dd, so[:], reads=[rso])
                            if kind in ("dk", "dv") and tt >= NTS:
                                sf, rsf = stf.next()
                                fw.op("scalar", lambda e, sf=sf, ps=ps: e.activation(out=sf[:], in_=ps[:], func=AF.Copy), reads=[rps], writes=[rsf])
                                pr0 = r0 - TS
                                if kind == "dk":
                                    dd = nk[pr0:pr0 + 128, col0 - 14336:col0 - 14336 + 512]
                                else:
                                    dd = nv[pr0:pr0 + 128, col0 - 16384:col0 - 16384 + 512]
                                fw.dma("sync", dd, sf[:], reads=[rsf])
                fw.replay()

        if os.environ.get("KSTOP") == "B":
            return nc
        with ExitStack() as ph:
            TM = TS
            nmax = TM // 128
            qTs = Rot([sbt(ph, "qT%d" % i, [128, 2, TM], BF16) for i in range(2)])
            kTs = Rot([sbt(ph, "kT%d" % i, [128, 2, TM], BF16) for i in range(2)])
            kts = Rot([sbt(ph, "kt%d" % i, [128, nmax, 256], BF16) for i in range(1)])
            vs = Rot([sbt(ph, "v%d" % i, [128, nmax, 512], BF16) for i in range(1)])
            gs = Rot([sbt(ph, "g%d" % i, [128, nmax, 512], BF16) for i in range(1)])
            qf = sbt(ph, "qf", [128, 2, TM], BF16)
            qb = sbt(ph, "qb", [128, 2, TM], BF16)
            rqf, rqb = Res(), Res()
            SbA = sbt(ph, "SbA", [128, nmax, 1024], BF16)
            rSbA = [Res() for _ in range(nmax)]
            Sb = sbt(ph, "Sb", [128, 2, 512])
            Sf = sbt(ph, "Sf", [128, 2, 512])
            rSb, rSf = Res(), Res()
            SfA = sbt(ph, "SfA", [128, nmax, 1024], BF16)
            rSfA = [Res() for _ in range(nmax)]
            goTh = sbt(ph, "goTh", [128, 4, TM], BF16)
            rgoTh = Res()
            aTs = Rot([sbt(ph, "aT%d" % i, [128, 128], BF16) for i in range(3)])
            kzs = Rot([sbt(ph, "kz%d" % i, [128, 256], BF16) for i in range(4)])
            gos = Rot([sbt(ph, "go%d" % i, [128, 512], BF16) for i in range(3)])
            junks = Rot([sbt(ph, "jk%d" % i, [128, 512], BF16) for i in range(1)])
            sss = Rot([sbt(ph, "rss%d" % i, [128, 1]) for i in range(4)])
            p_s = Rot([pst(ph, "p_s%d" % i) for i in range(2)], x=True)
            p_o = Rot([pst(ph, "p_o%d" % i) for i in range(2)], x=True)
            p_kv = Rot([pst(ph, "p_kv%d" % i) for i in range(3)], x=True)
            p_T = Rot([pst(ph, "p_T", BF16)], x=True)

            def ret_head(t0, T, is_s, pidx, h, shared=None):
                    n = T // 128
                    if shared is None:
                        qT, rqT_ = qTs.next()
                        kT, rkT_ = kTs.next()
                        kt, rkt = kts.next()
                        v, rv = vs.next()
                        g, rg = gs.next()
                        fw.dma("sync", qT[:, :, :T], rqT[2 * h:2 * h + 2, :, t0:t0 + T].rearrange("c p t -> p c t"), writes=[rqT_])
                        fw.dma("sync", kT[:, :, :T], rkT[2 * h:2 * h + 2, :, t0:t0 + T].rearrange("c p t -> p c t"), writes=[rkT_])
                        fw.dma("sync", kt[:, :n, :], rk_tok[t0:t0 + T, h * 256:(h + 1) * 256].rearrange("(c p) d -> p c d", p=128), writes=[rkt])
                        fw.dma("sync", v[:, :n, :], rv_tok[t0:t0 + T, h * 512:(h + 1) * 512].rearrange("(c p) d -> p c d", p=128), writes=[rv])
                        fw.dma("sync", g[:, :n, :], rg_tok[t0:t0 + T, h * 512:(h + 1) * 512].rearrange("(c p) d -> p c d", p=128), writes=[rg])
                        fw.op("scalar", lambda e, g=g, n=n: e.activation(out=g[:, :n, :], in_=g[:, :n, :], func=AF.Silu), reads=[rg], writes=[rg])
                        vc = lambda c: v[:, c, :]
                        gc = lambda c: g[:, c, :]
                        ktc = lambda c: kt[:, c, :]
                        qTd = lambda dc: qT[:, dc, :T]
                        kTd = lambda dc: kT[:, dc, :T]
                    else:
                        (qT, rqT_), (kT, rkT_), (kt, rkt), (v, rv), (g, rg) = shared
                        qTflat = qT[:].rearrange("p a t -> p (a t)")
                        kTflat = kT[:].rearrange("p a t -> p (a t)")
                        vc = lambda c: v[:, c * H + h, :]
                        gc = lambda c: g[:, c * H + h, :]
                        ktc = lambda c: kt[:, c * H + h, :]
                        qTd = lambda dc: qTflat[:, (2 * h + dc) * T:(2 * h + dc + 1) * T]
                        kTd = lambda dc: kTflat[:, (2 * h + dc) * T:(2 * h + dc + 1) * T]
                    if is_s:
                        fw.dma("sync", Sb[:], st_b[h].rearrange("(c p) e -> p c e", p=128), writes=[rSb])
                        fw.dma("sync", Sf[:], st_f[h].rearrange("(c p) e -> p c e", p=128), writes=[rSf])
                    else:
                        fw.op("gpsimd", lambda e: e.memset(Sb[:], 0.0), writes=[rSb])
                        fw.op("gpsimd", lambda e: e.memset(Sf[:], 0.0), writes=[rSf])
                    for dc in range(2):
                        fw.op("vector", lambda e, dc=dc, n=n, T=T, h=h: e.tensor_tensor(
                            out=qf[:, dc, :T].rearrange("p (c i) -> p c i", i=128), in0=qTd(dc).rearrange("p (c i) -> p c i", i=128),
                            in1=XIF[:, h:h + 1, :].broadcast_to([128, n, 128]), op=ALU.mult), reads=[rqT_], writes=[rqf])
                        fw.op("vector", lambda e, dc=dc, n=n, T=T, h=h: e.tensor_tensor(
                            out=qb[:, dc, :T].rearrange("p (c i) -> p c i", i=128), in0=qTd(dc).rearrange("p (c i) -> p c i", i=128),
                            in1=XIB[:, h:h + 1, :].broadcast_to([128, n, 128]), op=ALU.mult), reads=[rqT_], writes=[rqb])

                    steps = []
                    for i in range(n):
                        cb = n - 1 - i
                        if not (cb == 0 and is_s):
                            steps.append(("b", cb))
                        if i < n - 1 or not is_s:
                            steps.append(("f", i))

                    def emit_kz(st):
                        d, c = st
                        Zt = ZB if d == "b" else ZF
                        kz, rkz = kzs.next()
                        fw.op("scalar", lambda e, kz=kz, c=c, Zt=Zt: e.activation(out=kz[:], in_=ktc(c), func=AF.Copy, scale=Zt[:, h:h + 1]),
                              reads=[rkt], writes=[rkz])
                        return kz, rkz

                    fw.op("scalar", lambda e, n=n: e.activation(out=SbA[:, n - 1, :], in_=Sb[:].rearrange("p c e -> p (c e)"), func=AF.Copy),
                          reads=[rSb], writes=[rSbA[n - 1]])
                    fw.op("scalar", lambda e: e.activation(out=SfA[:, 0, :], in_=Sf[:].rearrange("p c e -> p (c e)"), func=AF.Copy),
                          reads=[rSf], writes=[rSfA[0]])
                    nxt = emit_kz(steps[0]) if steps else None
                    for si, (d, c) in enumerate(steps):
                        kz, rkz = nxt
                        nxt = emit_kz(steps[si + 1]) if si + 1 < len(steps) else None
                        S, rS, Gt = (Sb, rSb, GB) if d == "b" else (Sf, rSf, GF)
                        pks = []
                        for dc in range(2):
                            pk, rpk = p_kv.next()
                            fw.op("tensor", lambda e, pk=pk, kz=kz, dc=dc, c=c: e.matmul(pk[:], lhsT=kz[:, dc * 128:(dc + 1) * 128], rhs=vc(c), start=True, stop=True),
                                  reads=[rkz, rv], writes=[rpk])
                            pks.append((pk, rpk))
                        for dc in range(2):
                            pk, rpk = pks[dc]
                            fw.op("vector", lambda e, pk=pk, dc=dc, S=S, Gt=Gt: e.scalar_tensor_tensor(out=S[:, dc, :], in0=S[:, dc, :], scalar=Gt[:, h:h + 1], in1=pk[:],
                                                                                                       op0=ALU.mult, op1=ALU.add), reads=[rpk], writes=[rS])
                        if d == "b" and c > 0:
                            fw.op("scalar", lambda e, c=c: e.activation(out=SbA[:, c - 1, :], in_=Sb[:].rearrange("p c e -> p (c e)"), func=AF.Copy),
                                  reads=[rSb], writes=[rSbA[c - 1]])
                        if d == "f" and c < n - 1:
                            fw.op("scalar", lambda e, c=c: e.activation(out=SfA[:, c + 1, :], in_=Sf[:].rearrange("p c e -> p (c e)"), func=AF.Copy),
                                  reads=[rSf], writes=[rSfA[c + 1]])
                    if not is_s:
                        fw.dma("sync", nsb[pidx, h].rearrange("(c p) e -> p c e", p=128), Sb[:], reads=[rSb])
                        fw.dma("sync", nsf[pidx, h].rearrange("(c p) e -> p c e", p=128), Sf[:], reads=[rSf])

                    def stageA(c):
                        cs = slice(c * 128, (c + 1) * 128)
                        pss, rpss = p_s.next()
                        for dc in range(2):
                            fw.op("tensor", lambda e, pss=pss, dc=dc, cs=cs: e.matmul(pss[:, 0:128], lhsT=kTd(dc)[:, cs], rhs=qTd(dc)[:, cs], start=(dc == 0), stop=(dc == 1)),
                                  reads=[rkT_, rqT_], writes=[rpss], inc=(dc == 1))
                        aT, raT = aTs.next()
                        fw.op("vector", lambda e, aT=aT, pss=pss: e.tensor_tensor(out=aT[:], in0=pss[:, 0:128], in1=maskT[:, h, :], op=ALU.mult), reads=[rpss], writes=[raT])
                        return aT, raT

                    def stageB(c, aT, raT):
                        cs = slice(c * 128, (c + 1) * 128)
                        po, rpo = p_o.next()
                        fw.op("tensor", lambda e, po=po, aT=aT, c=c: e.matmul(po[:], lhsT=aT[:], rhs=vc(c), start=True, stop=False), reads=[raT, rv], writes=[rpo], inc=False)
                        for dc in range(2):
                            fw.op("tensor", lambda e, po=po, dc=dc, cs=cs, c=c: e.matmul(po[:], lhsT=qf[:, dc, cs], rhs=SfA[:, c, dc * 512:(dc + 1) * 512], start=False, stop=False),
                                  reads=[rqf, rSfA[c]], writes=[rpo], inc=False)
                        for dc in range(2):
                            fw.op("tensor", lambda e, po=po, dc=dc, cs=cs, c=c: e.matmul(po[:], lhsT=qb[:, dc, cs], rhs=SbA[:, c, dc * 512:(dc + 1) * 512], start=False, stop=(dc == 1)),
                                  reads=[rqb, rSbA[c]], writes=[rpo], inc=(dc == 1))
                        ss, rss = sss.next()
                        jk, rjk = junks.next()
                        fw.op("scalar", lambda e, jk=jk, po=po, ss=ss: e.activation(out=jk[:], in_=po[:], func=AF.Square, scale=1.0 / math.sqrt(512.0), accum_out=ss[:]),
                              reads=[rpo], writes=[rjk, rss])
                        fw.op("scalar", lambda e, ss=ss: e.activation(out=ss[:], in_=ss[:], func=AF.Sqrt, bias=EPS, scale=1.0), reads=[rss], writes=[rss])
                        fw.op("vector", lambda e, ss=ss: e.reciprocal(out=ss[:], in_=ss[:]), reads=[rss], writes=[rss])
                        go, rgo = gos.next()
                        fw.op("vector", lambda e, go=go, po=po, ss=ss, c=c: e.scalar_tensor_tensor(out=go[:], in0=po[:], scalar=ss[:], in1=gc(c), op0=ALU.mult, op1=ALU.mult),
                              reads=[rpo, rss, rg], writes=[rgo])
                        return go, rgo, cs

                    def stageC(go, rgo, cs):
                        pT, rpT = p_T.next()
                        for j in range(4):
                            fw.op("tensor", lambda e, pT=pT, go=go, j=j: e.transpose(out=pT[:, j * 128:(j + 1) * 128], in_=go[:, j * 128:(j + 1) * 128], identity=identb[:]),
                                  reads=[rgo], writes=[rpT], inc=(j == 3))
                        fw.op("vector", lambda e, pT=pT, cs=cs: e.tensor_copy(out=goTh[:, :, cs], in_=pT[:, 0:512].rearrange("p (j i) -> p j i", i=128)),
                              reads=[rpT], writes=[rgoTh])

                    a_cur = stageA(0)
                    prevB = None
                    for c in range(n):
                        a_nxt = stageA(c + 1) if c + 1 < n else None
                        b_cur = stageB(c, *a_cur)
                        if prevB is not None:
                            stageC(*prevB)
                        prevB = b_cur
                        a_cur = a_nxt
                    stageC(*prevB)
                    fw.dma("sync", goT[4 * h:4 * h + 4, :, t0:t0 + T].rearrange("c p t -> p c t"), goTh[:, :, :T], reads=[rgoTh])

            for b in range(4):
                fw.dma("gpsimd", wro_b[b], w_ret_o[:, b * 512:(b + 1) * 512].rearrange("(k p) n -> p k n", p=128))
                fw.dma("gpsimd", wdo_b[b], w_diff_o[:, b * 512:(b + 1) * 512].rearrange("(k p) n -> p k n", p=128))
                fw.dma("gpsimd", wout_b[b], w_out[:, b * 512:(b + 1) * 512].rearrange("(k p) n -> p k n", p=128))
            for b in range(16):
                fw.dma("gpsimd", wup_b[b], w_up[:, b * 512:(b + 1) * 512].rearrange("(k p) n -> p k n", p=128))
            for fb in range(4):
                for cb in range(4):
                    fw.dma("gpsimd", wdn_b[fb * 4 + cb], w_down[fb * 2048:(fb + 1) * 2048, cb * 512:(cb + 1) * 512].rearrange("(k p) n -> p k n", p=128))
            for (t0, T, is_s, pidx) in seqs:
                if is_s or T * H > TM:
                    for h in range(H):
                        ret_head(t0, T, is_s, pidx, h)
                else:
                    n = T // 128
                    sh = [qTs.next(), kTs.next(), kts.next(), vs.next(), gs.next()]
                    (qT, rqT_), (kT, rkT_), (kt, rkt), (v, rv), (g, rg) = sh
                    fw.dma("sync", qT[:].rearrange("p a t -> p (a t)")[:, 0:16 * T].rearrange("p (c t) -> p c t", t=T),
                           rqT[:, :, t0:t0 + T].rearrange("c p t -> p c t"), writes=[rqT_])
                    fw.dma("sync", kT[:].rearrange("p a t -> p (a t)")[:, 0:16 * T].rearrange("p (c t) -> p c t", t=T),
                           rkT[:, :, t0:t0 + T].rearrange("c p t -> p c t"), writes=[rkT_])
                    fw.dma("sync", kt[:, 0:n * H, :].rearrange("p (c h) d -> p c (h d)", h=H), rk_tok[t0:t0 + T, :].rearrange("(c p) d -> p c d", p=128), writes=[rkt])
                    fw.dma("sync", v[:, 0:n * H, :].rearrange("p (c h) d -> p c (h d)", h=H), rv_tok[t0:t0 + T, :].rearrange("(c p) d -> p c d", p=128), writes=[rv])
                    fw.dma("sync", g[:, 0:n * H, :].rearrange("p (c h) d -> p c (h d)", h=H), rg_tok[t0:t0 + T, :].rearrange("(c p) d -> p c d", p=128), writes=[rg])
                    fw.op("scalar", lambda e, g=g, n=n: e.activation(out=g[:, 0:n * H, :], in_=g[:, 0:n * H, :], func=AF.Silu), reads=[rg], writes=[rg])
                    for h in range(H):
                        ret_head(t0, T, is_s, pidx, h, shared=sh)
            fw.replay()

        if os.environ.get("KSTOP") == "C":
            return nc
        with ExitStack() as ph:
            TKM = TS + PAST
            NP = PAST // 128
            q2s = Rot([sbt(ph, "q2_%d" % i, [128, 2, TS], BF16) for i in range(2)])
            k2s = Rot([sbt(ph, "k2_%d" % i, [128, 2, TKM], BF16) for i in range(2)])
            v2s = Rot([sbt(ph, "v2_%d" % i, [128, TKM // 128, 256], BF16) for i in range(2)])
            ckf = sbt(ph, "ckf", [128, NP, 256])
            ckb = sbt(ph, "ckb", [128, NP, 256], BF16)
            rckf, rckb = Res(), Res()
            eTs = Rot([sbt(ph, "eT%d" % i, [128, 512], BF16) for i in range(3)])
            o1s = Rot([sbt(ph, "o1_%d" % i, [128, 2, 512]) for i in range(2)])
            r1 = sbt(ph, "r1", [128, 512])
            rr1 = Res()
            tmpd = Rot([sbt(ph, "tmpd%d" % i, [128, 512]) for i in range(2)])
            sq = sbt(ph, "sq", [128, 2, 512], BF16)
            rsq = Res()
            sd = sbt(ph, "sd", [128, 512])
            rsd = Res()
            dst_ = Rot([sbt(ph, "dst%d" % i, [128, 2, 512], BF16) for i in range(2)])
            p_sT = Rot([pst(ph, "p_sT%d" % i) for i in range(2)], x=True)
            accs = [[(pst(ph, "acc%d_%d" % (m, j)), Res(True)) for j in range(3)] for m in range(2)]
            p_ck = p_sT
            scale = 1.0 / math.sqrt(128.0)

            def attn_head(t0, T, is_s, pidx, h):
                    Tk = T + (PAST if is_s else 0)
                    nk_ = Tk // 128
                    q2, rq2 = q2s.next()
                    k2, rk2 = k2s.next()
                    v2, rv2 = v2s.next()
                    fw.dma("sync", q2[:, :, :T], dqT[2 * h:2 * h + 2, :, t0:t0 + T].rearrange("c p t -> p c t"), writes=[rq2])
                    fw.dma("sync", k2[:, :, :T], dkT[2 * h:2 * h + 2, :, t0:t0 + T].rearrange("c p t -> p c t"), writes=[rk2])
                    fw.dma("sync", v2[:, :T // 128, :], dv_tok[t0:t0 + T, h * 256:(h + 1) * 256].rearrange("(c p) d -> p c d", p=128), writes=[rv2])
                    if is_s:
                        fw.dma("gpsimd", v2[:, T // 128:nk_, :], cache_v[:, h, :].rearrange("(c p) d -> p c d", p=128), writes=[rv2])
                        fw.dma("sync", ckf[:], cache_k[:, h, :, :].rearrange("(c p) m d -> p c (m d)", p=128), writes=[rckf])
                        fw.op("vector", lambda e: e.tensor_copy(out=ckb[:], in_=ckf[:]), reads=[rckf], writes=[rckb])
                        for c in range(NP):
                            pc, rpc = p_ck.next()
                            pcb = pc[:].bitcast(BF16)
                            for m in range(2):
                                fw.op("tensor", lambda e, pcb=pcb, c=c, m=m: e.transpose(out=pcb[:, m * 128:(m + 1) * 128], in_=ckb[:, c, m * 128:(m + 1) * 128], identity=identb[:]),
                                      reads=[rckb], writes=[rpc], inc=(m == 1))
                            fw.op("vector", lambda e, pcb=pcb, c=c, T=T, k2=k2: e.tensor_copy(out=k2[:, :, T + c * 128:T + (c + 1) * 128],
                                                                                               in_=pcb[:, 0:256].rearrange("p (m i) -> p m i", i=128)),
                                  reads=[rpc], writes=[rk2])
                    pend_epi = [None]

                    def epilogue(o1, ro1, q0, qn):
                        fw.op("scalar", lambda e: e.activation(out=sq[:, :, :qn], in_=o1[:, :, :qn], func=AF.Square, scale=1.0 / 16.0), reads=[ro1], writes=[rsq])
                        ps, rps = p_sT.next()
                        for half in range(2):
                            fw.op("tensor", lambda e, half=half: e.matmul(ps[:, :qn], lhsT=onesb[:], rhs=sq[:, half, :qn], start=(half == 0), stop=(half == 1)),
                                  reads=[rsq], writes=[rps], inc=(half == 1))
                        fw.op("scalar", lambda e: e.activation(out=sd[:, :qn], in_=ps[:, :qn], func=AF.Sqrt, bias=SUBLN_EPS, scale=1.0), reads=[rps], writes=[rsd])
                        fw.op("vector", lambda e: e.reciprocal(out=sd[:, :qn], in_=sd[:, :qn]), reads=[rsd], writes=[rsd])
                        ds, rds = dst_.next()
                        for half in range(2):
                            fw.op("vector", lambda e, half=half: e.scalar_tensor_tensor(out=ds[:, half, :qn], in0=o1[:, half, :qn], scalar=GS[:, half:half + 1],
                                                                                      in1=sd[:, :qn], op0=ALU.mult, op1=ALU.mult),
                                  reads=[ro1, rsd], writes=[rds])
                        fw.dma("sync", doT[2 * h:2 * h + 2, :, t0 + q0:t0 + q0 + qn].rearrange("c p t -> p c t"), ds[:, :, :qn], reads=[rds])

                    for q0 in range(0, T, 512):
                        qn = min(512, T - q0)
                        o1, ro1 = o1s.next()
                        def evac_m(m, o1=o1, ro1=ro1, qn=qn):
                            (alo, ralo), (ahi, rahi), (asum, rasum) = accs[m]
                            if m == 0:
                                fw.op("vector", lambda e: e.reciprocal(out=r1[:, :qn], in_=asum[:, :qn]), reads=[rasum], writes=[rr1])
                                fw.op("vector", lambda e: e.tensor_tensor(out=o1[:, 0, :qn], in0=alo[:, :qn], in1=r1[:, :qn], op=ALU.mult),
                                      reads=[ralo, rr1], writes=[ro1])
                                fw.op("vector", lambda e: e.tensor_tensor(out=o1[:, 1, :qn], in0=ahi[:, :qn], in1=r1[:, :qn], op=ALU.mult),
                                      reads=[rahi, rr1], writes=[ro1])
                                if pend_epi[0] is not None:
                                    epilogue(*pend_epi[0])
                                    pend_epi[0] = None
                                    p_sT.i += 1
                            else:
                                fw.op("vector", lambda e: e.reciprocal(out=r1[:, :qn], in_=asum[:, :qn]), reads=[rasum], writes=[rr1])
                                fw.op("vector", lambda e: e.tensor_scalar(out=r1[:, :qn], in0=r1[:, :qn], scalar1=neglam[:, 0:1], scalar2=None, op0=ALU.mult),
                                      reads=[rr1], writes=[rr1])
                                for half, (aa, raa) in enumerate([(alo, ralo), (ahi, rahi)]):
                                    td, rtd = tmpd.next()
                                    fw.op("vector", lambda e, td=td, aa=aa: e.tensor_tensor(out=td[:, :qn], in0=aa[:, :qn], in1=r1[:, :qn], op=ALU.mult),
                                          reads=[raa, rr1], writes=[rtd])
                                    fw.op("vector", lambda e, td=td, half=half: e.tensor_tensor(out=o1[:, half, :qn], in0=o1[:, half, :qn], in1=td[:, :qn], op=ALU.add),
                                          reads=[rtd], writes=[ro1])

                        steps = [(m, kt) for m in range(2) for kt in range(nk_)]
                        pend = None
                        for si in range(len(steps) + 1):
                            cur = None
                            if si < len(steps):
                                m, kt = steps[si]
                                ps, rps = p_sT.next()
                                fw.op("tensor", lambda e, ps=ps, m=m, kt=kt, q0=q0, qn=qn: e.matmul(
                                    ps[:, :qn], lhsT=k2[:, m, kt * 128:(kt + 1) * 128], rhs=q2[:, m, q0:q0 + qn], start=True, stop=True),
                                    reads=[rk2, rq2], writes=[rps])
                                cur = (ps, rps, m, kt)
                            if pend is None:
                                pend = cur
                                continue
                            ps, rps, m, kt = pend
                            pend = cur
                            (alo, ralo), (ahi, rahi), (asum, rasum) = accs[m]
                            eT, reT = eTs.next()
                            fw.op("scalar", lambda e, eT=eT, ps=ps, qn=qn: e.activation(out=eT[:, :qn], in_=ps[:, :qn], func=AF.Exp, scale=scale), reads=[rps], writes=[reT])
                            st_, sp_ = (kt == 0), (kt == nk_ - 1)
                            fw.op("tensor", lambda e, alo=alo, kt=kt, eT=eT, qn=qn, st_=st_, sp_=sp_: e.matmul(
                                alo[:, :qn], lhsT=v2[:, kt, 0:128], rhs=eT[:, :qn], start=st_, stop=sp_), reads=[rv2, reT], writes=[ralo], inc=False)
                            fw.op("tensor", lambda e, ahi=ahi, kt=kt, eT=eT, qn=qn, st_=st_, sp_=sp_: e.matmul(
                                ahi[:, :qn], lhsT=v2[:, kt, 128:256], rhs=eT[:, :qn], start=st_, stop=sp_), reads=[rv2, reT], writes=[rahi], inc=False)
                            fw.op("tensor", lambda e, asum=asum, eT=eT, qn=qn, st_=st_, sp_=sp_: e.matmul(
                                asum[:, :qn], lhsT=onesb[:], rhs=eT[:, :qn], start=st_, stop=sp_), reads=[reT], writes=[rasum], inc=True)
                            if kt == nk_ - 1:
                                evac_m(m)
                        pend_epi[0] = (o1, ro1, q0, qn)
                    epilogue(*pend_epi[0])

            for (t0, T, is_s, pidx) in seqs:
                for h in range(H):
                    attn_head(t0, T, is_s, pidx, h)
            fw.replay()

        if os.environ.get("KSTOP") == "D":
            return nc
        with ExitStack() as phEF:
            actT = sbt(phEF, "actT", [128, KC, 512], BF16)
            ract = Res()
            def ef_group(t0, tn):
                cond = 0 if t0 < TS else 1
                ntl = tn // 128
                with ExitStack() as ph:
                    gd = sbt(ph, "gd", [128, 48, 512], BF16)
                    rgd = Res()
                    fw.dma("sync", gd[:, 0:32, :tn], goT[:, :, t0:t0 + tn].rearrange("c p t -> p c t"), writes=[rgd])
                    fw.dma("sync", gd[:, 32:48, :tn], doT[:, :, t0:t0 + tn].rearrange("c p t -> p c t"), writes=[rgd])
                    Wr = Rot([sbt(ph, "Wr%d" % i, [128, 32, 512], BF16) for i in range(2)])
                    Wd = Rot([sbt(ph, "Wd%d" % i, [128, 16, 512], BF16) for i in range(2)])
                    gts = Rot([sbt(ph, "gts%d" % i, [128, 2, 512], BF16) for i in range(2)])
                    m1s = Rot([sbt(ph, "m1_%d" % i, [128, 512]) for i in range(2)])
                    m2s = Rot([sbt(ph, "m2_%d" % i, [128, 512]) for i in range(2)])
                    p_r = Rot([pst(ph, "p_r%d" % i) for i in range(2)], x=True)
                    p_d = Rot([pst(ph, "p_d%d" % i) for i in range(2)], x=True)
                    for oc in range(KC):
                        sub = oc % 4
                        if sub == 0:
                            wr, rwr = Wr.next()
                            wd, rwd = Wd.next()
                            fw.dma("gpsimd", wr[:], wro_b[oc // 4], writes=[rwr])
                            fw.dma("gpsimd", wd[:], wdo_b[oc // 4], writes=[rwd])
                        gt_, rgt = gts.next()
                        fw.dma("sync", gt_[:, 0, :tn], gatesT[oc, :, t0:t0 + tn], writes=[rgt])
                        fw.dma("sync", gt_[:, 1, :tn], gatesT[16 + oc, :, t0:t0 + tn], writes=[rgt])
                        pr, rpr = p_r.next()
                        pd, rpd = p_d.next()
                        for k in range(32):
                            fw.op("tensor", lambda e, pr=pr, wr=wr, k=k, sub=sub: e.matmul(pr[:, :tn], lhsT=wr[:, k, sub * 128:(sub + 1) * 128], rhs=gd[:, k, :tn], start=(k == 0), stop=(k == 31)),
                                  reads=[rwr, rgd], writes=[rpr], inc=(k == 31))
                        for k in range(16):
                            fw.op("tensor", lambda e, pd=pd, wd=wd, k=k, sub=sub: e.matmul(pd[:, :tn], lhsT=wd[:, k, sub * 128:(sub + 1) * 128], rhs=gd[:, 32 + k, :tn], start=(k == 0), stop=(k == 15)),
                                  reads=[rwd, rgd], writes=[rpd], inc=(k == 15))
                        m1, rm1 = m1s.next()
                        m2, rm2 = m2s.next()
                        fw.op("vector", lambda e, m1=m1, pr=pr, gt_=gt_: e.tensor_tensor(out=m1[:, :tn], in0=pr[:, :tn], in1=gt_[:, 0, :tn], op=ALU.mult),
                              reads=[rpr, rgt], writes=[rm1])
                        fw.op("vector", lambda e, m2=m2, pd=pd, gt_=gt_: e.tensor_tensor(out=m2[:, :tn], in0=pd[:, :tn], in1=gt_[:, 1, :tn], op=ALU.mult),
                              reads=[rpd, rgt], writes=[rm2])
                        fw.op("vector", lambda e, m1=m1, m2=m2, oc=oc: e.tensor_tensor(out=actT[:, oc, :tn], in0=m1[:, :tn], in1=m2[:, :tn], op=ALU.add),
                              reads=[rm1, rm2], writes=[ract])
                    fw.replay()

                with ExitStack() as ph:
                    yz = sbt(ph, "yz", [128, 4, D])
                    ryz = [Res() for _ in range(4)]
                    h2T = sbt(ph, "h2T", [128, KC, 512], BF16)
                    rh2 = Res()
                    W = Rot([sbt(ph, "Wf%d" % i, [128, KC, 512], BF16) for i in range(2)])
                    Wo2 = sbt(ph, "Wo2", [128, KC, 512], BF16)
                    Wo3 = sbt(ph, "Wo3", [128, KC, 512], BF16)
                    GT = sbt(ph, "GT", [128, D])
                    rGT = Res()
                    xl = Rot([sbt(ph, "xl%d" % i, [128, D]) for i in range(2)])
                    tmp = {"ss": Rot([sbt(ph, "ss%d" % i, [128, 8]) for i in range(4)]),
                           "junk": Rot([sbt(ph, "junk", [128, D], BF16)]),
                           "xn": Rot([sbt(ph, "xn%d" % i, [128, D], BF16) for i in range(1)])}
                    rl = Rot([sbt(ph, "rl%d" % i, [128, 512]) for i in range(2)])
                    psT = Rot([pst(ph, "psT%d" % i, BF16) for i in range(2)], x=True)
                    pm = Rot([pst(ph, "pm%d" % i) for i in range(6)], x=True)
                    rx1 = [Res() for _ in range(4)]

                    def post_norm_residual(tt, which, src_rows, rsrc, dst_dram, rdst, keep=None):
                        ss, rss = tmp["ss"].next()
                        jk, rjk = tmp["junk"].next()
                        fw.op("scalar", lambda e: e.activation(out=jk[:], in_=yz[:, tt, :], func=AF.Square, scale=1.0 / math.sqrt(D), accum_out=ss[:, 0:1]),
                              reads=[ryz[tt]], writes=[rjk, rss])
                        fw.op("scalar", lambda e: e.activation(out=ss[:, 0:1], in_=ss[:, 0:1], func=AF.Sqrt, bias=EPS, scale=1.0), reads=[rss], writes=[rss])
                        fw.op("vector", lambda e: e.reciprocal(out=ss[:, 0:1], in_=ss[:, 0:1]), reads=[rss], writes=[rss])
                        x, rx = xl.next()
                        fw.dma("sync", x[:], src_rows, reads=rsrc, writes=[rx])
                        fw.op("vector", lambda e: e.scalar_tensor_tensor(out=yz[:, tt, :], in0=yz[:, tt, :], scalar=ss[:, 0:1], in1=GT[:], op0=ALU.mult, op1=ALU.mult),
                              reads=[rss, rGT], writes=[ryz[tt]])
                        fw.op("vector", lambda e: e.tensor_tensor(out=yz[:, tt, :], in0=yz[:, tt, :], in1=x[:], op=ALU.add), reads=[rx], writes=[ryz[tt]])
                        fw.dma("sync", dst_dram, yz[:, tt, :], reads=[ryz[tt]], writes=rdst)

                    fw.dma("sync", GT[:], gt_scr[cond, 0:1, :].partition_broadcast(128), writes=[rGT])
                    Wo = [W.tiles[0], W.tiles[1], (Wo2, Res()), (Wo3, Res())]
                    for cb in range(4):
                        fw.dma("gpsimd", Wo[cb][0][:], wout_b[cb], writes=[Wo[cb][1]])

                    def mm_wout(tt):
                        for cb in range(4):
                            Wt, rW = Wo[cb]
                            ps, rps = pm.next()
                            for k in range(KC):
                                fw.op("tensor", lambda e, ps=ps, Wt=Wt, k=k, tt=tt: e.matmul(ps[:], lhsT=actT[:, k, tt * 128:(tt + 1) * 128], rhs=Wt[:, k, :],
                                                                                            start=(k == 0), stop=(k == KC - 1)),
                                      reads=[rW, ract], writes=[rps], inc=(k == KC - 1))
                            fw.op("scalar", lambda e, ps=ps, tt=tt, cb=cb: e.activation(out=yz[:, tt, cb * 512:(cb + 1) * 512], in_=ps[:], func=AF.Copy),
                                  reads=[rps], writes=[ryz[tt]])

                    mm_wout(0)
                    for tt in range(ntl):
                        if tt + 1 < ntl:
                            mm_wout(tt + 1)
                        gtt = t0 // 128 + tt
                        post_norm_residual(tt, 0, xrows(gtt), [], x1_scr[gtt * 128:(gtt + 1) * 128, :], [rx1[tt]])
                        norm_mod_T(tmp, yz[:, tt, :], ryz[tt], cond, A2, 48, lambda k, tt=tt: h2T[:, k, tt * 128:(tt + 1) * 128], rh2, psT)
                    fw.dma("sync", GT[:], gt_scr[cond, 1:2, :].partition_broadcast(128), writes=[rGT])
                    for fb in range(4):
                        for ub in range(4):
                            Wt, rW = W.next()
                            c0 = fb * 2048 + ub * 512
                            fw.dma("gpsimd", Wt[:], wup_b[fb * 4 + ub], writes=[rW])
                            for sub in range(4):
                                ps, rps = pm.next()
                                for k in range(KC):
                                    fw.op("tensor", lambda e, ps=ps, Wt=Wt, k=k, sub=sub: e.matmul(ps[:, :tn], lhsT=Wt[:, k, sub * 128:(sub + 1) * 128], rhs=h2T[:, k, :tn],
                                                                                                  start=(k == 0), stop=(k == KC - 1)),
                                          reads=[rW, rh2], writes=[rps], inc=(k == KC - 1))
                                r_, rr_ = rl.next()
                                fw.op("scalar", lambda e, r_=r_, ps=ps: e.activation(out=r_[:, :tn], in_=ps[:, :tn], func=AF.Relu), reads=[rps], writes=[rr_])
                                fc = ub * 4 + sub
                                fw.op("vector", lambda e, r_=r_, fc=fc: e.tensor_tensor(out=actT[:, fc, :tn], in0=r_[:, :tn], in1=r_[:, :tn], op=ALU.mult),
                                      reads=[rr_], writes=[ract])
                        for cb in range(4):
                            Wt, rW = W.next()
                            fw.dma("gpsimd", Wt[:], wdn_b[fb * 4 + cb], writes=[rW])
                            for tt in range(ntl):
                                ps, rps = pm.next()
                                for k in range(KC):
                                    fw.op("tensor", lambda e, ps=ps, Wt=Wt, k=k, tt=tt: e.matmul(ps[:], lhsT=actT[:, k, tt * 128:(tt + 1) * 128], rhs=Wt[:, k, :],
                                                                                                start=(k == 0), stop=(k == KC - 1)),
                                          reads=[rW, ract], writes=[rps], inc=(k == KC - 1))
                                zs = yz[:, tt, cb * 512:(cb + 1) * 512]
                                if fb == 0:
                                    fw.op("scalar", lambda e, ps=ps, zs=zs: e.activation(out=zs, in_=ps[:], func=AF.Copy), reads=[rps], writes=[ryz[tt]])
                                else:
                                    fw.op("vector", lambda e, ps=ps, zs=zs: e.tensor_tensor(out=zs, in0=ps[:], in1=zs, op=ALU.add), reads=[rps], writes=[ryz[tt]])
                    for tt in range(ntl):
                        gtt = t0 // 128 + tt
                        post_norm_residual(tt, 1, x1_scr[gtt * 128:(gtt + 1) * 128, :], [rx1[tt]], yrows(gtt), [])
                    fw.replay()

            for (t0, tn) in groups:
                ef_group(t0, tn)
    return nc


def _consts(TS):
    c = np.zeros((128, NCONST), np.float32)
    p = np.arange(128, dtype=np.float32)[:, None]
    i = np.arange(128, dtype=np.float32)[None, :]
    c[:, 0:128] = np.eye(128, dtype=np.float32)
    c[:, 128:256] = (i >= p)
    c[:, 256:384] = (p >= i)
    c[:, 384:512] = np.maximum(i - p, 0)
    c[:, 512:640] = np.maximum(p - i, 0)
    c[:, 640:768] = i + 1.0
    c[:, 768:896] = 128.0 - i
    c[:, 896] = 127.0 - p[:, 0]
    c[:, 897] = p[:, 0]
    f = np.arange(128)
    partner = np.where((f % 64) < 32, f + 32, f - 32)
    P = np.zeros((128, 128), np.float32)
    P[partner, f] = 1.0
    c[:, 898:1026] = P
    t = np.arange(TS)
    row = (t // 64).astype(np.float32)
    col = (t % 64).astype(np.float32)
    inv = (10000.0 ** (-np.arange(0, 64, 2, dtype=np.float32) / 64.0)).astype(np.float32)
    ang = np.where((f[:, None] < 64), row[None, :] * inv[f % 32][:, None], col[None, :] * inv[f % 32][:, None]).astype(np.float32)
    sgn = np.where((f % 64) < 32, -1.0, 1.0).astype(np.float32)[:, None]
    cos = np.cos(ang).astype(np.float32)
    sin = (np.sin(ang) * sgn).astype(np.float32)
    return c, cos, sin


_NC_CACHE = {}


def run(inputs, TS, TP, PAST):
    key = (TS, TP, PAST)
    if key not in _NC_CACHE:
        _NC_CACHE[key] = build(TS, TP, PAST)
    nc = _NC_CACHE[key]
    f = lambda a: np.ascontiguousarray(np.asarray(a, dtype=np.float32))
    cst, cos, sin = _consts(TS)
    shared = {
        "w_ada": f(inputs["w_ada"][0]), "b_ada": f(inputs["b_ada"]), "w_in": f(inputs["w_in"][0]),
        "glog": f(np.concatenate([inputs["ret_gamma_logit_fwd"][0], inputs["ret_gamma_logit_bwd"][0]])[None, :]),
        "w_ret_o": f(inputs["w_ret_o"][0]),
        "lam4": f(np.concatenate([inputs["lambda_q1"][0], inputs["lambda_k1"][0], inputs["lambda_q2"][0], inputs["lambda_k2"][0]])[None, :]),
        "g_sub": f(inputs["g_diff_subln"]), "w_diff_o": f(inputs["w_diff_o"][0]), "w_out": f(inputs["w_out"][0]),
        "w_up": f(inputs["w_mlp_up"][0]), "w_down": f(inputs["w_mlp_down"][0]),
        "gpost": f(np.stack([inputs["g_mix_post"][0], inputs["g_mlp_post"][0]])),
        "consts": cst, "rope_cos": cos, "rope_sin": sin,
    }
    in_maps = []
    for c in range(8):
        m = dict(shared)
        m["x_s"] = f(inputs["x_sample"][c])
        m["x_p"] = f(np.asarray(inputs["x_prompt"][2 * c:2 * c + 2]).reshape(2 * TP, D))
        m["cache_k"] = f(inputs["cache_k"][c, 0])
        m["cache_v"] = f(inputs["cache_v"][c, 0])
        m["st_f"] = f(inputs["state_ret_fwd"][c, 0])
        m["st_b"] = f(inputs["state_ret_bwd"][c, 0])
        m["cvec"] = f(np.stack([inputs["c"][c], inputs["c_ctx"], inputs["g_mix_pre"][0], inputs["g_mlp_pre"][0]]))
        in_maps.append(m)
    res = run_bass_kernel_spmd(nc, in_maps, core_ids=list(range(8)))
    R = res.results
    y_prompt = np.concatenate([R[c]["y_p"].reshape(2, TP, D) for c in range(8)], axis=0)
    y_sample = np.stack([R[c]["y_s"] for c in range(8)], axis=0)
    new_k = np.concatenate([R[c]["nk"].reshape(2, 1, TP, H, 2, 128) for c in range(8)], axis=0)
    new_v = np.concatenate([R[c]["nv"].reshape(2, 1, TP, H, 256) for c in range(8)], axis=0)
    new_sf = np.concatenate([R[c]["nsf"].reshape(2, 1, H, 256, 512) for c in range(8)], axis=0)
    new_sb = np.concatenate([R[c]["nsb"].reshape(2, 1, H, 256, 512) for c in range(8)], axis=0)
    return tuple(np.asarray(a, dtype=np.float32) for a in (y_prompt, y_sample, new_k, new_v, new_sf, new_sb))


def kernel(**inputs):
    return run(inputs, 2048, 256, 256)
```
